# Optimizing a Trainium2 kernel written in Bass

```python
import jax, jax.numpy as jnp
from jax import lax
import numpy as np

D_MODEL = 1024
BATCH = 8
SEQ = 2048
DEPTH = 1
DEC_BATCH = 128
DEC_SEQ = 1
PAST_LEN = 16384
PAGE_SIZE = 128

D_MIX = D_MODEL
HEAD_DIM = 64
D_A = D_MIX // 2
D_B = D_MIX - D_A
N_GROUPS_A = D_A // HEAD_DIM
N_GROUPS_B = D_B // HEAD_DIM
K_A = 31
K_B = 3
D_FF = 2816
D_IN = 2 * D_A + 3 * D_B
LN_EPS = 1e-5
ALPHA = (2 * DEPTH) ** 0.25
BETA = (8 * DEPTH) ** -0.25

kernel_name = "hybrid_conformer_shortconv_decoder_step"


def layer_norm(x, g, b):
    xf = x.astype(jnp.float32)
    mu = jnp.mean(xf, axis=-1, keepdims=True)
    var = jnp.mean(jnp.square(xf - mu), axis=-1, keepdims=True)
    return ((xf - mu) * lax.rsqrt(var + LN_EPS)).astype(x.dtype) * g + b


def swiglu(x, w_gate, w_up, w_down):
    return (jax.nn.silu(x @ w_gate) * (x @ w_up)) @ w_down


def causal_depthwise(ctx, new, w):
    k = w.shape[0]
    full = jnp.concatenate([ctx, new], axis=1)
    out = lax.conv_general_dilated(
        full, w[:, None, :], window_strides=(1,), padding='VALID',
        dimension_numbers=('NWC', 'WIO', 'NWC'), feature_group_count=w.shape[1])
    return out, full[:, full.shape[1] - (k - 1):]


def token_mixers(h, ctx_a, ctx_b, w_in, b_in, w_dw_a, b_dw_a, ln_conv_g, ln_conv_b, w_dw_b, w_o, b_o):
    z = h @ w_in + b_in
    a_val, a_gate, g_b, g_c, b_val = jnp.split(
        z, [D_A, 2 * D_A, 2 * D_A + D_B, 2 * D_A + 2 * D_B], axis=-1)
    u = a_val * jax.nn.sigmoid(a_gate)
    ua, new_ctx_a = causal_depthwise(ctx_a, u, w_dw_a)
    ua = jax.nn.silu(layer_norm(ua + b_dw_a, ln_conv_g, ln_conv_b))
    v = g_c * b_val
    vb, new_ctx_b = causal_depthwise(ctx_b, v, w_dw_b)
    ub = g_b * vb
    out = jnp.concatenate([ua, ub], axis=-1) @ w_o + b_o
    return out, new_ctx_a, new_ctx_b


def trunk_layer(x, ctx_a, ctx_b, ln_f1_g, ln_f1_b, f1_wg, f1_wu, f1_wd,
                w_in, b_in, w_dw_a, b_dw_a, ln_conv_g, ln_conv_b, w_dw_b, w_o, b_o,
                ln_mix_g, ln_mix_b, f2_wg, f2_wu, f2_wd, ln_f2_g, ln_f2_b):
    x = layer_norm(ALPHA * x + 0.5 * swiglu(x, f1_wg, f1_wu, f1_wd), ln_f1_g, ln_f1_b)
    m, new_a, new_b = token_mixers(x, ctx_a, ctx_b, w_in, b_in, w_dw_a, b_dw_a,
                                   ln_conv_g, ln_conv_b, w_dw_b, w_o, b_o)
    x = layer_norm(ALPHA * x + m, ln_mix_g, ln_mix_b)
    x = layer_norm(ALPHA * x + 0.5 * swiglu(x, f2_wg, f2_wu, f2_wd), ln_f2_g, ln_f2_b)
    return x, new_a, new_b


def setup_inputs(seed: int = 0) -> dict:
    key = jax.random.key(seed)
    ks = iter(jax.random.split(key, 32))
    f32 = jnp.float32

    def nrm(shape, scale):
        return jax.random.normal(next(ks), shape, f32) * scale

    def gain(shape):
        return 1.0 + nrm(shape, 0.02)

    L = DEPTH
    return {
        "x_prompt": nrm((BATCH, SEQ, D_MODEL), 1.0),
        "x_sample": nrm((DEC_BATCH, DEC_SEQ, D_MODEL), 1.0),
        "state_conv_a": nrm((L, DEC_BATCH, K_A - 1, D_A), 0.5),
        "state_conv_b": nrm((L, DEC_BATCH, K_B - 1, D_B), 0.5),
        "ln_f1_g": gain((L, D_MODEL)),
        "ln_f1_b": nrm((L, D_MODEL), 0.02),
        "f1_wg": nrm((L, D_MODEL, D_FF), D_MODEL ** -0.5),
        "f1_wu": nrm((L, D_MODEL, D_FF), D_MODEL ** -0.5),
        "f1_wd": nrm((L, D_FF, D_MODEL), BETA * D_FF ** -0.5),
        "w_in": nrm((L, D_MODEL, D_IN), D_MODEL ** -0.5),
        "b_in": nrm((L, D_IN), 0.02),
        "w_dw_a": nrm((L, K_A, D_A), K_A ** -0.5),
        "b_dw_a": nrm((L, D_A), 0.02),
        "ln_conv_g": gain((L, D_A)),
        "ln_conv_b": nrm((L, D_A), 0.02),
        "w_dw_b": nrm((L, K_B, D_B), K_B ** -0.5),
        "w_o": nrm((L, D_MIX, D_MODEL), BETA * D_MIX ** -0.5),
        "b_o": nrm((L, D_MODEL), 0.02),
        "ln_mix_g": gain((L, D_MODEL)),
        "ln_mix_b": nrm((L, D_MODEL), 0.02),
        "f2_wg": nrm((L, D_MODEL, D_FF), D_MODEL ** -0.5),
        "f2_wu": nrm((L, D_MODEL, D_FF), D_MODEL ** -0.5),
        "f2_wd": nrm((L, D_FF, D_MODEL), BETA * D_FF ** -0.5),
        "ln_f2_g": gain((L, D_MODEL)),
        "ln_f2_b": nrm((L, D_MODEL), 0.02),
    }


def reference(x_prompt, x_sample, state_conv_a, state_conv_b,
              ln_f1_g, ln_f1_b, f1_wg, f1_wu, f1_wd,
              w_in, b_in, w_dw_a, b_dw_a, ln_conv_g, ln_conv_b, w_dw_b, w_o, b_o,
              ln_mix_g, ln_mix_b, f2_wg, f2_wu, f2_wd, ln_f2_g, ln_f2_b):
    yp, ys = x_prompt, x_sample
    pa_list, pb_list, sa_list, sb_list = [], [], [], []
    for l in range(DEPTH):
        params = (ln_f1_g[l], ln_f1_b[l], f1_wg[l], f1_wu[l], f1_wd[l],
                  w_in[l], b_in[l], w_dw_a[l], b_dw_a[l], ln_conv_g[l], ln_conv_b[l],
                  w_dw_b[l], w_o[l], b_o[l], ln_mix_g[l], ln_mix_b[l],
                  f2_wg[l], f2_wu[l], f2_wd[l], ln_f2_g[l], ln_f2_b[l])
        ctx_a0 = jnp.zeros((yp.shape[0], K_A - 1, D_A), yp.dtype)
        ctx_b0 = jnp.zeros((yp.shape[0], K_B - 1, D_B), yp.dtype)
        yp, pa, pb = trunk_layer(yp, ctx_a0, ctx_b0, *params)
        ys, sa, sb = trunk_layer(ys, state_conv_a[l], state_conv_b[l], *params)
        pa_list.append(pa); pb_list.append(pb); sa_list.append(sa); sb_list.append(sb)
    new_conv_a_prompt = jnp.stack(pa_list)
    new_conv_b_prompt = jnp.stack(pb_list)
    new_conv_a_sample = jnp.stack(sa_list)
    new_conv_b_sample = jnp.stack(sb_list)
    return (yp, ys, new_conv_a_prompt, new_conv_b_prompt, new_conv_a_sample, new_conv_b_sample)
```

```python
import contextlib
import numpy as np
import concourse.bass as bass
import concourse.mybir as mybir
from concourse.bass_utils import run_bass_kernel_spmd

F32 = mybir.dt.float32
BF16 = mybir.dt.bfloat16
AF = mybir.ActivationFunctionType
ALU = mybir.AluOpType
AX = mybir.AxisListType

D = 1024
FF = 2816
NFC = FF // 128
KC = D // 128
DA = 512
DIN = 2560
NIC = DIN // 128
SEQ = 2048
NS = 16
KA = 31
KB = 3
NCORE = 8
BLK = 1024
NTOKB = BLK + NS
ALPHA = float(2.0 ** 0.25)
EPS = 1e-5
NSLOT = 4

C_BIN = 0
C_WA = C_BIN + NIC
C_BA = C_WA + 4 * KA
C_LG = C_BA + 4
C_LB = C_LG + 4
C_WB = C_LB + 4
C_WP = C_WB + 4 * KB
NCST = C_WP + 128
R2W = 1052


class _Op:
    __slots__ = ("eng", "fn", "deps", "dma", "idx", "inc", "waits", "count")


class Prog:
    ENGS = ("pe", "act", "dve", "pool", "sp")

    def __init__(self, nc, stack):
        self.nc = nc
        self.stack = stack
        self.ops = []
        self.lastw = {}
        self.readers = {}
        self.dma_cnt = {}
        self.sems = {}
        self.eng_ops = {e: [] for e in self.ENGS}

    def sem(self, name):
        if name not in self.sems:
            self.sems[name] = self.stack.enter_context(self.nc.semaphore(name))
        return self.sems[name]

    def op(self, eng, fn, reads=(), writes=(), dma=None):
        o = _Op()
        o.eng = eng
        o.fn = fn
        o.inc = False
        o.waits = []
        o.count = 0
        oid = len(self.ops)
        deps = {}
        for r in reads:
            w = self.lastw.get(r)
            if w is not None:
                deps[w] = True
        for wt in writes:
            w = self.lastw.get(wt)
            if w is not None and w not in deps:
                deps[w] = False
            for rd in self.readers.get(wt, ()):
                if rd not in deps:
                    deps[rd] = False
        deps.pop(oid, None)
        o.deps = deps
        if dma is not None:
            self.dma_cnt[dma] = self.dma_cnt.get(dma, 0) + 16
            o.dma = (dma, self.dma_cnt[dma])
            self.sem(dma)
        else:
            o.dma = None
        for r in reads:
            self.readers.setdefault(r, []).append(oid)
        for wt in writes:
            self.lastw[wt] = oid
            self.readers[wt] = []
        o.idx = len(self.eng_ops[eng])
        self.eng_ops[eng].append(o)
        self.ops.append(o)
        return oid

    def finalize(self, final_wait_eng="sp"):
        nc = self.nc
        esem = {e: self.sem("S_" + e) for e in ("pe", "act", "dve", "pool")}
        seen_eng = {e: {f: -1 for f in self.ENGS} for e in self.ENGS}
        seen_dma = {e: {} for e in self.ENGS}
        pending = []
        for o in self.ops:
            e = o.eng
            dma_w = {}
            eng_w = {}
            for pid, raw in o.deps.items():
                p = self.ops[pid]
                if p.dma is not None:
                    s, v = p.dma
                    if seen_dma[e].get(s, 0) >= v:
                        continue
                    if dma_w.get(s, 0) < v:
                        dma_w[s] = v
                else:
                    if p.eng == e:
                        if e == "pe" or e == "sp":
                            continue
                    if seen_eng[e][p.eng] >= p.idx:
                        continue
                    if eng_w.get(p.eng, -1) < p.idx:
                        eng_w[p.eng] = p.idx
            for s, v in dma_w.items():
                seen_dma[e][s] = v
                o.waits.append(("dma", s, v))
            for f, idx in eng_w.items():
                seen_eng[e][f] = idx
                prod = self.eng_ops[f][idx]
                prod.inc = True
                o.waits.append(("eng", f, prod))
        for e in ("pe", "act", "dve", "pool"):
            c = 0
            for o in self.eng_ops[e]:
                if o.inc:
                    c += 1
                    o.count = c
        final = [(self.sems[s], v) for s, v in self.dma_cnt.items()]
        sems = self.sems

        def body_for(ename):
            def body(eng):
                for o in self.eng_ops[ename]:
                    for w in o.waits:
                        if w[0] == "dma":
                            eng.wait_ge(sems[w[1]], w[2])
                        else:
                            eng.wait_ge(esem[w[1]], w[2].count)
                    ins = o.fn(eng)
                    if o.dma is not None:
                        ins.then_inc(sems[o.dma[0]], 16)
                    elif o.inc:
                        ins.then_inc(esem[ename], 1)
                if ename == final_wait_eng:
                    for s, v in final:
                        eng.wait_ge(s, v)
            return body

        with nc.Block() as block:
            block.tensor(body_for("pe"))
            block.scalar(body_for("act"))
            block.vector(body_for("dve"))
            block.gpsimd(body_for("pool"))
            block.sync(body_for("sp"))


def build_nc():
    nc = bass.Bass("TRN2", target_bir_lowering=False)
    dt = nc.dram_tensor

    def din(name, shape):
        return dt(name, list(shape), F32, kind="ExternalInput").ap()

    def dout(name, shape):
        return dt(name, list(shape), F32, kind="ExternalOutput").ap()

    xp = din("xp", [SEQ, D])
    xs = din("xs", [NS, D])
    sca = din("sca", [NS * 30, DA])
    scb = din("scb", [NS * 2, DA])
    wgu = [din("wgu1", [NFC, 128, 2 * KC * 128]), din("wgu2", [NFC, 128, 2 * KC * 128])]
    wdd = [din("wd1", [FF, D]), din("wd2", [FF, D])]
    win = din("win", [NIC, 128, KC * 128])
    wod = din("wo", [D, D])
    lnbc = din("lnbc", [6, 128, D])
    cstd = din("cst", [128, NCST])
    bod = din("bo", [1, D])
    identd = din("ident", [128, 128])
    estkd = din("estk", [128, 4, 32])
    uscr_t = nc.dram_tensor("uscr", [2, 128, 1056], BF16).ap()
    uscr = uscr_t
    yp = dout("yp", [SEQ, D])
    ys = dout("ys", [NS, D])
    nap = dout("nap", [30, DA])
    nbp = dout("nbp", [2, DA])
    nas = dout("nas", [NS * 30, DA])
    nbs = dout("nbs", [NS * 2, DA])

    stack = contextlib.ExitStack()
    with stack:
        def sb(name, shape, dtype):
            return stack.enter_context(nc.sbuf_tensor("sb_" + name, list(shape), dtype))

        P = Prog(nc, stack)
        ident = sb("ident", [128, 128], F32)
        cst = sb("cst", [128, NCST], F32)
        lnb = sb("lnb", [128, 2, D], F32)
        bohl = sb("bohl", [1, 2, D], BF16)
        ones = sb("ones", [1, 128], BF16)
        ones2 = sb("ones2", [2, 128], BF16)
        bo2 = sb("bo2", [2, D], BF16)
        xT = sb("xT", [128, KC, NTOKB], BF16)
        ring = sb("ring", [128, NSLOT, 2 * KC * 128], BF16)
        xres = sb("xres", [128, 9, D], F32)
        zt = sb("zt", [128, 2, D], F32)
        xn = sb("xn", [128, 2, D], F32)
        sg = sb("sg", [128, 2, 512], F32)
        sgS = sb("sgS", [128, NS], F32)
        NXB = 4
        xb = zt[:].bitcast(BF16).rearrange("p a (s d) -> p (a s) d", s=2)
        ident_bf = sb("ident_bf", [128, 128], BF16)
        sast = sb("sast", [128, 4, 512], F32)
        sbst = sb("sbst", [32, 512], F32)
        stt = sb("stt", [128, 2, 2, 6], F32)
        mv = sb("mv", [128, 2, 8], F32)
        mv_ln = mv
        mvc = sb("mvc", [128, 2, 8], F32)
        sttc = sb("sttc", [128, 2, 6], F32)
        neghalf = sb("neghalf", [128, 2], F32)
        hbg = sb("hbg", [128, 4], F32)
        halo = sb("halo", [128, 4, 32], F32)
        dummy = sb("dummy", [128, 8], F32)
        NDG = 8
        dg = sb("dg", [128, NDG, 128], BF16)
        E4 = sb("E4", [128, 4, 32], F32)
        R_HT = NFC * NTOKB
        R_WD = NFC * D
        RB = sb("R", [128, R_HT + R_WD], BF16)
        hT = RB[:, 0:R_HT].rearrange("p (f t) -> p f t", f=NFC)
        wd = RB[:, R_HT:R_HT + R_WD].rearrange("p (f d) -> p f d", f=NFC)
        RF = RB[:].bitcast(F32)
        off = 0

        def rf(n):
            nonlocal off
            a = RF[:, off:off + n]
            off += n
            return a
        UV = rf(4 * 1056).rearrange("p (c t) -> p c t", c=4)
        R2 = RB[:, 0:2 * 4 * R2W].rearrange("p (s g t) -> p s g t", s=2, g=4)
        acc = rf(4 * 1024).rearrange("p (c t) -> p c t", c=4)
        CA = rf(4 * 480).rearrange("p (c t) -> p c t", c=4)
        CB = rf(4 * 32).rearrange("p (c t) -> p c t", c=4)
        prod = rf(480)
        xnc = rf(2 * 512).rearrange("p (s t) -> p s t", s=2)
        USb = rf(4 * 16).rearrange("p (c t) -> p c t", c=4)
        VSb = rf(4 * 16).rearrange("p (c t) -> p c t", c=4)
        accS = rf(4 * 16).rearrange("p (c t) -> p c t", c=4)
        redS = rf(4 * 16).rearrange("p (c t) -> p c t", c=4)
        off_bf = off * 2
        catT = RB[:, off_bf:off_bf + KC * NTOKB].rearrange("p (k t) -> p k t", k=KC)
        off_bf += KC * NTOKB
        wo = RB[:, off_bf:off_bf + KC * D].rearrange("p (k d) -> p k d", k=KC)
        off_bf += KC * D
        U_bf = RB[:, off_bf:off_bf + 4 * 1056].rearrange("p (c t) -> p c t", c=4)
        off_bf += 4 * 1056
        assert off_bf <= R_HT + R_WD, (off_bf, R_HT + R_WD)

        ps = stack.enter_context(nc.psum_tensor("ps", [128, 8 * 512], F32))
        psb = ps[:].bitcast(BF16)

        def pst(b, n=1):
            return tuple(("ps", b + i) for i in range(n))

        P.op("sp", lambda e: e.dma_start(out=ident[:], in_=identd), writes=["ident"], dma="c_ident")
        P.op("sp", lambda e: e.dma_start(out=cst[:], in_=cstd), writes=["cst"], dma="c_cst")
        P.op("sp", lambda e: e.dma_start(out=E4[:], in_=estkd), writes=["E4"], dma="c_e4")
        P.op("sp", lambda e: e.dma_start(out=xn[0:1, 1, :], in_=bod), writes=[("xn", 1)], dma="c_bo")
        P.op("dve", lambda e: e.memset(ones[:], 1.0), writes=["ones"])
        P.op("dve", lambda e: e.tensor_copy(out=ident_bf[:], in_=ident[:]), reads=["ident"], writes=["ident_bf"])
        P.op("dve", lambda e: e.tensor_copy(out=bohl[0:1, 0, :], in_=xn[0:1, 1, :]),
             reads=[("xn", 1)], writes=["bohi"])
        P.op("dve", lambda e: e.tensor_copy(out=xn[0:1, 0, :], in_=bohl[0:1, 0, :]),
             reads=["bohi"], writes=[("xn", 0)])
        P.op("dve", lambda e: e.tensor_tensor(out=bohl[0:1, 1, :], in0=xn[0:1, 1, :], in1=xn[0:1, 0, :],
                                              op=ALU.subtract),
             reads=[("xn", 1), ("xn", 0)], writes=["bolo"])
        P.op("dve", lambda e: e.memset(ones2[:], 1.0), writes=["ones2"])
        P.op("act", lambda e: e.activation(out=dummy[0:2, 4:5], in_=ones2[0:2, 0:1], func=AF.Silu), reads=["ones2"],
             writes=["dummy_act"])
        P.op("dve", lambda e: e.memset(halo[:], 0.0), writes=["halo"] + [("haloA", c) for c in range(4)])
        P.op("dve", lambda e: e.memset(neghalf[:], -0.5), writes=["neghalf"])
        P.op("dve", lambda e: e.tensor_scalar(out=hbg[:, :], in0=cst[:, C_BIN + 4:C_BIN + 8], scalar1=0.5, scalar2=None,
                                              op0=ALU.mult),
             reads=["cst"], writes=["hbg"])

        rmode_tok = "RMODE"
        state = {"ring_i": 0, "psr": 0, "lnslot": 0, "dg_i": 0, "upar": 0, "xb_i": 0}

        def ring_slot():
            s = state["ring_i"] % NSLOT
            state["ring_i"] += 1
            return s

        preloaded = {}

        def fetch_win(b, fo, prefetch=False):
            key = ("win", b, fo)
            if key in preloaded and not prefetch:
                return preloaded.pop(key)
            s_ = ring_slot()
            P.op("pool", lambda e: e.dma_start(out=ring[:, s_, 0:KC * 128], in_=win[fo]),
                 writes=[("ring", s_)], dma="ring%d" % s_)
            if prefetch:
                preloaded[key] = s_
            return s_

        def fetch_wgu(b, which, f, prefetch=False):
            key = ("wgu", b, which, f)
            if key in preloaded and not prefetch:
                return preloaded.pop(key)
            s_ = ring_slot()
            extra = [("xres", 3)] if (b == 0 and which == 0 and 1 <= f < NSLOT) else []
            P.op("pool", lambda e: e.dma_start(out=ring[:, s_, :], in_=wgu[which][f], max_dma_last_dim=8192),
                 reads=extra, writes=[("ring", s_)], dma="ring%d" % s_)
            if prefetch:
                preloaded[key] = s_
            return s_

        def tiles_of(b):
            t = [(i, i * 128, 128) for i in range(8)]
            if b == 1:
                t.append((8, BLK, NS))
            return t

        def tgroups_of(b):
            g = [(0, 512), (512, 512)]
            if b == 1:
                g.append((BLK, NS))
            return g

        def load_ln(idx, b):
            P.op("sp", lambda e: e.dma_start(out=lnb[:, 0, :], in_=lnbc[2 * idx]),
                 writes=[("lnb", 0)], dma="c_lnb0")
            P.op("sp", lambda e: e.dma_start(out=lnb[:, 1, :], in_=lnbc[2 * idx + 1]),
                 writes=[("lnb", 1)], dma="c_lnb1")

        def rmode_switch():
            P.op("dve", lambda e: e.memset(dummy[:, 0:1], 0.0), reads=[], writes=[rmode_tok])

        def rstd_chain(n, sl, eps, mv=None, tg=""):
            if mv is None:
                mv = mv_ln
            P.op("dve", lambda e: e.tensor_scalar(out=mv[:n, sl, 4:5], in0=mv[:n, sl, 1:2], scalar1=eps, scalar2=None,
                                                  op0=ALU.add),
                 reads=[("mv" + tg, sl)], writes=[("mv1" + tg, sl)])
            P.op("pool", lambda e: e.tensor_tensor(out=mv[:n, sl, 2:3], in0=mv[:n, sl, 4:5], in1=neghalf[:n, 0:1], op=ALU.pow),
                 reads=[("mv1" + tg, sl), "neghalf"], writes=[("mv2" + tg, sl)])
            P.op("dve", lambda e: e.scalar_tensor_tensor(out=mv[:n, sl, 3:4], in0=mv[:n, sl, 0:1], scalar=-1.0,
                                                          in1=mv[:n, sl, 2:3], op0=ALU.mult, op1=ALU.mult),
                 reads=[("mv" + tg, sl), ("mv2" + tg, sl)], writes=[("mv3" + tg, sl)])

        def cast_x(tile):
            (t, tok0, n) = tile
            sl = state["xb_i"] % NXB
            state["xb_i"] += 1
            P.op("act", lambda e: e.activation(out=xb[:n, sl, :], in_=xres[:n, t, :], func=AF.Identity),
                 reads=[("xres", t)], writes=[("xb", sl)])
            return sl

        def transposes_to_xT(b, t, tok0, n, bank=None, xslot=None):
            if bank is None:
                bank = state["psr"] % 4
                state["psr"] += 1
            if xslot is None:
                sl = state["xb_i"] % NXB
                state["xb_i"] += 1
                P.op("act", lambda e: e.activation(out=xb[:n, sl, :], in_=xres[:n, t, :], func=AF.Identity),
                     reads=[("xres", t)], writes=[("xb", sl)])
            else:
                sl = xslot

            def pe(e):
                ins = None
                for k in range(KC):
                    ins = e.transpose(out=psb[:, bank * 1024 + k * n: bank * 1024 + (k + 1) * n],
                                      in_=xb[:n, sl, k * 128:(k + 1) * 128],
                                      identity=ident_bf[:n, :n])
                return ins
            P.op("pe", pe, reads=[("xb", sl), "ident_bf"], writes=pst(bank))

            def ev(e):
                return e.activation(out=xT[:, :, tok0:tok0 + n],
                                    in_=psb[:, bank * 1024: bank * 1024 + KC * n].rearrange("p (k n) -> p k n", k=KC),
                                    func=AF.Identity)
            P.op("act", ev, reads=pst(bank), writes=[("xT", tok0)])

        def ln_head(t, n, src_ps_banks, in_scale, eps, after_z=None):
            sl = state["lnslot"] % 2
            state["lnslot"] += 1
            pb = src_ps_banks
            P.op("dve", lambda e: e.scalar_tensor_tensor(out=xn[:n, sl, :], in0=xres[:n, t, :], scalar=in_scale,
                                                          in1=ps[:n, pb * 512:(pb + 2) * 512],
                                                          op0=ALU.mult, op1=ALU.add),
                 reads=[("xres", t)] + list(pst(pb, 2)), writes=[("xn", sl)])
            if after_z is not None:
                after_z()
            for h in range(2):
                P.op("dve", lambda e, h=h: e.bn_stats(out=stt[:n, sl, h, :], in_=xn[:n, sl, h * 512:(h + 1) * 512]),
                     reads=[("xn", sl)], writes=[("stt", sl, h)])
            P.op("dve", lambda e: e.bn_aggr(out=mv[:n, sl, 0:2], in_=stt[:n, sl, :, :].rearrange("p a b -> p (a b)")),
                 reads=[("stt", sl, 0), ("stt", sl, 1)], writes=[("mv", sl)])
            rstd_chain(n, sl, eps)
            return (t, n, sl)

        def ln_tail(ctx, final_store=None, make_bf=True):
            (t, n, sl) = ctx
            P.op("act", lambda e: e.activation(out=xn[:n, sl, :], in_=xn[:n, sl, :], func=AF.Identity,
                                               scale=mv[:n, sl, 2:3], bias=mv[:n, sl, 3:4]),
                 reads=[("xn", sl), ("mv2", sl), ("mv3", sl)], writes=[("xn", sl)])
            P.op("dve", lambda e: e.tensor_tensor(out=xn[:n, sl, :], in0=xn[:n, sl, :], in1=lnb[:n, 0, :], op=ALU.mult),
                 reads=[("xn", sl), ("lnb", 0)], writes=[("xn", sl)])
            if final_store is None:
                xs_ = None
                if make_bf:
                    xs_ = state["xb_i"] % NXB
                    state["xb_i"] += 1
                    P.op("dve", lambda e: e.tensor_tensor(out=xb[:n, xs_, :], in0=xn[:n, sl, :], in1=lnb[:n, 1, :], op=ALU.add),
                         reads=[("xn", sl), ("lnb", 1)], writes=[("xb", xs_)])
                P.op("dve", lambda e: e.tensor_tensor(out=xres[:n, t, :], in0=xn[:n, sl, :], in1=lnb[:n, 1, :], op=ALU.add),
                     reads=[("xn", sl), ("lnb", 1)], writes=[("xres", t)])
                return xs_
            dst = final_store
            P.op("dve", lambda e: e.tensor_tensor(out=xn[:n, sl, :], in0=xn[:n, sl, :], in1=lnb[:n, 1, :], op=ALU.add),
                 reads=[("xn", sl), ("lnb", 1)], writes=[("xn", sl)])
            P.op("sp", lambda e: e.dma_start(out=dst, in_=xn[:n, sl, :]), reads=[("xn", sl)],
                 dma="st_xn%d" % sl)
            return None

        def layer_norm_tile(t, n, src_ps_banks, in_scale, eps, b, final_store=None, after_z=None):
            return ln_tail(ln_head(t, n, src_ps_banks, in_scale, eps, after_z=after_z), final_store=final_store)

        def ffn(b, which, in_scale, eps, final, defer_last=False, pre=None):
            tiles = tiles_of(b)
            tgs = tgroups_of(b)
            wd_chunks = 11
            slots = {}

            def load_w(f):
                slots[f] = fetch_wgu(b, which, f)
                if 4 <= f < 4 + wd_chunks:
                    i = f - 4
                    P.op("pool", lambda e: e.dma_start(
                        out=wd[:, 2 * i:2 * i + 2, :],
                        in_=wdd[which][2 * i * 128:(2 * i + 2) * 128, :].rearrange("(f p) d -> p f d", p=128)),
                        reads=[rmode_tok], writes=[("wd", i)], dma="wd%d" % i)

            def u_group(f, tok0, n):
                s_ = slots[f]
                if n == 512:
                    par = state["upar"] % 2
                    state["upar"] += 1
                    bg, bu = par, 2 + par
                    sgt = sg[:, par, :]
                    sgtok = ("sg", par)
                else:
                    bg, bu = 4, 5
                    sgt = sgS[:, :]
                    sgtok = "sgS"

                def pe(e):
                    ins = None
                    for gu, bk in ((0, bg), (1, bu)):
                        for k in range(KC):
                            c0 = (gu * KC + k) * 128
                            ins = e.matmul(ps[:, bk * 512: bk * 512 + n], lhsT=ring[:, s_, c0:c0 + 128],
                                           rhs=xT[:, k, tok0:tok0 + n], start=(k == 0), stop=(k == KC - 1))
                    return ins
                xtoks = [("xT", t0) for (_, t0, nn) in tiles if t0 >= tok0 and t0 < tok0 + max(n, 1)]
                P.op("pe", pe, reads=[("ring", s_)] + xtoks, writes=[("ps", bg), ("ps", bu)])
                P.op("act", lambda e: e.activation(out=sgt[:, :n], in_=ps[:, bg * 512: bg * 512 + n], func=AF.Silu),
                     reads=[("ps", bg)], writes=[sgtok])
                P.op("dve", lambda e: e.tensor_tensor(
                    out=hT[:, f, tok0:tok0 + n], in0=sgt[:, :n], in1=ps[:, bu * 512: bu * 512 + n], op=ALU.mult),
                    reads=[sgtok, ("ps", bu), rmode_tok], writes=[("hT", f, tok0)])

            pre = list(pre) if pre else []
            npre = min(len(pre), NSLOT)
            for f in range(npre):
                load_w(f)
            for f in range(npre):
                u_group(f, *tgs[0])
                pre.pop(0)()
            for th in pre:
                th()
            for f in range(NFC):
                if f >= npre:
                    load_w(f)
                for gi, (tok0, n) in enumerate(tgs):
                    if f < npre and gi == 0:
                        continue
                    u_group(f, tok0, n)
            if which == 0:
                for fo in (12, 16, 13, 17):
                    fetch_win(b, fo, prefetch=True)
            elif b == 0:
                for f in range(NSLOT):
                    fetch_wgu(1, 0, f, prefetch=True)
            wd_toks = [("wd", i) for i in range(wd_chunks)]
            pending = None
            for (t, tok0, n) in tiles:
                yb = 4 + 2 * (t % 2)
                g0 = 0 if tok0 < 512 else (512 if tok0 < BLK else BLK)

                def pe(e, tok0=tok0, n=n, yb=yb):
                    ins = None
                    for f in range(NFC):
                        for h in range(2):
                            ins = e.matmul(ps[:n, (yb + h) * 512:(yb + h + 1) * 512], lhsT=hT[:, f, tok0:tok0 + n],
                                           rhs=wd[:, f, h * 512:(h + 1) * 512], start=(f == 0), stop=(f == NFC - 1))
                    return ins
                P.op("pe", pe, reads=[("hT", f, g0) for f in range(NFC)] + wd_toks + [rmode_tok], writes=pst(yb, 2))
                if pending is not None:
                    transposes_to_xT(b, *pending)
                    pending = None
                if final:
                    hook = None
                    if b == 0:
                        if t >= 1:
                            xpose_x_tile(1, tiles_of(1)[t - 1])
                        if t == 2:
                            xpose_x_tile(1, tiles_of(1)[8])
                        hook = (lambda t=t: load_x(1, tiles_of(1)[t]))
                    dst = yp[b * BLK + tok0: b * BLK + tok0 + n, :] if t < 8 else ys[:, :]
                    layer_norm_tile(t, n, yb, in_scale, eps, b, final_store=dst, after_z=hook)
                else:
                    xs_ = layer_norm_tile(t, n, yb, in_scale, eps, b)
                    pending = (t, tok0, n, None, xs_)
            if pending is not None and not defer_last:
                transposes_to_xT(b, *pending)
                pending = None
            return pending

        def load_x(b, tile):
            (t, tok0, n) = tile
            src = xp[b * BLK + tok0: b * BLK + tok0 + n, :] if t < 8 else xs[:, :]
            P.op("sp", lambda e: e.dma_start(out=xres[:n, t, :], in_=src), writes=[("xres", t)], dma="ld_x%d" % t)

        def xpose_x_tile(b, tile):
            (t, tok0, n) = tile
            transposes_to_xT(b, t, tok0, n)

        def load_x_tile(b, tile):
            load_x(b, tile)
            xpose_x_tile(b, tile)

        def early_sample_state():
            sca3 = sca.rearrange("(s j) c -> s j c", j=30)
            nas3 = nas.rearrange("(s j) c -> s j c", j=30)
            scb3 = scb.rearrange("(s j) c -> s j c", j=2)
            nbs3 = nbs.rearrange("(s j) c -> s j c", j=2)
            for r in range(4):
                rows = 128 if r < 3 else 96
                P.op("sp", lambda e, r=r, rows=rows: e.dma_start(out=sast[:rows, r, :], in_=sca[r * 128: r * 128 + rows, :]),
                     writes=["sast"] if r == 3 else [("sast", r)], dma="sa")
            P.op("sp", lambda e: e.dma_start(out=sbst[:, :], in_=scb), writes=["sbst"], dma="sbld")
            P.op("sp", lambda e: e.dma_start(out=nas3[:, 0:29, :], in_=sca3[:, 1:30, :]), dma="st_nas0")
            P.op("sp", lambda e: e.dma_start(out=nbs3[:, 0:1, :], in_=scb3[:, 1:2, :]), dma="st_nbs0")

        def mixer(b, pendT=None):
            tiles = tiles_of(b)
            tgs = tgroups_of(b)
            has_s = (b == 1)
            rmode_switch()
            MR = [rmode_tok]
            sca3 = sca.rearrange("(s j) c -> s j c", j=30)
            nas3 = nas.rearrange("(s j) c -> s j c", j=30)
            scb3 = scb.rearrange("(s j) c -> s j c", j=2)
            nbs3 = nbs.rearrange("(s j) c -> s j c", j=2)
            if has_s:
                for ch in range(4):
                    pb = ch % 2

                    def pe(e, ch=ch, pb=pb):
                        ins = None
                        for r in range(4):
                            rows = 128 if r < 3 else 96
                            ins = e.transpose(out=ps[:, pb * 512 + r * 128: pb * 512 + r * 128 + rows],
                                              in_=sast[:rows, r, ch * 128:(ch + 1) * 128],
                                              identity=ident[:rows, :rows])
                        return ins
                    P.op("pe", pe, reads=["sast", "ident"], writes=pst(pb))
                    P.op("act", lambda e, ch=ch, pb=pb: e.activation(out=CA[:, ch, :], in_=ps[:, pb * 512: pb * 512 + 480],
                                                                     func=AF.Identity),
                         reads=list(pst(pb)) + MR, writes=[("CA", ch)])
                pb = 2

                def pe2(e, pb=pb):
                    ins = None
                    for ch in range(4):
                        ins = e.transpose(out=ps[:, pb * 512 + ch * 32: pb * 512 + (ch + 1) * 32],
                                          in_=sbst[:32, ch * 128:(ch + 1) * 128], identity=ident[:32, :32])
                    return ins
                P.op("pe", pe2, reads=["sbst", "ident"], writes=pst(pb))
                P.op("act", lambda e, pb=pb: e.activation(out=CB[:, :, :], in_=ps[:, pb * 512: pb * 512 + 128].rearrange("p (c t) -> p c t", c=4),
                                                   func=AF.Identity),
                     reads=list(pst(pb)) + MR, writes=["CB"])

            def win_chunk(fo, consumer):
                s = fetch_win(b, fo)
                for gi, (tok0, n) in enumerate(tgs):
                    bk = state["psr"] % 4
                    state["psr"] += 1

                    def pe(e, tok0=tok0, n=n, bk=bk):
                        ins = None
                        for k in range(KC):
                            ins = e.matmul(ps[:, bk * 512: bk * 512 + n], lhsT=ring[:, s, k * 128:(k + 1) * 128],
                                           rhs=xT[:, k, tok0:tok0 + n], start=(k == 0), stop=(k == KC - 1))
                        return ins
                    xtoks = [("xT", t0) for (_, t0, nn) in tiles if t0 >= tok0 and t0 < tok0 + n]
                    P.op("pe", pe, reads=[("ring", s)] + xtoks, writes=[("ps", bk)])
                    consumer(gi, tok0, n, bk)

            def cc(col):
                return cst[:, col:col + 1]

            P.op("dve", lambda e: e.tensor_copy(out=UV[:, :, 0:2], in_=halo[:, :, 30:32]), reads=["halo"] + MR,
                 writes=[("UVh", c) for c in range(4)])
            for ch in range(4):
                s_gc = fetch_win(b, 12 + ch)
                s_bv = fetch_win(b, 16 + ch)
                for gi, (tok0, n) in enumerate(tgs):
                    b1 = state["psr"] % 4
                    b2 = (state["psr"] + 1) % 4
                    state["psr"] += 2
                    xtoks = [("xT", t0) for (_, t0, nn) in tiles if t0 >= tok0 and t0 < tok0 + n]
                    for (slot_w, bk) in ((s_gc, b1), (s_bv, b2)):
                        def pe(e, tok0=tok0, n=n, bk=bk, slot_w=slot_w):
                            ins = None
                            for k in range(KC):
                                ins = e.matmul(ps[:, bk * 512: bk * 512 + n], lhsT=ring[:, slot_w, k * 128:(k + 1) * 128],
                                               rhs=xT[:, k, tok0:tok0 + n], start=(k == 0), stop=(k == KC - 1))
                            return ins
                        P.op("pe", pe, reads=[("ring", slot_w)] + xtoks, writes=[("ps", bk)])
                    sl = gi % 2
                    P.op("act", lambda e, n=n, b1=b1, sl=sl, ch=ch: e.activation(
                        out=sg[:, sl, :n], in_=ps[:, b1 * 512: b1 * 512 + n], func=AF.Identity, bias=cc(C_BIN + 12 + ch)),
                        reads=[("ps", b1), "cst"], writes=[("sg", sl)])
                    if tok0 < BLK:
                        dst = UV[:, ch, 2 + tok0: 2 + tok0 + n]
                        wtok = ("UV", ch, tok0)
                    else:
                        dst = VSb[:, ch, :n]
                        wtok = ("VS", ch)
                    P.op("dve", lambda e, n=n, b2=b2, sl=sl, ch=ch, dst=dst: e.scalar_tensor_tensor(
                        out=dst, in0=ps[:, b2 * 512: b2 * 512 + n], scalar=cc(C_BIN + 16 + ch), in1=sg[:, sl, :n],
                        op0=ALU.add, op1=ALU.mult),
                        reads=[("ps", b2), ("sg", sl), "cst"] + MR, writes=[wtok])
                    if ch == 0 and gi == 0 and pendT is not None:
                        transposes_to_xT(b, *pendT)
                wb = C_WB + ch * KB
                uvr = [("UV", ch, 0), ("UV", ch, 512), ("UVh", ch)]
                P.op("dve", lambda e, ch=ch, wb=wb: e.tensor_scalar(out=acc[:, ch, :], in0=UV[:, ch, 0:BLK], scalar1=cc(wb),
                                                                   scalar2=None, op0=ALU.mult),
                     reads=uvr + ["cst"] + MR, writes=[("acc", ch)])
                for j in (1, 2):
                    P.op("dve", lambda e, ch=ch, wb=wb, j=j: e.scalar_tensor_tensor(
                        out=acc[:, ch, :], in0=UV[:, ch, j:j + BLK], scalar=cc(wb + j), in1=acc[:, ch, :],
                        op0=ALU.mult, op1=ALU.add),
                        reads=uvr + [("acc", ch), "cst"], writes=[("acc", ch)])
                if has_s:
                    cb3 = CB[:, ch, :].rearrange("p (s j) -> p s j", j=2)
                    P.op("dve", lambda e, ch=ch, wb=wb, cb3=cb3: e.tensor_scalar(
                        out=accS[:, ch, :], in0=cb3[:, :, 0], scalar1=cc(wb), scalar2=None, op0=ALU.mult),
                        reads=["CB", "cst"] + MR, writes=[("accS", ch)])
                    P.op("dve", lambda e, ch=ch, wb=wb, cb3=cb3: e.scalar_tensor_tensor(
                        out=accS[:, ch, :], in0=cb3[:, :, 1], scalar=cc(wb + 1), in1=accS[:, ch, :],
                        op0=ALU.mult, op1=ALU.add),
                        reads=["CB", "cst", ("accS", ch)], writes=[("accS", ch)])
                    P.op("dve", lambda e, ch=ch, wb=wb: e.scalar_tensor_tensor(
                        out=accS[:, ch, :], in0=VSb[:, ch, :], scalar=cc(wb + 2), in1=accS[:, ch, :],
                        op0=ALU.mult, op1=ALU.add),
                        reads=[("VS", ch), "cst", ("accS", ch)], writes=[("accS", ch)])

                def cons_gb(gi, tok0, n, bk, ch=ch):
                    if tok0 < BLK:
                        src = acc[:, ch, tok0:tok0 + n]
                        rt = ("acc", ch)
                    else:
                        src = accS[:, ch, :n]
                        rt = ("accS", ch)
                    P.op("dve", lambda e: e.scalar_tensor_tensor(
                        out=catT[:, 4 + ch, tok0:tok0 + n], in0=ps[:, bk * 512: bk * 512 + n], scalar=cc(C_BIN + 8 + ch),
                        in1=src, op0=ALU.add, op1=ALU.mult),
                        reads=[("ps", bk), rt, "cst"] + MR, writes=[("catT", 4 + ch, tok0)])
                win_chunk(8 + ch, cons_gb)
            P.op("dve", lambda e: e.tensor_copy(out=halo[:, :, 30:32], in_=UV[:, :, BLK:BLK + 2]),
                 reads=[("UV", c, 512) for c in range(4)], writes=["halo"])
            if b == 1:
                pb = 3

                def pev(e, pb=pb):
                    ins = None
                    for ch in range(4):
                        ins = e.transpose(out=ps[:2, pb * 512 + ch * 128: pb * 512 + (ch + 1) * 128],
                                          in_=UV[:, ch, BLK:BLK + 2], identity=ident[:, :])
                    return ins
                P.op("pe", pev, reads=[("UV", c, 512) for c in range(4)] + ["ident"], writes=pst(pb))
                P.op("act", lambda e, pb=pb: e.activation(out=xn[:2, 1, 0:512], in_=ps[:2, pb * 512:(pb + 1) * 512], func=AF.Identity),
                     reads=pst(pb), writes=[("xn", 1)])
                P.op("sp", lambda e: e.dma_start(out=nbp, in_=xn[:2, 1, 0:512]), reads=[("xn", 1)], dma="st_nbp")
                pb = 2

                def pevs(e, pb=pb):
                    ins = None
                    for ch in range(4):
                        ins = e.transpose(out=ps[:NS, pb * 512 + ch * 128: pb * 512 + (ch + 1) * 128],
                                          in_=VSb[:, ch, :], identity=ident[:, :])
                    return ins
                P.op("pe", pevs, reads=[("VS", c) for c in range(4)] + ["ident"], writes=pst(pb))
                P.op("act", lambda e, pb=pb: e.activation(out=xn[:NS, 0, 512:1024], in_=ps[:NS, pb * 512:(pb + 1) * 512], func=AF.Identity),
                     reads=pst(pb), writes=[("xn", 0)])
                P.op("sp", lambda e: e.dma_start(out=nbs3[:, 1, :], in_=xn[:NS, 0, 512:1024]), reads=[("xn", 0)],
                     dma="st_nbs1")

            P.op("pool", lambda e: e.dma_start(out=wo[:, :, :], in_=wod.rearrange("(k p) d -> p k d", p=128)),
                 reads=MR, writes=["wo"], dma="wo")
            P.op("dve", lambda e: e.memset(dummy[:, 7:8], 0.0),
                 writes=["R2g"] + [("UV", c, g_) for c in range(4) for g_ in (0, 512)] + [("UVh", c) for c in range(4)])
            P.op("dve", lambda e: e.memset(U_bf[:, :, 1054:1056], 0.0), reads=MR, writes=["Ubfz"])
            P.op("dve", lambda e: e.tensor_copy(out=U_bf[:, :, 0:30], in_=halo[:, :, 0:30]),
                 reads=[("haloA", c) for c in range(4)] + MR, writes=[("Ubfh", c) for c in range(4)])

            def gv_matmuls(ch):
                s_g = fetch_win(b, 4 + ch)
                s_v = fetch_win(b, ch)
                for gi, (tok0, n) in enumerate(tgs):
                    b1 = state["psr"] % 4
                    b2 = (state["psr"] + 1) % 4
                    state["psr"] += 2
                    xtoks = [("xT", t0) for (_, t0, nn) in tiles if t0 >= tok0 and t0 < tok0 + n]
                    for (slot_w, bk) in ((s_g, b1), (s_v, b2)):
                        def pe(e, tok0=tok0, n=n, bk=bk, slot_w=slot_w):
                            ins = None
                            for k in range(KC):
                                ins = e.matmul(ps[:, bk * 512: bk * 512 + n], lhsT=ring[:, slot_w, k * 128:(k + 1) * 128],
                                               rhs=xT[:, k, tok0:tok0 + n], start=(k == 0), stop=(k == KC - 1))
                            return ins
                        P.op("pe", pe, reads=[("ring", slot_w)] + xtoks, writes=[("ps", bk)])
                    sl = gi % 2
                    P.op("act", lambda e, n=n, b1=b1, sl=sl, ch=ch: e.activation(
                        out=sg[:, sl, :n], in_=ps[:, b1 * 512: b1 * 512 + n], func=AF.Tanh, scale=0.5,
                        bias=hbg[:, ch:ch + 1]),
                        reads=[("ps", b1), "hbg"], writes=[("sg", sl)])
                    P.op("dve", lambda e, n=n, sl=sl: e.tensor_scalar(out=sg[:, sl, :n], in0=sg[:, sl, :n], scalar1=0.5,
                                                                     scalar2=0.5, op0=ALU.mult, op1=ALU.add),
                         reads=[("sg", sl)], writes=[("sg", sl)])
                    if tok0 < BLK:
                        dst = U_bf[:, ch, 30 + tok0: 30 + tok0 + n]
                        wtok = ("Ubf", ch, tok0)
                    else:
                        dst = USb[:, ch, :n]
                        wtok = ("US", ch)
                    P.op("dve", lambda e, n=n, b2=b2, sl=sl, ch=ch, dst=dst: e.scalar_tensor_tensor(
                        out=dst, in0=ps[:, b2 * 512: b2 * 512 + n], scalar=cc(C_BIN + ch), in1=sg[:, sl, :n],
                        op0=ALU.add, op1=ALU.mult),
                        reads=[("ps", b2), ("sg", sl), "cst"] + MR, writes=[wtok])
                    if tok0 == 512:
                        P.op("dve", lambda e, b2=b2, sl=sl, ch=ch: e.scalar_tensor_tensor(
                            out=halo[:, ch, 0:30], in0=ps[:, b2 * 512 + 482: b2 * 512 + 512], scalar=cc(C_BIN + ch),
                            in1=sg[:, sl, 482:512], op0=ALU.add, op1=ALU.mult),
                            reads=[("ps", b2), ("sg", sl), "cst"], writes=[("haloA", ch)])
            def replicas(ch):
                rs = ch % 2
                P.op("sp", lambda e: e.dma_start(out=uscr[rs], in_=U_bf[:, ch, :]),
                     reads=[("Ubf", ch, 0), ("Ubf", ch, 512), ("Ubfh", ch), "Ubfz"]
                     + ([("R2", ch - 2)] if ch >= 2 else []),
                     writes=[("uscr", rs)], dma="us_%d" % rs)
                src = uscr_t[rs].rearrange("(g c) t -> c g t", g=4)
                for s_ in range(4):
                    P.op("sp", lambda e, s_=s_: e.dma_start(out=R2[32 * s_:32 * s_ + 32, rs, :, :],
                                                             in_=src[:, :, s_:s_ + R2W]),
                         reads=[("uscr", rs), "R2g"] + ([("R2done", ch - 2)] if ch >= 2 else []),
                         writes=[("R2", ch)] if s_ == 3 else [("R2", ch, s_)], dma="r2_%d" % rs)

            R2toks = {ch: [("R2", ch)] for ch in range(4)}

            def conv_matmuls(ch):
                wa = C_WA + ch * KA
                cb = (4, 5) if ch % 2 == 0 else (6, 7)
                rs = ch % 2
                for q in range(8):
                    ds_ = state["dg_i"] % NDG
                    state["dg_i"] += 1
                    wcol = C_WP + ch * 32 + q * 4
                    P.op("dve", lambda e, ds_=ds_, wcol=wcol: e.tensor_tensor(
                        out=dg[:, ds_, :].rearrange("p (g c) -> p g c", g=4), in0=E4[:, :, :],
                        in1=cst[:, wcol:wcol + 4].unsqueeze(2).to_broadcast([128, 4, 32]), op=ALU.mult),
                        reads=["E4", "cst"], writes=[("dg", ds_)])

                    def pe(e, ds_=ds_, q=q, rs=rs, cb=cb):
                        ins = None
                        for gi in range(2):
                            for g_ in range(4):
                                ins = e.matmul(ps[32 * g_:32 * g_ + 32, cb[gi] * 512:(cb[gi] + 1) * 512],
                                               lhsT=dg[:, ds_, 32 * g_:32 * g_ + 32],
                                               rhs=R2[:, rs, g_, gi * 512 + 4 * q: gi * 512 + 4 * q + 512],
                                               start=(q == 0), stop=(q == 7), tile_position=(0, 32 * g_))
                        return ins
                    P.op("pe", pe, reads=[("dg", ds_)] + R2toks[ch] + MR, writes=[("ps", cb[0]), ("ps", cb[1])])
                for gi in range(2):
                    P.op("act", lambda e, gi=gi, ch=ch, cb=cb: e.activation(
                        out=acc[:, ch, gi * 512:(gi + 1) * 512], in_=ps[:, cb[gi] * 512:(cb[gi] + 1) * 512], func=AF.Identity,
                        bias=cc(C_BA + ch)),
                        reads=[("ps", cb[gi]), "cst"] + MR,
                        writes=[("acc", ch), ("R2done", ch)] if gi == 1 else [("acc", ch, "lo"), ("acc", ch)])
                if has_s:
                    ca3 = CA[:, ch, :].rearrange("p (s j) -> p s j", j=30)
                    w30 = cst[:, wa:wa + 30].unsqueeze(1).to_broadcast([128, NS, 30])
                    P.op("dve", lambda e, ca3=ca3, w30=w30: e.tensor_tensor(
                        out=prod.rearrange("p (s j) -> p s j", j=30), in0=ca3, in1=w30, op=ALU.mult),
                        reads=[("CA", ch), "cst"] + MR, writes=["prod"])
                    P.op("dve", lambda e, ch=ch: e.tensor_reduce(out=redS[:, ch, :], in_=prod.rearrange("p (s j) -> p s j", j=30),
                                                                 axis=AX.X, op=ALU.add),
                         reads=["prod"], writes=[("redS", ch)])
                    P.op("dve", lambda e, ch=ch, wa=wa: e.scalar_tensor_tensor(
                        out=accS[:, ch, :], in0=USb[:, ch, :], scalar=cc(wa + 30), in1=redS[:, ch, :],
                        op0=ALU.mult, op1=ALU.add),
                        reads=[("US", ch), ("redS", ch), "cst"], writes=[("accS", ch)])
                    P.op("dve", lambda e, ch=ch: e.tensor_scalar(out=accS[:, ch, :], in0=accS[:, ch, :], scalar1=cc(C_BA + ch),
                                                                 scalar2=None, op0=ALU.add),
                         reads=[("accS", ch), "cst"], writes=[("accS", ch)])

            gv_matmuls(0)
            replicas(0)
            gv_matmuls(1)
            replicas(1)
            gv_matmuls(2)
            conv_matmuls(0)
            replicas(2)
            conv_matmuls(1)
            gv_matmuls(3)
            replicas(3)
            conv_matmuls(2)
            conv_matmuls(3)
            if b == 1:
                pb = 3

                def peu(e, pb=pb):
                    ins = None
                    for ch in range(4):
                        ins = e.transpose(out=ps[:30, pb * 512 + ch * 128: pb * 512 + (ch + 1) * 128],
                                          in_=halo[:, ch, 0:30], identity=ident[:, :])
                    return ins
                P.op("pe", peu, reads=[("haloA", c) for c in range(4)] + ["ident"], writes=pst(pb))
                P.op("act", lambda e, pb=pb: e.activation(out=xn[:30, 1, 512:1024], in_=ps[:30, pb * 512:(pb + 1) * 512], func=AF.Identity),
                     reads=pst(pb), writes=[("xn", 1)])
                P.op("sp", lambda e: e.dma_start(out=nap, in_=xn[:30, 1, 512:1024]), reads=[("xn", 1)], dma="st_nap")
                pb = 2

                def peus(e, pb=pb):
                    ins = None
                    for ch in range(4):
                        ins = e.transpose(out=ps[:NS, pb * 512 + ch * 128: pb * 512 + (ch + 1) * 128],
                                          in_=USb[:, ch, :], identity=ident[:, :])
                    return ins
                P.op("pe", peus, reads=[("US", c) for c in range(4)] + ["ident"], writes=pst(pb))
                P.op("act", lambda e, pb=pb: e.activation(out=xn[:NS, 0, 0:512], in_=ps[:NS, pb * 512:(pb + 1) * 512], func=AF.Identity),
                     reads=pst(pb), writes=[("xn", 0)])
                P.op("sp", lambda e: e.dma_start(out=nas3[:, 29, :], in_=xn[:NS, 0, 0:512]), reads=[("xn", 0)],
                     dma="st_nas1")
            load_ln(1, b)

            def S1(tile):
                (t, tok0, n) = tile
                pb = t % 2
                sl = t % 2
                if t < 8:
                    srcs = [acc[:, ch, tok0:tok0 + n] for ch in range(4)]
                    rts = [("acc", ch) for ch in range(4)] + [("acc", ch, "lo") for ch in range(4)]
                else:
                    srcs = [accS[:, ch, :n] for ch in range(4)]
                    rts = [("accS", ch) for ch in range(4)]

                def pe(e):
                    ins = None
                    for ch in range(4):
                        ins = e.transpose(out=ps[:n, pb * 512 + ch * 128: pb * 512 + (ch + 1) * 128], in_=srcs[ch],
                                          identity=ident[:, :])
                    return ins
                P.op("pe", pe, reads=rts + ["ident"] + MR, writes=pst(pb))

            def S1post(tile):
                (t, tok0, n) = tile
                pb = t % 2
                sl = t % 2
                lsl = t % 2
                P.op("dve", lambda e: e.bn_stats(out=sttc[:n, lsl, :], in_=ps[:n, pb * 512:(pb + 1) * 512]),
                     reads=pst(pb), writes=[("sttc", lsl)])
                P.op("dve", lambda e: e.bn_aggr(out=mvc[:n, lsl, 0:2], in_=sttc[:n, lsl, :]),
                     reads=[("sttc", lsl)], writes=[("mvc", lsl)])
                rstd_chain(n, lsl, EPS, mv=mvc, tg="c")
                P.op("act", lambda e: e.activation(
                    out=xnc[:n, sl, :], in_=ps[:n, pb * 512:(pb + 1) * 512], func=AF.Identity,
                    scale=mvc[:n, lsl, 2:3], bias=mvc[:n, lsl, 3:4]),
                    reads=list(pst(pb)) + [("mv2c", lsl), ("mv3c", lsl)] + MR, writes=[("xnc", sl)])

            def S2(tile):
                (t, tok0, n) = tile
                sl = t % 2
                pb2 = 2

                def pe2(e):
                    ins = None
                    for ch in range(4):
                        ins = e.transpose(out=ps[:, pb2 * 512 + ch * n: pb2 * 512 + (ch + 1) * n],
                                          in_=xnc[:n, sl, ch * 128:(ch + 1) * 128], identity=ident[:n, :n])
                    return ins
                P.op("pe", pe2, reads=[("xnc", sl), "ident"], writes=pst(pb2))
                for ch in range(4):
                    P.op("act", lambda e, ch=ch: e.activation(
                        out=catT[:, ch, tok0:tok0 + n], in_=ps[:, pb2 * 512 + ch * n: pb2 * 512 + (ch + 1) * n],
                        func=AF.Silu, scale=cc(C_LG + ch), bias=cc(C_LB + ch)),
                        reads=list(pst(pb2)) + ["cst"] + MR, writes=[("catT", ch, tok0)])

            def S3(tile):
                (t, tok0, n) = tile
                yb = 4 + 2 * (t % 2)

                def pe(e):
                    ins = None
                    for k in range(KC):
                        for h in range(2):
                            ins = e.matmul(ps[:n, (yb + h) * 512:(yb + h + 1) * 512], lhsT=catT[:, k, tok0:tok0 + n],
                                           rhs=wo[:, k, h * 512:(h + 1) * 512], start=(k == 0), stop=False)
                    for h in range(2):
                        ins = e.matmul(ps[:n, (yb + h) * 512:(yb + h + 1) * 512], lhsT=ones2[0:2, :n],
                                       rhs=bo2[0:2, h * 512:(h + 1) * 512], start=False, stop=True)
                    return ins
                g0 = 0 if tok0 < 512 else (512 if tok0 < BLK else BLK)
                P.op("pe", pe, reads=[("catT", k, tok0) for k in range(4)] + [("catT", k, g0) for k in range(4, 8)]
                     + ["wo", "ones2", "bo2"] + MR,
                     writes=pst(yb, 2))

            lnctx = {}
            xslots = {}

            def S3a(tile):
                (t, tok0, n) = tile
                lnctx[t] = ln_head(t, n, 4 + 2 * (t % 2), ALPHA, EPS)

            def S3b(tile):
                (t, tok0, n) = tile
                ln_tail(lnctx[t], make_bf=False)

            def S3c(tile):
                xslots[tile[0]] = cast_x(tile)

            def S4(tile, bank=3):
                (t, tok0, n) = tile
                transposes_to_xT(b, t, tok0, n, bank=bank, xslot=xslots[t])

            nt = len(tiles)
            def step(i):
                if i < nt:
                    S1(tiles[i])
                if 0 <= i - 1 < nt:
                    S1post(tiles[i - 1])
                if 0 <= i - 2 < nt:
                    S2(tiles[i - 2])
                if 0 <= i - 3 < nt:
                    S3(tiles[i - 3])
                if 0 <= i - 4 < nt:
                    S3a(tiles[i - 4])
                if 0 <= i - 5 < nt:
                    S3b(tiles[i - 5])
                if 0 <= i - 6 < nt:
                    S3c(tiles[i - 6])
                if 0 <= i - 7 < nt:
                    if i < nt + 3:
                        S4(tiles[i - 7])
                    else:
                        S4(tiles[i - 7], bank=4 + 2 * ((nt - 2) % 2) + (i % 2))
            for i in range(nt + 3):
                if i == 2:
                    for f in range(NSLOT):
                        fetch_wgu(b, 1, f, prefetch=True)
                step(i)
            rmode_switch()
            return [(lambda i=i: step(i)) for i in range(nt + 3, nt + 7)]


        for b in range(2):
            if b == 0:
                t0s = tiles_of(0)
                for tile in t0s:
                    load_x(0, tile)
                cs = {}
                for j in range(4 + 2):
                    if j < 4:
                        cs[j] = cast_x(t0s[j])
                    if j >= 2:
                        (t, tok0, n) = t0s[j - 2]
                        transposes_to_xT(0, t, tok0, n, xslot=cs[j - 2])
                load_x(1, tiles_of(1)[8])
                P.op("sp", lambda e: e.dma_start(out=bo2[0:1, :], in_=bohl[0:1, 0, :]), reads=["bohi"], writes=[("bo2", 0)],
                     dma="c_bo2")
                P.op("sp", lambda e: e.dma_start(out=bo2[1:2, :], in_=bohl[0:1, 1, :]), reads=["bolo"], writes=["bo2"],
                     dma="c_bo2")
                early_sample_state()
                pre1 = [(lambda tile=tile: xpose_x_tile(0, tile)) for tile in t0s[4:]]
            else:
                pre1 = [lambda: xpose_x_tile(1, tiles_of(1)[7])]
            load_ln(0, b)
            pend = ffn(b, 0, 2.0 * ALPHA, 4.0 * EPS, final=False, defer_last=True, pre=pre1)
            tail = mixer(b, pend)
            tail.append(lambda b=b: load_ln(2, b))
            ffn(b, 1, 2.0 * ALPHA, 4.0 * EPS, final=True, pre=tail)

        P.finalize()
    return nc


_NC_CACHE = {}


def _prep_weights(inputs):
    f32 = np.float32

    def tile_cols(w, nchunk):
        return np.ascontiguousarray(w.reshape(KC, 128, nchunk, 128).transpose(2, 1, 0, 3))
    out = {}
    for i, (g, u, d) in enumerate((("f1_wg", "f1_wu", "f1_wd"), ("f2_wg", "f2_wu", "f2_wd"))):
        tg = tile_cols(np.asarray(inputs[g][0], f32), NFC)
        tu = tile_cols(np.asarray(inputs[u][0], f32), NFC)
        out["wgu%d" % (i + 1)] = np.ascontiguousarray(np.stack([tg, tu], axis=2).reshape(NFC, 128, 2 * KC * 128))
        out["wd%d" % (i + 1)] = np.ascontiguousarray(np.asarray(inputs[d][0], f32))
    out["win"] = np.ascontiguousarray(tile_cols(np.asarray(inputs["w_in"][0], f32), NIC).reshape(NIC, 128, KC * 128))
    out["wo"] = np.ascontiguousarray(np.asarray(inputs["w_o"][0], f32))
    ln = [inputs[k][0] for k in ("ln_f1_g", "ln_f1_b", "ln_mix_g", "ln_mix_b", "ln_f2_g", "ln_f2_b")]
    out["lnbc"] = np.ascontiguousarray(np.broadcast_to(np.stack(ln).astype(f32)[:, None, :], (6, 128, D)))
    cst = np.zeros((128, NCST), f32)
    cst[:, C_BIN:C_BIN + NIC] = np.asarray(inputs["b_in"][0], f32).reshape(NIC, 128).T
    wa = np.asarray(inputs["w_dw_a"][0], f32)
    cst[:, C_WA:C_WA + 4 * KA] = wa.reshape(KA, 4, 128).transpose(2, 1, 0).reshape(128, 4 * KA)
    cst[:, C_BA:C_BA + 4] = np.asarray(inputs["b_dw_a"][0], f32).reshape(4, 128).T
    cst[:, C_LG:C_LG + 4] = np.asarray(inputs["ln_conv_g"][0], f32).reshape(4, 128).T
    cst[:, C_LB:C_LB + 4] = np.asarray(inputs["ln_conv_b"][0], f32).reshape(4, 128).T
    wb = np.asarray(inputs["w_dw_b"][0], f32)
    cst[:, C_WB:C_WB + 4 * KB] = wb.reshape(KB, 4, 128).transpose(2, 1, 0).reshape(128, 4 * KB)
    wp = np.zeros((4, 32, 4, 8, 4), f32)
    wa4 = wa.reshape(KA, 4, 4, 32)
    for q in range(8):
        for s_ in range(4):
            j = 4 * q + s_
            if j < KA:
                wp[s_, :, :, q, :] = wa4[j].transpose(2, 0, 1)
    cst[:, C_WP:C_WP + 128] = wp.reshape(128, 128)
    out["cst"] = cst
    est = np.zeros((4, 32, 4, 32), f32)
    est[:, np.arange(32), :, np.arange(32)] = 1.0
    out["estk"] = np.ascontiguousarray(est.reshape(128, 4, 32))
    out["bo"] = np.ascontiguousarray(np.asarray(inputs["b_o"], f32).reshape(1, D))
    out["ident"] = np.eye(128, dtype=f32)
    return out


def kernel(**inputs):
    f32 = np.float32
    if "nc" not in _NC_CACHE:
        _NC_CACHE["nc"] = build_nc()
    nc = _NC_CACHE["nc"]
    shared = _prep_weights(inputs)
    x_prompt = np.asarray(inputs["x_prompt"], f32)
    x_sample = np.asarray(inputs["x_sample"], f32)
    sca = np.asarray(inputs["state_conv_a"], f32)[0]
    scb = np.asarray(inputs["state_conv_b"], f32)[0]
    in_maps = []
    for c in range(NCORE):
        m = dict(shared)
        m["xp"] = np.ascontiguousarray(x_prompt[c])
        m["xs"] = np.ascontiguousarray(x_sample[c * NS:(c + 1) * NS, 0, :])
        m["sca"] = np.ascontiguousarray(sca[c * NS:(c + 1) * NS].reshape(NS * 30, DA))
        m["scb"] = np.ascontiguousarray(scb[c * NS:(c + 1) * NS].reshape(NS * 2, DA))
        in_maps.append(m)
    res = run_bass_kernel_spmd(nc, in_maps, core_ids=list(range(NCORE)))
    r = res.results
    y_prompt = np.stack([r[c]["yp"] for c in range(NCORE)]).astype(f32)
    y_sample = np.concatenate([r[c]["ys"] for c in range(NCORE)], axis=0).reshape(NCORE * NS, 1, D).astype(f32)
    nap = np.stack([r[c]["nap"] for c in range(NCORE)])[None].astype(f32)
    nbp = np.stack([r[c]["nbp"] for c in range(NCORE)])[None].astype(f32)
    nas = np.concatenate([r[c]["nas"].reshape(NS, 30, DA) for c in range(NCORE)], axis=0)[None].astype(f32)
    nbs = np.concatenate([r[c]["nbs"].reshape(NS, 2, DA) for c in range(NCORE)], axis=0)[None].astype(f32)
    return (y_prompt, y_sample, nap, nbp, nas, nbs)
```

```python
import contextlib
import numpy as np
import concourse.bass as bass
import concourse.mybir as mybir
from concourse.bass_utils import run_bass_kernel_spmd

F32 = mybir.dt.float32
BF16 = mybir.dt.bfloat16
AF = mybir.ActivationFunctionType
ALU = mybir.AluOpType
AX = mybir.AxisListType

D = 1024
FF = 2816
NFC = FF // 128
KC = D // 128
DA = 512
DIN = 2560
NIC = DIN // 128
SEQ = 2048
NS = 16
KA = 31
KB = 3
NCORE = 8
BLK = 1024
NTOKB = BLK + NS
ALPHA = float(2.0 ** 0.25)
EPS = 1e-5
NSLOT = 4

C_BIN = 0
C_WA = C_BIN + NIC
C_BA = C_WA + 4 * KA
C_LG = C_BA + 4
C_LB = C_LG + 4
C_WB = C_LB + 4
C_WP = C_WB + 4 * KB
NCST = C_WP + 128
R2W = 1052


class _Op:
    __slots__ = ("eng", "fn", "deps", "dma", "idx", "inc", "waits", "count")


class Prog:
    ENGS = ("pe", "act", "dve", "pool", "sp")

    def __init__(self, nc, stack):
        self.nc = nc
        self.stack = stack
        self.ops = []
        self.lastw = {}
        self.readers = {}
        self.dma_cnt = {}
        self.sems = {}
        self.eng_ops = {e: [] for e in self.ENGS}

    def sem(self, name):
        if name not in self.sems:
            self.sems[name] = self.stack.enter_context(self.nc.semaphore(name))
        return self.sems[name]

    def op(self, eng, fn, reads=(), writes=(), dma=None):
        o = _Op()
        o.eng = eng
        o.fn = fn
        o.inc = False
        o.waits = []
        o.count = 0
        oid = len(self.ops)
        deps = {}
        for r in reads:
            w = self.lastw.get(r)
            if w is not None:
                deps[w] = True
        for wt in writes:
            w = self.lastw.get(wt)
            if w is not None and w not in deps:
                deps[w] = False
            for rd in self.readers.get(wt, ()):
                if rd not in deps:
                    deps[rd] = False
        deps.pop(oid, None)
        o.deps = deps
        if dma is not None:
            self.dma_cnt[dma] = self.dma_cnt.get(dma, 0) + 16
            o.dma = (dma, self.dma_cnt[dma])
            self.sem(dma)
        else:
            o.dma = None
        for r in reads:
            self.readers.setdefault(r, []).append(oid)
        for wt in writes:
            self.lastw[wt] = oid
            self.readers[wt] = []
        o.idx = len(self.eng_ops[eng])
        self.eng_ops[eng].append(o)
        self.ops.append(o)
        return oid

    def finalize(self, final_wait_eng="sp"):
        nc = self.nc
        esem = {e: self.sem("S_" + e) for e in ("pe", "act", "dve", "pool")}
        seen_eng = {e: {f: -1 for f in self.ENGS} for e in self.ENGS}
        seen_dma = {e: {} for e in self.ENGS}
        pending = []
        for o in self.ops:
            e = o.eng
            dma_w = {}
            eng_w = {}
            for pid, raw in o.deps.items():
                p = self.ops[pid]
                if p.dma is not None:
                    s, v = p.dma
                    if seen_dma[e].get(s, 0) >= v:
                        continue
                    if dma_w.get(s, 0) < v:
                        dma_w[s] = v
                else:
                    if p.eng == e:
                        if e == "pe" or e == "sp":
                            continue
                    if seen_eng[e][p.eng] >= p.idx:
                        continue
                    if eng_w.get(p.eng, -1) < p.idx:
                        eng_w[p.eng] = p.idx
            for s, v in dma_w.items():
                seen_dma[e][s] = v
                o.waits.append(("dma", s, v))
            for f, idx in eng_w.items():
                seen_eng[e][f] = idx
                prod = self.eng_ops[f][idx]
                prod.inc = True
                o.waits.append(("eng", f, prod))
        for e in ("pe", "act", "dve", "pool"):
            c = 0
            for o in self.eng_ops[e]:
                if o.inc:
                    c += 1
                    o.count = c
        final = [(self.sems[s], v) for s, v in self.dma_cnt.items()]
        sems = self.sems

        def body_for(ename):
            def body(eng):
                for o in self.eng_ops[ename]:
                    for w in o.waits:
                        if w[0] == "dma":
                            eng.wait_ge(sems[w[1]], w[2])
                        else:
                            eng.wait_ge(esem[w[1]], w[2].count)
                    ins = o.fn(eng)
                    if o.dma is not None:
                        ins.then_inc(sems[o.dma[0]], 16)
                    elif o.inc:
                        ins.then_inc(esem[ename], 1)
                if ename == final_wait_eng:
                    for s, v in final:
                        eng.wait_ge(s, v)
            return body

        with nc.Block() as block:
            block.tensor(body_for("pe"))
            block.scalar(body_for("act"))
            block.vector(body_for("dve"))
            block.gpsimd(body_for("pool"))
            block.sync(body_for("sp"))


def build_nc():
    nc = bass.Bass("TRN2", target_bir_lowering=False)
    dt = nc.dram_tensor

    def din(name, shape):
        return dt(name, list(shape), F32, kind="ExternalInput").ap()

    def dout(name, shape):
        return dt(name, list(shape), F32, kind="ExternalOutput").ap()

    xp = din("xp", [SEQ, D])
    xs = din("xs", [NS, D])
    sca = din("sca", [NS * 30, DA])
    scb = din("scb", [NS * 2, DA])
    wgu = [din("wgu1", [NFC, 128, 2 * KC * 128]), din("wgu2", [NFC, 128, 2 * KC * 128])]
    wdd = [din("wd1", [FF, D]), din("wd2", [FF, D])]
    win = din("win", [NIC, 128, KC * 128])
    wod = din("wo", [D, D])
    lnbc = din("lnbc", [6, 128, D])
    cstd = din("cst", [128, NCST])
    bod = din("bo", [1, D])
    identd = din("ident", [128, 128])
    estkd = din("estk", [128, 4, 32])
    uscr_t = nc.dram_tensor("uscr", [2, 128, 1056], BF16).ap()
    uscr = uscr_t
    yp = dout("yp", [SEQ, D])
    ys = dout("ys", [NS, D])
    nap = dout("nap", [30, DA])
    nbp = dout("nbp", [2, DA])
    nas = dout("nas", [NS * 30, DA])
    nbs = dout("nbs", [NS * 2, DA])

    stack = contextlib.ExitStack()
    with stack:
        def sb(name, shape, dtype):
            return stack.enter_context(nc.sbuf_tensor("sb_" + name, list(shape), dtype))

        P = Prog(nc, stack)
        ident = sb("ident", [128, 128], F32)
        cst = sb("cst", [128, NCST], F32)
        lnb = sb("lnb", [128, 2, D], F32)
        dgb = sb("dgb", [128, 2, 8 * 128], BF16)
        bohl = dgb[0:1, :, :]
        ones = sb("ones", [1, 128], BF16)
        ones2 = sb("ones2", [2, 128], BF16)
        bo2 = sb("bo2", [2, D], BF16)
        xT = sb("xT", [128, KC, NTOKB], BF16)
        ring = sb("ring", [128, NSLOT, 2 * KC * 128], BF16)
        xres = sb("xres", [128, 9, D], F32)
        zt = sb("zt", [128, 2, D], F32)
        xn = sb("xn", [128, 2, D], F32)
        sg = sb("sg", [128, 2, 512], F32)
        sgS = sb("sgS", [128, NS], F32)
        NXB = 4
        xb = zt[:].bitcast(BF16).rearrange("p a (s d) -> p (a s) d", s=2)
        ident_bf = sb("ident_bf", [128, 128], BF16)
        sast = sb("sast", [128, 4, 512], F32)
        sbst = sb("sbst", [32, 512], F32)
        stt = sb("stt", [128, 2, 2, 6], F32)
        mv = sb("mv", [128, 2, 8], F32)
        mv_ln = mv
        mvc = sb("mvc", [128, 2, 8], F32)
        sttc = sb("sttc", [128, 2, 6], F32)
        neghalf = sb("neghalf", [128, 2], F32)
        hbg = sb("hbg", [128, 4], F32)
        halo = sb("halo", [128, 4, 32], F32)
        dummy = sb("dummy", [128, 8], F32)
        E4 = sb("E4", [128, 4, 32], F32)
        R_HT = NFC * NTOKB
        R_WD = NFC * D
        RB = sb("R", [128, R_HT + R_WD], BF16)
        hT = RB[:, 0:R_HT].rearrange("p (f t) -> p f t", f=NFC)
        wd = RB[:, R_HT:R_HT + R_WD].rearrange("p (f d) -> p f d", f=NFC)
        RF = RB[:].bitcast(F32)
        off = 0

        def rf(n):
            nonlocal off
            a = RF[:, off:off + n]
            off += n
            return a
        UV = rf(4 * 1056).rearrange("p (c t) -> p c t", c=4)
        R2 = RB[:, 0:2 * 4 * R2W].rearrange("p (s g t) -> p s g t", s=2, g=4)
        acc = rf(4 * 1024).rearrange("p (c t) -> p c t", c=4)
        CA = rf(4 * 480).rearrange("p (c t) -> p c t", c=4)
        CB = rf(4 * 32).rearrange("p (c t) -> p c t", c=4)
        prod = rf(480)
        xnc = rf(2 * 512).rearrange("p (s t) -> p s t", s=2)
        USb = rf(4 * 16).rearrange("p (c t) -> p c t", c=4)
        VSb = rf(4 * 16).rearrange("p (c t) -> p c t", c=4)
        accS = rf(4 * 16).rearrange("p (c t) -> p c t", c=4)
        redS = rf(4 * 16).rearrange("p (c t) -> p c t", c=4)
        off_bf = off * 2
        catT = RB[:, off_bf:off_bf + KC * NTOKB].rearrange("p (k t) -> p k t", k=KC)
        off_bf += KC * NTOKB
        wo = RB[:, off_bf:off_bf + KC * D].rearrange("p (k d) -> p k d", k=KC)
        off_bf += KC * D
        U_bf = RB[:, off_bf:off_bf + 4 * 1056].rearrange("p (c t) -> p c t", c=4)
        off_bf += 4 * 1056
        assert off_bf <= R_HT + R_WD, (off_bf, R_HT + R_WD)

        ps = stack.enter_context(nc.psum_tensor("ps", [128, 8 * 512], F32))
        psb = ps[:].bitcast(BF16)

        def pst(b, n=1):
            return tuple(("ps", b + i) for i in range(n))

        P.op("sp", lambda e: e.dma_start(out=ident[:], in_=identd), writes=["ident"], dma="c_ident")
        P.op("sp", lambda e: e.dma_start(out=cst[:], in_=cstd), writes=["cst"], dma="c_cst")
        P.op("sp", lambda e: e.dma_start(out=E4[:], in_=estkd), writes=["E4"], dma="c_e4")
        P.op("sp", lambda e: e.dma_start(out=xn[0:1, 1, :], in_=bod), writes=[("xn", 1)], dma="c_bo")
        P.op("dve", lambda e: e.memset(ones[:], 1.0), writes=["ones"])
        P.op("dve", lambda e: e.tensor_copy(out=ident_bf[:], in_=ident[:]), reads=["ident"], writes=["ident_bf"])
        P.op("dve", lambda e: e.tensor_copy(out=bohl[0:1, 0, :], in_=xn[0:1, 1, :]),
             reads=[("xn", 1)], writes=["bohi"])
        P.op("dve", lambda e: e.tensor_copy(out=xn[0:1, 0, :], in_=bohl[0:1, 0, :]),
             reads=["bohi"], writes=[("xn", 0)])
        P.op("dve", lambda e: e.tensor_tensor(out=bohl[0:1, 1, :], in0=xn[0:1, 1, :], in1=xn[0:1, 0, :],
                                              op=ALU.subtract),
             reads=[("xn", 1), ("xn", 0)], writes=["bolo"])
        P.op("dve", lambda e: e.memset(ones2[:], 1.0), writes=["ones2"])
        P.op("act", lambda e: e.activation(out=dummy[0:2, 4:5], in_=ones2[0:2, 0:1], func=AF.Silu), reads=["ones2"],
             writes=["dummy_act"])
        P.op("dve", lambda e: e.memset(halo[:], 0.0), writes=["halo"] + [("haloA", c) for c in range(4)])
        P.op("dve", lambda e: e.memset(neghalf[:], -0.5), writes=["neghalf"])
        P.op("dve", lambda e: e.tensor_scalar(out=hbg[:, :], in0=cst[:, C_BIN + 4:C_BIN + 8], scalar1=0.5, scalar2=None,
                                              op0=ALU.mult),
             reads=["cst"], writes=["hbg"])

        rmode_tok = "RMODE"
        state = {"ring_i": 0, "psr": 0, "lnslot": 0, "dg_i": 0, "upar": 0, "xb_i": 0}

        def ring_slot():
            s = state["ring_i"] % NSLOT
            state["ring_i"] += 1
            return s

        preloaded = {}

        def fetch_win(b, fo, prefetch=False):
            key = ("win", b, fo)
            if key in preloaded and not prefetch:
                return preloaded.pop(key)
            s_ = ring_slot()
            P.op("pool", lambda e: e.dma_start(out=ring[:, s_, 0:KC * 128], in_=win[fo]),
                 writes=[("ring", s_)], dma="ring%d" % s_)
            if prefetch:
                preloaded[key] = s_
            return s_

        def fetch_wgu(b, which, f, prefetch=False):
            key = ("wgu", b, which, f)
            if key in preloaded and not prefetch:
                return preloaded.pop(key)
            s_ = ring_slot()
            extra = [("xres", 3)] if (b == 0 and which == 0 and 1 <= f < NSLOT) else []
            P.op("pool", lambda e: e.dma_start(out=ring[:, s_, :], in_=wgu[which][f], max_dma_last_dim=8192),
                 reads=extra, writes=[("ring", s_)], dma="ring%d" % s_)
            if prefetch:
                preloaded[key] = s_
            return s_

        def tiles_of(b):
            t = [(i, i * 128, 128) for i in range(8)]
            if b == 1:
                t.append((8, BLK, NS))
            return t

        def tgroups_of(b):
            g = [(0, 512), (512, 512)]
            if b == 1:
                g.append((BLK, NS))
            return g

        def load_ln(idx, b):
            P.op("sp", lambda e: e.dma_start(out=lnb[:, 0, :], in_=lnbc[2 * idx]),
                 writes=[("lnb", 0)], dma="c_lnb0")
            P.op("sp", lambda e: e.dma_start(out=lnb[:, 1, :], in_=lnbc[2 * idx + 1]),
                 writes=[("lnb", 1)], dma="c_lnb1")

        def rmode_switch():
            P.op("dve", lambda e: e.memset(dummy[:, 0:1], 0.0), reads=[], writes=[rmode_tok])

        def rstd_chain(n, sl, eps, mv=None, tg=""):
            if mv is None:
                mv = mv_ln
            P.op("dve", lambda e: e.tensor_scalar(out=mv[:n, sl, 4:5], in0=mv[:n, sl, 1:2], scalar1=eps, scalar2=None,
                                                  op0=ALU.add),
                 reads=[("mv" + tg, sl)], writes=[("mv1" + tg, sl)])
            P.op("pool", lambda e: e.tensor_tensor(out=mv[:n, sl, 2:3], in0=mv[:n, sl, 4:5], in1=neghalf[:n, 0:1], op=ALU.pow),
                 reads=[("mv1" + tg, sl), "neghalf"], writes=[("mv2" + tg, sl)])
            P.op("dve", lambda e: e.scalar_tensor_tensor(out=mv[:n, sl, 3:4], in0=mv[:n, sl, 0:1], scalar=-1.0,
                                                          in1=mv[:n, sl, 2:3], op0=ALU.mult, op1=ALU.mult),
                 reads=[("mv" + tg, sl), ("mv2" + tg, sl)], writes=[("mv3" + tg, sl)])

        def cast_x(tile):
            (t, tok0, n) = tile
            sl = state["xb_i"] % NXB
            state["xb_i"] += 1
            P.op("act", lambda e: e.activation(out=xb[:n, sl, :], in_=xres[:n, t, :], func=AF.Identity),
                 reads=[("xres", t)], writes=[("xb", sl)])
            return sl

        def transposes_to_xT(b, t, tok0, n, bank=None, xslot=None):
            if bank is None:
                bank = state["psr"] % 4
                state["psr"] += 1
            if xslot is None:
                sl = state["xb_i"] % NXB
                state["xb_i"] += 1
                P.op("act", lambda e: e.activation(out=xb[:n, sl, :], in_=xres[:n, t, :], func=AF.Identity),
                     reads=[("xres", t)], writes=[("xb", sl)])
            else:
                sl = xslot

            def pe(e):
                ins = None
                for k in range(KC):
                    ins = e.transpose(out=psb[:, bank * 1024 + k * n: bank * 1024 + (k + 1) * n],
                                      in_=xb[:n, sl, k * 128:(k + 1) * 128],
                                      identity=ident_bf[:n, :n])
                return ins
            P.op("pe", pe, reads=[("xb", sl), "ident_bf"], writes=pst(bank))

            def ev(e):
                return e.activation(out=xT[:, :, tok0:tok0 + n],
                                    in_=psb[:, bank * 1024: bank * 1024 + KC * n].rearrange("p (k n) -> p k n", k=KC),
                                    func=AF.Identity)
            P.op("act", ev, reads=pst(bank), writes=[("xT", tok0)])

        def ln_head(t, n, src_ps_banks, in_scale, eps, after_z=None):
            sl = state["lnslot"] % 2
            state["lnslot"] += 1
            pb = src_ps_banks
            P.op("dve", lambda e: e.scalar_tensor_tensor(out=xn[:n, sl, :], in0=xres[:n, t, :], scalar=in_scale,
                                                          in1=ps[:n, pb * 512:(pb + 2) * 512],
                                                          op0=ALU.mult, op1=ALU.add),
                 reads=[("xres", t)] + list(pst(pb, 2)), writes=[("xn", sl)])
            if after_z is not None:
                after_z()
            for h in range(2):
                P.op("dve", lambda e, h=h: e.bn_stats(out=stt[:n, sl, h, :], in_=xn[:n, sl, h * 512:(h + 1) * 512]),
                     reads=[("xn", sl)], writes=[("stt", sl, h)])
            P.op("dve", lambda e: e.bn_aggr(out=mv[:n, sl, 0:2], in_=stt[:n, sl, :, :].rearrange("p a b -> p (a b)")),
                 reads=[("stt", sl, 0), ("stt", sl, 1)], writes=[("mv", sl)])
            rstd_chain(n, sl, eps)
            return (t, n, sl)

        def ln_tail(ctx, final_store=None, make_bf=True):
            (t, n, sl) = ctx
            P.op("act", lambda e: e.activation(out=xn[:n, sl, :], in_=xn[:n, sl, :], func=AF.Identity,
                                               scale=mv[:n, sl, 2:3], bias=mv[:n, sl, 3:4]),
                 reads=[("xn", sl), ("mv2", sl), ("mv3", sl)], writes=[("xn", sl)])
            P.op("dve", lambda e: e.tensor_tensor(out=xn[:n, sl, :], in0=xn[:n, sl, :], in1=lnb[:n, 0, :], op=ALU.mult),
                 reads=[("xn", sl), ("lnb", 0)], writes=[("xn", sl)])
            if final_store is None:
                xs_ = None
                if make_bf:
                    xs_ = state["xb_i"] % NXB
                    state["xb_i"] += 1
                    P.op("dve", lambda e: e.tensor_tensor(out=xb[:n, xs_, :], in0=xn[:n, sl, :], in1=lnb[:n, 1, :], op=ALU.add),
                         reads=[("xn", sl), ("lnb", 1)], writes=[("xb", xs_)])
                P.op("dve", lambda e: e.tensor_tensor(out=xres[:n, t, :], in0=xn[:n, sl, :], in1=lnb[:n, 1, :], op=ALU.add),
                     reads=[("xn", sl), ("lnb", 1)], writes=[("xres", t)])
                return xs_
            dst = final_store
            P.op("dve", lambda e: e.tensor_tensor(out=xn[:n, sl, :], in0=xn[:n, sl, :], in1=lnb[:n, 1, :], op=ALU.add),
                 reads=[("xn", sl), ("lnb", 1)], writes=[("xn", sl)])
            P.op("sp", lambda e: e.dma_start(out=dst, in_=xn[:n, sl, :]), reads=[("xn", sl)],
                 dma="st_xn%d" % sl)
            return None

        def layer_norm_tile(t, n, src_ps_banks, in_scale, eps, b, final_store=None, after_z=None):
            return ln_tail(ln_head(t, n, src_ps_banks, in_scale, eps, after_z=after_z), final_store=final_store)

        def ffn(b, which, in_scale, eps, final, defer_last=False, pre=None):
            tiles = tiles_of(b)
            tgs = tgroups_of(b)
            wd_chunks = 11
            slots = {}

            def load_w(f):
                slots[f] = fetch_wgu(b, which, f)
                if 4 <= f < 4 + wd_chunks:
                    i = f - 4
                    P.op("pool", lambda e: e.dma_start(
                        out=wd[:, 2 * i:2 * i + 2, :],
                        in_=wdd[which][2 * i * 128:(2 * i + 2) * 128, :].rearrange("(f p) d -> p f d", p=128)),
                        reads=[rmode_tok], writes=[("wd", i)], dma="wd%d" % i)

            def u_group(f, tok0, n):
                s_ = slots[f]
                if n == 512:
                    par = state["upar"] % 2
                    state["upar"] += 1
                    bg, bu = par, 2 + par
                    sgt = sg[:, par, :]
                    sgtok = ("sg", par)
                else:
                    bg, bu = 4, 5
                    sgt = sgS[:, :]
                    sgtok = "sgS"

                def pe(e):
                    ins = None
                    for gu, bk in ((0, bg), (1, bu)):
                        for k in range(KC):
                            c0 = (gu * KC + k) * 128
                            ins = e.matmul(ps[:, bk * 512: bk * 512 + n], lhsT=ring[:, s_, c0:c0 + 128],
                                           rhs=xT[:, k, tok0:tok0 + n], start=(k == 0), stop=(k == KC - 1))
                    return ins
                xtoks = [("xT", t0) for (_, t0, nn) in tiles if t0 >= tok0 and t0 < tok0 + max(n, 1)]
                P.op("pe", pe, reads=[("ring", s_)] + xtoks, writes=[("ps", bg), ("ps", bu)])
                P.op("act", lambda e: e.activation(out=sgt[:, :n], in_=ps[:, bg * 512: bg * 512 + n], func=AF.Silu),
                     reads=[("ps", bg)], writes=[sgtok])
                P.op("dve", lambda e: e.tensor_tensor(
                    out=hT[:, f, tok0:tok0 + n], in0=sgt[:, :n], in1=ps[:, bu * 512: bu * 512 + n], op=ALU.mult),
                    reads=[sgtok, ("ps", bu), rmode_tok], writes=[("hT", f, tok0)])

            pre = list(pre) if pre else []
            npre = min(len(pre), NSLOT)
            for f in range(npre):
                load_w(f)
            for f in range(npre):
                u_group(f, *tgs[0])
                pre.pop(0)()
            for th in pre:
                th()
            for f in range(NFC):
                if f >= npre:
                    load_w(f)
                for gi, (tok0, n) in enumerate(tgs):
                    if f < npre and gi == 0:
                        continue
                    u_group(f, tok0, n)
            if which == 0:
                for fo in (12, 16, 13, 17):
                    fetch_win(b, fo, prefetch=True)
            elif b == 0:
                for f in range(NSLOT):
                    fetch_wgu(1, 0, f, prefetch=True)
            wd_toks = [("wd", i) for i in range(wd_chunks)]
            pending = None
            for (t, tok0, n) in tiles:
                yb = 4 + 2 * (t % 2)
                g0 = 0 if tok0 < 512 else (512 if tok0 < BLK else BLK)

                def pe(e, tok0=tok0, n=n, yb=yb):
                    ins = None
                    for f in range(NFC):
                        for h in range(2):
                            ins = e.matmul(ps[:n, (yb + h) * 512:(yb + h + 1) * 512], lhsT=hT[:, f, tok0:tok0 + n],
                                           rhs=wd[:, f, h * 512:(h + 1) * 512], start=(f == 0), stop=(f == NFC - 1))
                    return ins
                P.op("pe", pe, reads=[("hT", f, g0) for f in range(NFC)] + wd_toks + [rmode_tok], writes=pst(yb, 2))
                if pending is not None:
                    transposes_to_xT(b, *pending)
                    pending = None
                if final:
                    hook = None
                    if b == 0:
                        if t >= 1:
                            xpose_x_tile(1, tiles_of(1)[t - 1])
                        if t == 2:
                            xpose_x_tile(1, tiles_of(1)[8])
                        hook = (lambda t=t: load_x(1, tiles_of(1)[t]))
                    dst = yp[b * BLK + tok0: b * BLK + tok0 + n, :] if t < 8 else ys[:, :]
                    layer_norm_tile(t, n, yb, in_scale, eps, b, final_store=dst, after_z=hook)
                else:
                    xs_ = layer_norm_tile(t, n, yb, in_scale, eps, b)
                    pending = (t, tok0, n, None, xs_)
            if pending is not None and not defer_last:
                transposes_to_xT(b, *pending)
                pending = None
            return pending

        def load_x(b, tile):
            (t, tok0, n) = tile
            src = xp[b * BLK + tok0: b * BLK + tok0 + n, :] if t < 8 else xs[:, :]
            P.op("sp", lambda e: e.dma_start(out=xres[:n, t, :], in_=src), writes=[("xres", t)], dma="ld_x%d" % t)

        def xpose_x_tile(b, tile):
            (t, tok0, n) = tile
            transposes_to_xT(b, t, tok0, n)

        def load_x_tile(b, tile):
            load_x(b, tile)
            xpose_x_tile(b, tile)

        def early_sample_state():
            sca3 = sca.rearrange("(s j) c -> s j c", j=30)
            nas3 = nas.rearrange("(s j) c -> s j c", j=30)
            scb3 = scb.rearrange("(s j) c -> s j c", j=2)
            nbs3 = nbs.rearrange("(s j) c -> s j c", j=2)
            for r in range(4):
                rows = 128 if r < 3 else 96
                P.op("sp", lambda e, r=r, rows=rows: e.dma_start(out=sast[:rows, r, :], in_=sca[r * 128: r * 128 + rows, :]),
                     writes=["sast"] if r == 3 else [("sast", r)], dma="sa")
            P.op("sp", lambda e: e.dma_start(out=sbst[:, :], in_=scb), writes=["sbst"], dma="sbld")
            P.op("sp", lambda e: e.dma_start(out=nas3[:, 0:29, :], in_=sca3[:, 1:30, :]), dma="st_nas0")
            P.op("sp", lambda e: e.dma_start(out=nbs3[:, 0:1, :], in_=scb3[:, 1:2, :]), dma="st_nbs0")

        def mixer(b, pendT=None):
            tiles = tiles_of(b)
            tgs = tgroups_of(b)
            has_s = (b == 1)
            rmode_switch()
            MR = [rmode_tok]
            sca3 = sca.rearrange("(s j) c -> s j c", j=30)
            nas3 = nas.rearrange("(s j) c -> s j c", j=30)
            scb3 = scb.rearrange("(s j) c -> s j c", j=2)
            nbs3 = nbs.rearrange("(s j) c -> s j c", j=2)
            if has_s:
                for ch in range(4):
                    pb = ch % 2

                    def pe(e, ch=ch, pb=pb):
                        ins = None
                        for r in range(4):
                            rows = 128 if r < 3 else 96
                            ins = e.transpose(out=ps[:, pb * 512 + r * 128: pb * 512 + r * 128 + rows],
                                              in_=sast[:rows, r, ch * 128:(ch + 1) * 128],
                                              identity=ident[:rows, :rows])
                        return ins
                    P.op("pe", pe, reads=["sast", "ident"], writes=pst(pb))
                    P.op("act", lambda e, ch=ch, pb=pb: e.activation(out=CA[:, ch, :], in_=ps[:, pb * 512: pb * 512 + 480],
                                                                     func=AF.Identity),
                         reads=list(pst(pb)) + MR, writes=[("CA", ch)])
                pb = 2

                def pe2(e, pb=pb):
                    ins = None
                    for ch in range(4):
                        ins = e.transpose(out=ps[:, pb * 512 + ch * 32: pb * 512 + (ch + 1) * 32],
                                          in_=sbst[:32, ch * 128:(ch + 1) * 128], identity=ident[:32, :32])
                    return ins
                P.op("pe", pe2, reads=["sbst", "ident"], writes=pst(pb))
                P.op("act", lambda e, pb=pb: e.activation(out=CB[:, :, :], in_=ps[:, pb * 512: pb * 512 + 128].rearrange("p (c t) -> p c t", c=4),
                                                   func=AF.Identity),
                     reads=list(pst(pb)) + MR, writes=["CB"])

            def win_chunk(fo, consumer):
                s = fetch_win(b, fo)
                for gi, (tok0, n) in enumerate(tgs):
                    bk = state["psr"] % 4
                    state["psr"] += 1

                    def pe(e, tok0=tok0, n=n, bk=bk):
                        ins = None
                        for k in range(KC):
                            ins = e.matmul(ps[:, bk * 512: bk * 512 + n], lhsT=ring[:, s, k * 128:(k + 1) * 128],
                                           rhs=xT[:, k, tok0:tok0 + n], start=(k == 0), stop=(k == KC - 1))
                        return ins
                    xtoks = [("xT", t0) for (_, t0, nn) in tiles if t0 >= tok0 and t0 < tok0 + n]
                    P.op("pe", pe, reads=[("ring", s)] + xtoks, writes=[("ps", bk)])
                    consumer(gi, tok0, n, bk)

            def cc(col):
                return cst[:, col:col + 1]

            P.op("dve", lambda e: e.tensor_copy(out=UV[:, :, 0:2], in_=halo[:, :, 30:32]), reads=["halo"] + MR,
                 writes=[("UVh", c) for c in range(4)])
            for ch in range(4):
                s_gc = fetch_win(b, 12 + ch)
                s_bv = fetch_win(b, 16 + ch)
                for gi, (tok0, n) in enumerate(tgs):
                    b1 = state["psr"] % 4
                    b2 = (state["psr"] + 1) % 4
                    state["psr"] += 2
                    xtoks = [("xT", t0) for (_, t0, nn) in tiles if t0 >= tok0 and t0 < tok0 + n]
                    for (slot_w, bk) in ((s_gc, b1), (s_bv, b2)):
                        def pe(e, tok0=tok0, n=n, bk=bk, slot_w=slot_w):
                            ins = None
                            for k in range(KC):
                                ins = e.matmul(ps[:, bk * 512: bk * 512 + n], lhsT=ring[:, slot_w, k * 128:(k + 1) * 128],
                                               rhs=xT[:, k, tok0:tok0 + n], start=(k == 0), stop=(k == KC - 1))
                            return ins
                        P.op("pe", pe, reads=[("ring", slot_w)] + xtoks, writes=[("ps", bk)])
                    sl = gi % 2
                    P.op("act", lambda e, n=n, b1=b1, sl=sl, ch=ch: e.activation(
                        out=sg[:, sl, :n], in_=ps[:, b1 * 512: b1 * 512 + n], func=AF.Identity, bias=cc(C_BIN + 12 + ch)),
                        reads=[("ps", b1), "cst"], writes=[("sg", sl)])
                    if tok0 < BLK:
                        dst = UV[:, ch, 2 + tok0: 2 + tok0 + n]
                        wtok = ("UV", ch, tok0)
                    else:
                        dst = VSb[:, ch, :n]
                        wtok = ("VS", ch)
                    P.op("dve", lambda e, n=n, b2=b2, sl=sl, ch=ch, dst=dst: e.scalar_tensor_tensor(
                        out=dst, in0=ps[:, b2 * 512: b2 * 512 + n], scalar=cc(C_BIN + 16 + ch), in1=sg[:, sl, :n],
                        op0=ALU.add, op1=ALU.mult),
                        reads=[("ps", b2), ("sg", sl), "cst"] + MR, writes=[wtok])
                    if ch == 0 and gi == 0 and pendT is not None:
                        transposes_to_xT(b, *pendT)
                wb = C_WB + ch * KB
                uvr = [("UV", ch, 0), ("UV", ch, 512), ("UVh", ch)]
                P.op("dve", lambda e, ch=ch, wb=wb: e.tensor_scalar(out=acc[:, ch, :], in0=UV[:, ch, 0:BLK], scalar1=cc(wb),
                                                                   scalar2=None, op0=ALU.mult),
                     reads=uvr + ["cst"] + MR, writes=[("acc", ch)])
                for j in (1, 2):
                    P.op("dve", lambda e, ch=ch, wb=wb, j=j: e.scalar_tensor_tensor(
                        out=acc[:, ch, :], in0=UV[:, ch, j:j + BLK], scalar=cc(wb + j), in1=acc[:, ch, :],
                        op0=ALU.mult, op1=ALU.add),
                        reads=uvr + [("acc", ch), "cst"], writes=[("acc", ch)])
                if has_s:
                    cb3 = CB[:, ch, :].rearrange("p (s j) -> p s j", j=2)
                    P.op("dve", lambda e, ch=ch, wb=wb, cb3=cb3: e.tensor_scalar(
                        out=accS[:, ch, :], in0=cb3[:, :, 0], scalar1=cc(wb), scalar2=None, op0=ALU.mult),
                        reads=["CB", "cst"] + MR, writes=[("accS", ch)])
                    P.op("dve", lambda e, ch=ch, wb=wb, cb3=cb3: e.scalar_tensor_tensor(
                        out=accS[:, ch, :], in0=cb3[:, :, 1], scalar=cc(wb + 1), in1=accS[:, ch, :],
                        op0=ALU.mult, op1=ALU.add),
                        reads=["CB", "cst", ("accS", ch)], writes=[("accS", ch)])
                    P.op("dve", lambda e, ch=ch, wb=wb: e.scalar_tensor_tensor(
                        out=accS[:, ch, :], in0=VSb[:, ch, :], scalar=cc(wb + 2), in1=accS[:, ch, :],
                        op0=ALU.mult, op1=ALU.add),
                        reads=[("VS", ch), "cst", ("accS", ch)], writes=[("accS", ch)])

                def cons_gb(gi, tok0, n, bk, ch=ch):
                    if tok0 < BLK:
                        src = acc[:, ch, tok0:tok0 + n]
                        rt = ("acc", ch)
                    else:
                        src = accS[:, ch, :n]
                        rt = ("accS", ch)
                    P.op("dve", lambda e: e.scalar_tensor_tensor(
                        out=catT[:, 4 + ch, tok0:tok0 + n], in0=ps[:, bk * 512: bk * 512 + n], scalar=cc(C_BIN + 8 + ch),
                        in1=src, op0=ALU.add, op1=ALU.mult),
                        reads=[("ps", bk), rt, "cst"] + MR, writes=[("catT", 4 + ch, tok0)])
                win_chunk(8 + ch, cons_gb)
            P.op("dve", lambda e: e.tensor_copy(out=halo[:, :, 30:32], in_=UV[:, :, BLK:BLK + 2]),
                 reads=[("UV", c, 512) for c in range(4)], writes=["halo"])
            if b == 1:
                pb = 3

                def pev(e, pb=pb):
                    ins = None
                    for ch in range(4):
                        ins = e.transpose(out=ps[:2, pb * 512 + ch * 128: pb * 512 + (ch + 1) * 128],
                                          in_=UV[:, ch, BLK:BLK + 2], identity=ident[:, :])
                    return ins
                P.op("pe", pev, reads=[("UV", c, 512) for c in range(4)] + ["ident"], writes=pst(pb))
                P.op("act", lambda e, pb=pb: e.activation(out=xn[:2, 1, 0:512], in_=ps[:2, pb * 512:(pb + 1) * 512], func=AF.Identity),
                     reads=pst(pb), writes=[("xn", 1)])
                P.op("sp", lambda e: e.dma_start(out=nbp, in_=xn[:2, 1, 0:512]), reads=[("xn", 1)], dma="st_nbp")
                pb = 2

                def pevs(e, pb=pb):
                    ins = None
                    for ch in range(4):
                        ins = e.transpose(out=ps[:NS, pb * 512 + ch * 128: pb * 512 + (ch + 1) * 128],
                                          in_=VSb[:, ch, :], identity=ident[:, :])
                    return ins
                P.op("pe", pevs, reads=[("VS", c) for c in range(4)] + ["ident"], writes=pst(pb))
                P.op("act", lambda e, pb=pb: e.activation(out=xn[:NS, 0, 512:1024], in_=ps[:NS, pb * 512:(pb + 1) * 512], func=AF.Identity),
                     reads=pst(pb), writes=[("xn", 0)])
                P.op("sp", lambda e: e.dma_start(out=nbs3[:, 1, :], in_=xn[:NS, 0, 512:1024]), reads=[("xn", 0)],
                     dma="st_nbs1")

            P.op("pool", lambda e: e.dma_start(out=wo[:, :, :], in_=wod.rearrange("(k p) d -> p k d", p=128)),
                 reads=MR, writes=["wo"], dma="wo")
            P.op("dve", lambda e: e.memset(dummy[:, 7:8], 0.0),
                 writes=["R2g"] + [("UV", c, g_) for c in range(4) for g_ in (0, 512)] + [("UVh", c) for c in range(4)])
            P.op("dve", lambda e: e.memset(U_bf[:, :, 1054:1056], 0.0), reads=MR, writes=["Ubfz"])
            P.op("dve", lambda e: e.tensor_copy(out=U_bf[:, :, 0:30], in_=halo[:, :, 0:30]),
                 reads=[("haloA", c) for c in range(4)] + MR, writes=[("Ubfh", c) for c in range(4)])

            def gv_matmuls(ch):
                s_g = fetch_win(b, 4 + ch)
                s_v = fetch_win(b, ch)
                for gi, (tok0, n) in enumerate(tgs):
                    b1 = state["psr"] % 4
                    b2 = (state["psr"] + 1) % 4
                    state["psr"] += 2
                    xtoks = [("xT", t0) for (_, t0, nn) in tiles if t0 >= tok0 and t0 < tok0 + n]
                    for (slot_w, bk) in ((s_g, b1), (s_v, b2)):
                        def pe(e, tok0=tok0, n=n, bk=bk, slot_w=slot_w):
                            ins = None
                            for k in range(KC):
                                ins = e.matmul(ps[:, bk * 512: bk * 512 + n], lhsT=ring[:, slot_w, k * 128:(k + 1) * 128],
                                               rhs=xT[:, k, tok0:tok0 + n], start=(k == 0), stop=(k == KC - 1))
                            return ins
                        P.op("pe", pe, reads=[("ring", slot_w)] + xtoks, writes=[("ps", bk)])
                    sl = gi % 2
                    P.op("act", lambda e, n=n, b1=b1, sl=sl, ch=ch: e.activation(
                        out=sg[:, sl, :n], in_=ps[:, b1 * 512: b1 * 512 + n], func=AF.Tanh, scale=0.5,
                        bias=hbg[:, ch:ch + 1]),
                        reads=[("ps", b1), "hbg"], writes=[("sg", sl)])
                    P.op("dve", lambda e, n=n, sl=sl: e.tensor_scalar(out=sg[:, sl, :n], in0=sg[:, sl, :n], scalar1=0.5,
                                                                     scalar2=0.5, op0=ALU.mult, op1=ALU.add),
                         reads=[("sg", sl)], writes=[("sg", sl)])
                    if tok0 < BLK:
                        dst = U_bf[:, ch, 30 + tok0: 30 + tok0 + n]
                        wtok = ("Ubf", ch, tok0)
                    else:
                        dst = USb[:, ch, :n]
                        wtok = ("US", ch)
                    P.op("dve", lambda e, n=n, b2=b2, sl=sl, ch=ch, dst=dst: e.scalar_tensor_tensor(
                        out=dst, in0=ps[:, b2 * 512: b2 * 512 + n], scalar=cc(C_BIN + ch), in1=sg[:, sl, :n],
                        op0=ALU.add, op1=ALU.mult),
                        reads=[("ps", b2), ("sg", sl), "cst"] + MR, writes=[wtok])
                    if tok0 == 512:
                        P.op("dve", lambda e, b2=b2, sl=sl, ch=ch: e.scalar_tensor_tensor(
                            out=halo[:, ch, 0:30], in0=ps[:, b2 * 512 + 482: b2 * 512 + 512], scalar=cc(C_BIN + ch),
                            in1=sg[:, sl, 482:512], op0=ALU.add, op1=ALU.mult),
                            reads=[("ps", b2), ("sg", sl), "cst"], writes=[("haloA", ch)])
            def replicas(ch):
                rs = ch % 2
                P.op("sp", lambda e: e.dma_start(out=uscr[rs], in_=U_bf[:, ch, :]),
                     reads=[("Ubf", ch, 0), ("Ubf", ch, 512), ("Ubfh", ch), "Ubfz"]
                     + ([("R2", ch - 2)] if ch >= 2 else []),
                     writes=[("uscr", rs)], dma="us_%d" % rs)
                src = uscr_t[rs].rearrange("(g c) t -> c g t", g=4)
                for s_ in range(4):
                    P.op("sp", lambda e, s_=s_: e.dma_start(out=R2[32 * s_:32 * s_ + 32, rs, :, :],
                                                             in_=src[:, :, s_:s_ + R2W]),
                         reads=[("uscr", rs), "R2g"] + ([("R2done", ch - 2)] if ch >= 2 else []),
                         writes=[("R2", ch)] if s_ == 3 else [("R2", ch, s_)], dma="r2_%d" % rs)

            R2toks = {ch: [("R2", ch)] for ch in range(4)}

            def conv_matmuls(ch):
                wa = C_WA + ch * KA
                cb = (4, 5) if ch % 2 == 0 else (6, 7)
                rs = ch % 2
                ds_ = ch % 2
                wcol = C_WP + ch * 32
                P.op("dve", lambda e: e.tensor_tensor(
                    out=dgb[:, ds_, :].rearrange("p (q g c) -> p q g c", q=8, g=4),
                    in0=E4[:, :, :].unsqueeze(1).to_broadcast([128, 8, 4, 32]),
                    in1=cst[:, wcol:wcol + 32].rearrange("p (q g) -> p q g", q=8).unsqueeze(3).to_broadcast([128, 8, 4, 32]),
                    op=ALU.mult),
                    reads=["E4", "cst"], writes=[("dgb", ds_), "bohi", "bolo"])
                for q in range(8):
                    def pe(e, q=q):
                        ins = None
                        for gi in range(2):
                            for g_ in range(4):
                                ins = e.matmul(ps[32 * g_:32 * g_ + 32, cb[gi] * 512:(cb[gi] + 1) * 512],
                                               lhsT=dgb[:, ds_, q * 128 + 32 * g_: q * 128 + 32 * g_ + 32],
                                               rhs=R2[:, rs, g_, gi * 512 + 4 * q: gi * 512 + 4 * q + 512],
                                               start=(q == 0), stop=(q == 7), tile_position=(0, 32 * g_))
                        return ins
                    P.op("pe", pe, reads=[("dgb", ds_)] + R2toks[ch] + MR, writes=[("ps", cb[0]), ("ps", cb[1])])
                for gi in range(2):
                    P.op("act", lambda e, gi=gi, ch=ch, cb=cb: e.activation(
                        out=acc[:, ch, gi * 512:(gi + 1) * 512], in_=ps[:, cb[gi] * 512:(cb[gi] + 1) * 512], func=AF.Identity,
                        bias=cc(C_BA + ch)),
                        reads=[("ps", cb[gi]), "cst"] + MR,
                        writes=[("acc", ch), ("R2done", ch)] if gi == 1 else [("acc", ch, "lo"), ("acc", ch)])
                if has_s:
                    ca3 = CA[:, ch, :].rearrange("p (s j) -> p s j", j=30)
                    w30 = cst[:, wa:wa + 30].unsqueeze(1).to_broadcast([128, NS, 30])
                    P.op("dve", lambda e, ca3=ca3, w30=w30: e.tensor_tensor(
                        out=prod.rearrange("p (s j) -> p s j", j=30), in0=ca3, in1=w30, op=ALU.mult),
                        reads=[("CA", ch), "cst"] + MR, writes=["prod"])
                    P.op("dve", lambda e, ch=ch: e.tensor_reduce(out=redS[:, ch, :], in_=prod.rearrange("p (s j) -> p s j", j=30),
                                                                 axis=AX.X, op=ALU.add),
                         reads=["prod"], writes=[("redS", ch)])
                    P.op("dve", lambda e, ch=ch, wa=wa: e.scalar_tensor_tensor(
                        out=accS[:, ch, :], in0=USb[:, ch, :], scalar=cc(wa + 30), in1=redS[:, ch, :],
                        op0=ALU.mult, op1=ALU.add),
                        reads=[("US", ch), ("redS", ch), "cst"], writes=[("accS", ch)])
                    P.op("dve", lambda e, ch=ch: e.tensor_scalar(out=accS[:, ch, :], in0=accS[:, ch, :], scalar1=cc(C_BA + ch),
                                                                 scalar2=None, op0=ALU.add),
                         reads=[("accS", ch), "cst"], writes=[("accS", ch)])

            gv_matmuls(0)
            replicas(0)
            gv_matmuls(1)
            replicas(1)
            gv_matmuls(2)
            conv_matmuls(0)
            replicas(2)
            conv_matmuls(1)
            gv_matmuls(3)
            replicas(3)
            conv_matmuls(2)
            conv_matmuls(3)
            if b == 1:
                pb = 3

                def peu(e, pb=pb):
                    ins = None
                    for ch in range(4):
                        ins = e.transpose(out=ps[:30, pb * 512 + ch * 128: pb * 512 + (ch + 1) * 128],
                                          in_=halo[:, ch, 0:30], identity=ident[:, :])
                    return ins
                P.op("pe", peu, reads=[("haloA", c) for c in range(4)] + ["ident"], writes=pst(pb))
                P.op("act", lambda e, pb=pb: e.activation(out=xn[:30, 1, 512:1024], in_=ps[:30, pb * 512:(pb + 1) * 512], func=AF.Identity),
                     reads=pst(pb), writes=[("xn", 1)])
                P.op("sp", lambda e: e.dma_start(out=nap, in_=xn[:30, 1, 512:1024]), reads=[("xn", 1)], dma="st_nap")
                pb = 2

                def peus(e, pb=pb):
                    ins = None
                    for ch in range(4):
                        ins = e.transpose(out=ps[:NS, pb * 512 + ch * 128: pb * 512 + (ch + 1) * 128],
                                          in_=USb[:, ch, :], identity=ident[:, :])
                    return ins
                P.op("pe", peus, reads=[("US", c) for c in range(4)] + ["ident"], writes=pst(pb))
                P.op("act", lambda e, pb=pb: e.activation(out=xn[:NS, 0, 0:512], in_=ps[:NS, pb * 512:(pb + 1) * 512], func=AF.Identity),
                     reads=pst(pb), writes=[("xn", 0)])
                P.op("sp", lambda e: e.dma_start(out=nas3[:, 29, :], in_=xn[:NS, 0, 0:512]), reads=[("xn", 0)],
                     dma="st_nas1")
            load_ln(1, b)

            def S1(tile):
                (t, tok0, n) = tile
                pb = t % 2
                sl = t % 2
                if t < 8:
                    srcs = [acc[:, ch, tok0:tok0 + n] for ch in range(4)]
                    rts = [("acc", ch) for ch in range(4)] + [("acc", ch, "lo") for ch in range(4)]
                else:
                    srcs = [accS[:, ch, :n] for ch in range(4)]
                    rts = [("accS", ch) for ch in range(4)]

                def pe(e):
                    ins = None
                    for ch in range(4):
                        ins = e.transpose(out=ps[:n, pb * 512 + ch * 128: pb * 512 + (ch + 1) * 128], in_=srcs[ch],
                                          identity=ident[:, :])
                    return ins
                P.op("pe", pe, reads=rts + ["ident"] + MR, writes=pst(pb))

            def S1post(tile):
                (t, tok0, n) = tile
                pb = t % 2
                sl = t % 2
                lsl = t % 2
                P.op("dve", lambda e: e.bn_stats(out=sttc[:n, lsl, :], in_=ps[:n, pb * 512:(pb + 1) * 512]),
                     reads=pst(pb), writes=[("sttc", lsl)])
                P.op("dve", lambda e: e.bn_aggr(out=mvc[:n, lsl, 0:2], in_=sttc[:n, lsl, :]),
                     reads=[("sttc", lsl)], writes=[("mvc", lsl)])
                rstd_chain(n, lsl, EPS, mv=mvc, tg="c")
                P.op("act", lambda e: e.activation(
                    out=xnc[:n, sl, :], in_=ps[:n, pb * 512:(pb + 1) * 512], func=AF.Identity,
                    scale=mvc[:n, lsl, 2:3], bias=mvc[:n, lsl, 3:4]),
                    reads=list(pst(pb)) + [("mv2c", lsl), ("mv3c", lsl)] + MR, writes=[("xnc", sl)])

            def S2(tile):
                (t, tok0, n) = tile
                sl = t % 2
                pb2 = 2

                def pe2(e):
                    ins = None
                    for ch in range(4):
                        ins = e.transpose(out=ps[:, pb2 * 512 + ch * n: pb2 * 512 + (ch + 1) * n],
                                          in_=xnc[:n, sl, ch * 128:(ch + 1) * 128], identity=ident[:n, :n])
                    return ins
                P.op("pe", pe2, reads=[("xnc", sl), "ident"], writes=pst(pb2))
                for ch in range(4):
                    P.op("act", lambda e, ch=ch: e.activation(
                        out=catT[:, ch, tok0:tok0 + n], in_=ps[:, pb2 * 512 + ch * n: pb2 * 512 + (ch + 1) * n],
                        func=AF.Silu, scale=cc(C_LG + ch), bias=cc(C_LB + ch)),
                        reads=list(pst(pb2)) + ["cst"] + MR, writes=[("catT", ch, tok0)])

            def S3(tile):
                (t, tok0, n) = tile
                yb = 4 + 2 * (t % 2)

                def pe(e):
                    ins = None
                    for k in range(KC):
                        for h in range(2):
                            ins = e.matmul(ps[:n, (yb + h) * 512:(yb + h + 1) * 512], lhsT=catT[:, k, tok0:tok0 + n],
                                           rhs=wo[:, k, h * 512:(h + 1) * 512], start=(k == 0), stop=False)
                    for h in range(2):
                        ins = e.matmul(ps[:n, (yb + h) * 512:(yb + h + 1) * 512], lhsT=ones2[0:2, :n],
                                       rhs=bo2[0:2, h * 512:(h + 1) * 512], start=False, stop=True)
                    return ins
                g0 = 0 if tok0 < 512 else (512 if tok0 < BLK else BLK)
                P.op("pe", pe, reads=[("catT", k, tok0) for k in range(4)] + [("catT", k, g0) for k in range(4, 8)]
                     + ["wo", "ones2", "bo2"] + MR,
                     writes=pst(yb, 2))

            lnctx = {}
            xslots = {}

            def S3a(tile):
                (t, tok0, n) = tile
                lnctx[t] = ln_head(t, n, 4 + 2 * (t % 2), ALPHA, EPS)

            def S3b(tile):
                (t, tok0, n) = tile
                ln_tail(lnctx[t], make_bf=False)

            def S3c(tile):
                xslots[tile[0]] = cast_x(tile)

            def S4(tile, bank=3):
                (t, tok0, n) = tile
                transposes_to_xT(b, t, tok0, n, bank=bank, xslot=xslots[t])

            nt = len(tiles)
            def step(i):
                if i < nt:
                    S1(tiles[i])
                if 0 <= i - 1 < nt:
                    S1post(tiles[i - 1])
                if 0 <= i - 2 < nt:
                    S2(tiles[i - 2])
                if 0 <= i - 3 < nt:
                    S3(tiles[i - 3])
                if 0 <= i - 4 < nt:
                    S3a(tiles[i - 4])
                if 0 <= i - 5 < nt:
                    S3b(tiles[i - 5])
                if 0 <= i - 6 < nt:
                    S3c(tiles[i - 6])
                if 0 <= i - 7 < nt:
                    if i < nt + 3:
                        S4(tiles[i - 7])
                    else:
                        S4(tiles[i - 7], bank=4 + 2 * ((nt - 2) % 2) + (i % 2))
            for i in range(nt + 3):
                if i == 2:
                    for f in range(NSLOT):
                        fetch_wgu(b, 1, f, prefetch=True)
                step(i)
            rmode_switch()
            return [(lambda i=i: step(i)) for i in range(nt + 3, nt + 7)]


        for b in range(2):
            if b == 0:
                t0s = tiles_of(0)
                for tile in t0s:
                    load_x(0, tile)
                cs = {}
                for j in range(4 + 2):
                    if j < 4:
                        cs[j] = cast_x(t0s[j])
                    if j >= 2:
                        (t, tok0, n) = t0s[j - 2]
                        transposes_to_xT(0, t, tok0, n, xslot=cs[j - 2])
                load_x(1, tiles_of(1)[8])
                P.op("sp", lambda e: e.dma_start(out=bo2[0:1, :], in_=bohl[0:1, 0, :]), reads=["bohi"], writes=[("bo2", 0)],
                     dma="c_bo2")
                P.op("sp", lambda e: e.dma_start(out=bo2[1:2, :], in_=bohl[0:1, 1, :]), reads=["bolo"], writes=["bo2"],
                     dma="c_bo2")
                early_sample_state()
                pre1 = [(lambda tile=tile: xpose_x_tile(0, tile)) for tile in t0s[4:]]
            else:
                pre1 = [lambda: xpose_x_tile(1, tiles_of(1)[7])]
            load_ln(0, b)
            pend = ffn(b, 0, 2.0 * ALPHA, 4.0 * EPS, final=False, defer_last=True, pre=pre1)
            tail = mixer(b, pend)
            tail.append(lambda b=b: load_ln(2, b))
            ffn(b, 1, 2.0 * ALPHA, 4.0 * EPS, final=True, pre=tail)

        P.finalize()
    return nc


_NC_CACHE = {}


def _prep_weights(inputs):
    f32 = np.float32

    def tile_cols(w, nchunk):
        return np.ascontiguousarray(w.reshape(KC, 128, nchunk, 128).transpose(2, 1, 0, 3))
    out = {}
    for i, (g, u, d) in enumerate((("f1_wg", "f1_wu", "f1_wd"), ("f2_wg", "f2_wu", "f2_wd"))):
        tg = tile_cols(np.asarray(inputs[g][0], f32), NFC)
        tu = tile_cols(np.asarray(inputs[u][0], f32), NFC)
        out["wgu%d" % (i + 1)] = np.ascontiguousarray(np.stack([tg, tu], axis=2).reshape(NFC, 128, 2 * KC * 128))
        out["wd%d" % (i + 1)] = np.ascontiguousarray(np.asarray(inputs[d][0], f32))
    out["win"] = np.ascontiguousarray(tile_cols(np.asarray(inputs["w_in"][0], f32), NIC).reshape(NIC, 128, KC * 128))
    out["wo"] = np.ascontiguousarray(np.asarray(inputs["w_o"][0], f32))
    ln = [inputs[k][0] for k in ("ln_f1_g", "ln_f1_b", "ln_mix_g", "ln_mix_b", "ln_f2_g", "ln_f2_b")]
    out["lnbc"] = np.ascontiguousarray(np.broadcast_to(np.stack(ln).astype(f32)[:, None, :], (6, 128, D)))
    cst = np.zeros((128, NCST), f32)
    cst[:, C_BIN:C_BIN + NIC] = np.asarray(inputs["b_in"][0], f32).reshape(NIC, 128).T
    wa = np.asarray(inputs["w_dw_a"][0], f32)
    cst[:, C_WA:C_WA + 4 * KA] = wa.reshape(KA, 4, 128).transpose(2, 1, 0).reshape(128, 4 * KA)
    cst[:, C_BA:C_BA + 4] = np.asarray(inputs["b_dw_a"][0], f32).reshape(4, 128).T
    cst[:, C_LG:C_LG + 4] = np.asarray(inputs["ln_conv_g"][0], f32).reshape(4, 128).T
    cst[:, C_LB:C_LB + 4] = np.asarray(inputs["ln_conv_b"][0], f32).reshape(4, 128).T
    wb = np.asarray(inputs["w_dw_b"][0], f32)
    cst[:, C_WB:C_WB + 4 * KB] = wb.reshape(KB, 4, 128).transpose(2, 1, 0).reshape(128, 4 * KB)
    wp = np.zeros((4, 32, 4, 8, 4), f32)
    wa4 = wa.reshape(KA, 4, 4, 32)
    for q in range(8):
        for s_ in range(4):
            j = 4 * q + s_
            if j < KA:
                wp[s_, :, :, q, :] = wa4[j].transpose(2, 0, 1)
    cst[:, C_WP:C_WP + 128] = wp.reshape(128, 128)
    out["cst"] = cst
    est = np.zeros((4, 32, 4, 32), f32)
    est[:, np.arange(32), :, np.arange(32)] = 1.0
    out["estk"] = np.ascontiguousarray(est.reshape(128, 4, 32))
    out["bo"] = np.ascontiguousarray(np.asarray(inputs["b_o"], f32).reshape(1, D))
    out["ident"] = np.eye(128, dtype=f32)
    return out


def kernel(**inputs):
    f32 = np.float32
    if "nc" not in _NC_CACHE:
        _NC_CACHE["nc"] = build_nc()
    nc = _NC_CACHE["nc"]
    shared = _prep_weights(inputs)
    x_prompt = np.asarray(inputs["x_prompt"], f32)
    x_sample = np.asarray(inputs["x_sample"], f32)
    sca = np.asarray(inputs["state_conv_a"], f32)[0]
    scb = np.asarray(inputs["state_conv_b"], f32)[0]
    in_maps = []
    for c in range(NCORE):
        m = dict(shared)
        m["xp"] = np.ascontiguousarray(x_prompt[c])
        m["xs"] = np.ascontiguousarray(x_sample[c * NS:(c + 1) * NS, 0, :])
        m["sca"] = np.ascontiguousarray(sca[c * NS:(c + 1) * NS].reshape(NS * 30, DA))
        m["scb"] = np.ascontiguousarray(scb[c * NS:(c + 1) * NS].reshape(NS * 2, DA))
        in_maps.append(m)
    res = run_bass_kernel_spmd(nc, in_maps, core_ids=list(range(NCORE)))
    r = res.results
    y_prompt = np.stack([r[c]["yp"] for c in range(NCORE)]).astype(f32)
    y_sample = np.concatenate([r[c]["ys"] for c in range(NCORE)], axis=0).reshape(NCORE * NS, 1, D).astype(f32)
    nap = np.stack([r[c]["nap"] for c in range(NCORE)])[None].astype(f32)
    nbp = np.stack([r[c]["nbp"] for c in range(NCORE)])[None].astype(f32)
    nas = np.concatenate([r[c]["nas"].reshape(NS, 30, DA) for c in range(NCORE)], axis=0)[None].astype(f32)
    nbs = np.concatenate([r[c]["nbs"].reshape(NS, 2, DA) for c in range(NCORE)], axis=0)[None].astype(f32)
    return (y_prompt, y_sample, nap, nbp, nas, nbs)
```

```python
import contextlib
import numpy as np
import concourse.bass as bass
import concourse.mybir as mybir
from concourse.bass_utils import run_bass_kernel_spmd

F32 = mybir.dt.float32
BF16 = mybir.dt.bfloat16
AF = mybir.ActivationFunctionType
ALU = mybir.AluOpType
AX = mybir.AxisListType

D = 1024
FF = 2816
NFC = FF // 128
KC = D // 128
DA = 512
DIN = 2560
NIC = DIN // 128
SEQ = 2048
NS = 16
KA = 31
KB = 3
NCORE = 8
BLK = 1024
NTOKB = BLK + NS
ALPHA = float(2.0 ** 0.25)
EPS = 1e-5
NSLOT = 4

C_BIN = 0
C_WA = C_BIN + NIC
C_BA = C_WA + 4 * KA
C_LG = C_BA + 4
C_LB = C_LG + 4
C_WB = C_LB + 4
C_WP = C_WB + 4 * KB
NCST = C_WP + 128
R2W = 1052


class _Op:
    __slots__ = ("eng", "fn", "deps", "dma", "idx", "inc", "waits", "count")


class Prog:
    ENGS = ("pe", "act", "dve", "pool", "sp")

    def __init__(self, nc, stack):
        self.nc = nc
        self.stack = stack
        self.ops = []
        self.lastw = {}
        self.readers = {}
        self.dma_cnt = {}
        self.sems = {}
        self.eng_ops = {e: [] for e in self.ENGS}

    def sem(self, name):
        if name not in self.sems:
            self.sems[name] = self.stack.enter_context(self.nc.semaphore(name))
        return self.sems[name]

    def op(self, eng, fn, reads=(), writes=(), dma=None):
        o = _Op()
        o.eng = eng
        o.fn = fn
        o.inc = False
        o.waits = []
        o.count = 0
        oid = len(self.ops)
        deps = {}
        for r in reads:
            w = self.lastw.get(r)
            if w is not None:
                deps[w] = True
        for wt in writes:
            w = self.lastw.get(wt)
            if w is not None and w not in deps:
                deps[w] = False
            for rd in self.readers.get(wt, ()):
                if rd not in deps:
                    deps[rd] = False
        deps.pop(oid, None)
        o.deps = deps
        if dma is not None:
            self.dma_cnt[dma] = self.dma_cnt.get(dma, 0) + 16
            o.dma = (dma, self.dma_cnt[dma])
            self.sem(dma)
        else:
            o.dma = None
        for r in reads:
            self.readers.setdefault(r, []).append(oid)
        for wt in writes:
            self.lastw[wt] = oid
            self.readers[wt] = []
        o.idx = len(self.eng_ops[eng])
        self.eng_ops[eng].append(o)
        self.ops.append(o)
        return oid

    def finalize(self, final_wait_eng="sp"):
        nc = self.nc
        esem = {e: self.sem("S_" + e) for e in ("pe", "act", "dve", "pool")}
        seen_eng = {e: {f: -1 for f in self.ENGS} for e in self.ENGS}
        seen_dma = {e: {} for e in self.ENGS}
        pending = []
        for o in self.ops:
            e = o.eng
            dma_w = {}
            eng_w = {}
            for pid, raw in o.deps.items():
                p = self.ops[pid]
                if p.dma is not None:
                    s, v = p.dma
                    if seen_dma[e].get(s, 0) >= v:
                        continue
                    if dma_w.get(s, 0) < v:
                        dma_w[s] = v
                else:
                    if p.eng == e:
                        if e == "pe" or e == "sp":
                            continue
                    if seen_eng[e][p.eng] >= p.idx:
                        continue
                    if eng_w.get(p.eng, -1) < p.idx:
                        eng_w[p.eng] = p.idx
            for s, v in dma_w.items():
                seen_dma[e][s] = v
                o.waits.append(("dma", s, v))
            for f, idx in eng_w.items():
                seen_eng[e][f] = idx
                prod = self.eng_ops[f][idx]
                prod.inc = True
                o.waits.append(("eng", f, prod))
        for e in ("pe", "act", "dve", "pool"):
            c = 0
            for o in self.eng_ops[e]:
                if o.inc:
                    c += 1
                    o.count = c
        final = [(self.sems[s], v) for s, v in self.dma_cnt.items()]
        sems = self.sems

        def body_for(ename):
            def body(eng):
                for o in self.eng_ops[ename]:
                    for w in o.waits:
                        if w[0] == "dma":
                            eng.wait_ge(sems[w[1]], w[2])
                        else:
                            eng.wait_ge(esem[w[1]], w[2].count)
                    ins = o.fn(eng)
                    if o.dma is not None:
                        ins.then_inc(sems[o.dma[0]], 16)
                    elif o.inc:
                        ins.then_inc(esem[ename], 1)
                if ename == final_wait_eng:
                    for s, v in final:
                        eng.wait_ge(s, v)
            return body

        with nc.Block() as block:
            block.tensor(body_for("pe"))
            block.scalar(body_for("act"))
            block.vector(body_for("dve"))
            block.gpsimd(body_for("pool"))
            block.sync(body_for("sp"))


def build_nc():
    nc = bass.Bass("TRN2", target_bir_lowering=False)
    dt = nc.dram_tensor

    def din(name, shape):
        return dt(name, list(shape), F32, kind="ExternalInput").ap()

    def dout(name, shape):
        return dt(name, list(shape), F32, kind="ExternalOutput").ap()

    xp = din("xp", [SEQ, D])
    xs = din("xs", [NS, D])
    sca = din("sca", [NS * 30, DA])
    scb = din("scb", [NS * 2, DA])
    wgu = [din("wgu1", [NFC, 128, 2 * KC * 128]), din("wgu2", [NFC, 128, 2 * KC * 128])]
    wdd = [din("wd1", [FF, D]), din("wd2", [FF, D])]
    win = din("win", [NIC, 128, KC * 128])
    wod = din("wo", [D, D])
    lnbc = din("lnbc", [6, 128, D])
    cstd = din("cst", [128, NCST])
    bod = din("bo", [1, D])
    identd = din("ident", [128, 128])
    estkd = din("estk", [128, 4, 32])
    uscr_t = nc.dram_tensor("uscr", [2, 128, 1056], BF16).ap()
    uscr = uscr_t
    yp = dout("yp", [SEQ, D])
    ys = dout("ys", [NS, D])
    nap = dout("nap", [30, DA])
    nbp = dout("nbp", [2, DA])
    nas = dout("nas", [NS * 30, DA])
    nbs = dout("nbs", [NS * 2, DA])

    stack = contextlib.ExitStack()
    with stack:
        def sb(name, shape, dtype):
            return stack.enter_context(nc.sbuf_tensor("sb_" + name, list(shape), dtype))

        P = Prog(nc, stack)
        ident = sb("ident", [128, 128], F32)
        cst = sb("cst", [128, NCST], F32)
        lnb = sb("lnb", [128, 2, D], F32)
        dgb = sb("dgb", [128, 2, 8 * 128], BF16)
        bohl = dgb[0:1, :, :]
        ones = sb("ones", [1, 128], BF16)
        ones2 = sb("ones2", [2, 128], BF16)
        bo2 = sb("bo2", [2, D], BF16)
        xT = sb("xT", [128, KC, NTOKB], BF16)
        ring = sb("ring", [128, NSLOT, 2 * KC * 128], BF16)
        xres = sb("xres", [128, 9, D], F32)
        zt = sb("zt", [128, 2, D], F32)
        xn = sb("xn", [128, 2, D], F32)
        sg = sb("sg", [128, 2, 512], F32)
        sgS = sb("sgS", [128, NS], F32)
        NXB = 4
        xb = zt[:].bitcast(BF16).rearrange("p a (s d) -> p (a s) d", s=2)
        ident_bf = sb("ident_bf", [128, 128], BF16)
        sast = sb("sast", [128, 4, 512], F32)
        sbst = sb("sbst", [32, 512], F32)
        stt = sb("stt", [128, 2, 2, 6], F32)
        mv = sb("mv", [128, 2, 8], F32)
        mv_ln = mv
        mvc = sb("mvc", [128, 2, 8], F32)
        sttc = sb("sttc", [128, 2, 6], F32)
        neghalf = sb("neghalf", [128, 2], F32)
        hbg = sb("hbg", [128, 4], F32)
        halo = sb("halo", [128, 4, 32], F32)
        dummy = sb("dummy", [128, 8], F32)
        E4 = sb("E4", [128, 4, 32], F32)
        R_HT = NFC * NTOKB
        R_WD = NFC * D
        RB = sb("R", [128, R_HT + R_WD], BF16)
        hT = RB[:, 0:R_HT].rearrange("p (f t) -> p f t", f=NFC)
        wd = RB[:, R_HT:R_HT + R_WD].rearrange("p (f d) -> p f d", f=NFC)
        RF = RB[:].bitcast(F32)
        off = 0

        def rf(n):
            nonlocal off
            a = RF[:, off:off + n]
            off += n
            return a
        UV = rf(4 * 1056).rearrange("p (c t) -> p c t", c=4)
        R2 = RB[:, 0:2 * 4 * R2W].rearrange("p (s g t) -> p s g t", s=2, g=4)
        acc = rf(4 * 1024).rearrange("p (c t) -> p c t", c=4)
        CA = rf(4 * 480).rearrange("p (c t) -> p c t", c=4)
        CB = rf(4 * 32).rearrange("p (c t) -> p c t", c=4)
        prod = rf(480)
        xnc = rf(2 * 512).rearrange("p (s t) -> p s t", s=2)
        USb = rf(4 * 16).rearrange("p (c t) -> p c t", c=4)
        VSb = rf(4 * 16).rearrange("p (c t) -> p c t", c=4)
        accS = rf(4 * 16).rearrange("p (c t) -> p c t", c=4)
        redS = rf(4 * 16).rearrange("p (c t) -> p c t", c=4)
        off_bf = off * 2
        catT = RB[:, off_bf:off_bf + KC * NTOKB].rearrange("p (k t) -> p k t", k=KC)
        off_bf += KC * NTOKB
        wo = RB[:, off_bf:off_bf + KC * D].rearrange("p (k d) -> p k d", k=KC)
        off_bf += KC * D
        U_bf = RB[:, off_bf:off_bf + 4 * 1056].rearrange("p (c t) -> p c t", c=4)
        off_bf += 4 * 1056
        assert off_bf <= R_HT + R_WD, (off_bf, R_HT + R_WD)

        ps = stack.enter_context(nc.psum_tensor("ps", [128, 8 * 512], F32))
        psb = ps[:].bitcast(BF16)

        def pst(b, n=1):
            return tuple(("ps", b + i) for i in range(n))

        P.op("sp", lambda e: e.dma_start(out=ident[:], in_=identd), writes=["ident"], dma="c_ident")
        P.op("sp", lambda e: e.dma_start(out=cst[:], in_=cstd), writes=["cst"], dma="c_cst")
        P.op("sp", lambda e: e.dma_start(out=E4[:], in_=estkd), writes=["E4"], dma="c_e4")
        P.op("sp", lambda e: e.dma_start(out=xn[0:1, 1, :], in_=bod), writes=[("xn", 1)], dma="c_bo")
        P.op("dve", lambda e: e.memset(ones[:], 1.0), writes=["ones"])
        P.op("dve", lambda e: e.tensor_copy(out=ident_bf[:], in_=ident[:]), reads=["ident"], writes=["ident_bf"])
        P.op("dve", lambda e: e.tensor_copy(out=bohl[0:1, 0, :], in_=xn[0:1, 1, :]),
             reads=[("xn", 1)], writes=["bohi"])
        P.op("dve", lambda e: e.tensor_copy(out=xn[0:1, 0, :], in_=bohl[0:1, 0, :]),
             reads=["bohi"], writes=[("xn", 0)])
        P.op("dve", lambda e: e.tensor_tensor(out=bohl[0:1, 1, :], in0=xn[0:1, 1, :], in1=xn[0:1, 0, :],
                                              op=ALU.subtract),
             reads=[("xn", 1), ("xn", 0)], writes=["bolo"])
        P.op("dve", lambda e: e.memset(ones2[:], 1.0), writes=["ones2"])
        P.op("act", lambda e: e.activation(out=dummy[0:2, 4:5], in_=ones2[0:2, 0:1], func=AF.Silu), reads=["ones2"],
             writes=["dummy_act"])
        P.op("dve", lambda e: e.memset(halo[:], 0.0), writes=["halo"] + [("haloA", c) for c in range(4)])
        P.op("dve", lambda e: e.memset(neghalf[:], -0.5), writes=["neghalf"])
        P.op("dve", lambda e: e.tensor_scalar(out=hbg[:, :], in0=cst[:, C_BIN + 4:C_BIN + 8], scalar1=0.5, scalar2=None,
                                              op0=ALU.mult),
             reads=["cst"], writes=["hbg"])

        rmode_tok = "RMODE"
        state = {"ring_i": 0, "psr": 0, "lnslot": 0, "dg_i": 0, "upar": 0, "xb_i": 0}

        def ring_slot():
            s = state["ring_i"] % NSLOT
            state["ring_i"] += 1
            return s

        preloaded = {}

        def fetch_win(b, fo, prefetch=False):
            key = ("win", b, fo)
            if key in preloaded and not prefetch:
                return preloaded.pop(key)
            s_ = ring_slot()
            P.op("pool", lambda e: e.dma_start(out=ring[:, s_, 0:KC * 128], in_=win[fo]),
                 writes=[("ring", s_)], dma="ring%d" % s_)
            if prefetch:
                preloaded[key] = s_
            return s_

        def fetch_wgu(b, which, f, prefetch=False):
            key = ("wgu", b, which, f)
            if key in preloaded and not prefetch:
                return preloaded.pop(key)
            s_ = ring_slot()
            extra = [("xres", 3)] if (b == 0 and which == 0 and 1 <= f < NSLOT) else []
            P.op("pool", lambda e: e.dma_start(out=ring[:, s_, :], in_=wgu[which][f], max_dma_last_dim=8192),
                 reads=extra, writes=[("ring", s_)], dma="ring%d" % s_)
            if prefetch:
                preloaded[key] = s_
            return s_

        def tiles_of(b):
            t = [(i, i * 128, 128) for i in range(8)]
            if b == 1:
                t.append((8, BLK, NS))
            return t

        def tgroups_of(b):
            g = [(0, 512), (512, 512)]
            if b == 1:
                g.append((BLK, NS))
            return g

        def load_ln(idx, b):
            P.op("sp", lambda e: e.dma_start(out=lnb[:, 0, :], in_=lnbc[2 * idx]),
                 writes=[("lnb", 0)], dma="c_lnb0")
            P.op("sp", lambda e: e.dma_start(out=lnb[:, 1, :], in_=lnbc[2 * idx + 1]),
                 writes=[("lnb", 1)], dma="c_lnb1")

        def rmode_switch():
            P.op("dve", lambda e: e.memset(dummy[:, 0:1], 0.0), reads=[], writes=[rmode_tok])

        def rstd_chain(n, sl, eps, mv=None, tg=""):
            if mv is None:
                mv = mv_ln
            P.op("dve", lambda e: e.tensor_scalar(out=mv[:n, sl, 4:5], in0=mv[:n, sl, 1:2], scalar1=eps, scalar2=None,
                                                  op0=ALU.add),
                 reads=[("mv" + tg, sl)], writes=[("mv1" + tg, sl)])
            P.op("pool", lambda e: e.tensor_tensor(out=mv[:n, sl, 2:3], in0=mv[:n, sl, 4:5], in1=neghalf[:n, 0:1], op=ALU.pow),
                 reads=[("mv1" + tg, sl), "neghalf"], writes=[("mv2" + tg, sl)])
            P.op("dve", lambda e: e.scalar_tensor_tensor(out=mv[:n, sl, 3:4], in0=mv[:n, sl, 0:1], scalar=-1.0,
                                                          in1=mv[:n, sl, 2:3], op0=ALU.mult, op1=ALU.mult),
                 reads=[("mv" + tg, sl), ("mv2" + tg, sl)], writes=[("mv3" + tg, sl)])

        def cast_x(tile):
            (t, tok0, n) = tile
            sl = state["xb_i"] % NXB
            state["xb_i"] += 1
            P.op("act", lambda e: e.activation(out=xb[:n, sl, :], in_=xres[:n, t, :], func=AF.Identity),
                 reads=[("xres", t)], writes=[("xb", sl)])
            return sl

        def transposes_to_xT(b, t, tok0, n, bank=None, xslot=None):
            if bank is None:
                bank = state["psr"] % 4
                state["psr"] += 1
            if xslot is None:
                sl = state["xb_i"] % NXB
                state["xb_i"] += 1
                P.op("act", lambda e: e.activation(out=xb[:n, sl, :], in_=xres[:n, t, :], func=AF.Identity),
                     reads=[("xres", t)], writes=[("xb", sl)])
            else:
                sl = xslot

            def pe(e):
                ins = None
                for k in range(KC):
                    ins = e.transpose(out=psb[:, bank * 1024 + k * n: bank * 1024 + (k + 1) * n],
                                      in_=xb[:n, sl, k * 128:(k + 1) * 128],
                                      identity=ident_bf[:n, :n])
                return ins
            P.op("pe", pe, reads=[("xb", sl), "ident_bf"], writes=pst(bank))

            def ev(e):
                return e.activation(out=xT[:, :, tok0:tok0 + n],
                                    in_=psb[:, bank * 1024: bank * 1024 + KC * n].rearrange("p (k n) -> p k n", k=KC),
                                    func=AF.Identity)
            P.op("act", ev, reads=pst(bank), writes=[("xT", tok0)])

        def ln_head(t, n, src_ps_banks, in_scale, eps, after_z=None):
            sl = state["lnslot"] % 2
            state["lnslot"] += 1
            pb = src_ps_banks
            P.op("dve", lambda e: e.scalar_tensor_tensor(out=xn[:n, sl, :], in0=xres[:n, t, :], scalar=in_scale,
                                                          in1=ps[:n, pb * 512:(pb + 2) * 512],
                                                          op0=ALU.mult, op1=ALU.add),
                 reads=[("xres", t)] + list(pst(pb, 2)), writes=[("xn", sl)])
            if after_z is not None:
                after_z()
            for h in range(2):
                P.op("dve", lambda e, h=h: e.bn_stats(out=stt[:n, sl, h, :], in_=xn[:n, sl, h * 512:(h + 1) * 512]),
                     reads=[("xn", sl)], writes=[("stt", sl, h)])
            P.op("dve", lambda e: e.bn_aggr(out=mv[:n, sl, 0:2], in_=stt[:n, sl, :, :].rearrange("p a b -> p (a b)")),
                 reads=[("stt", sl, 0), ("stt", sl, 1)], writes=[("mv", sl)])
            rstd_chain(n, sl, eps)
            return (t, n, sl)

        def ln_tail(ctx, final_store=None, make_bf=True):
            (t, n, sl) = ctx
            P.op("act", lambda e: e.activation(out=xn[:n, sl, :], in_=xn[:n, sl, :], func=AF.Identity,
                                               scale=mv[:n, sl, 2:3], bias=mv[:n, sl, 3:4]),
                 reads=[("xn", sl), ("mv2", sl), ("mv3", sl)], writes=[("xn", sl)])
            P.op("dve", lambda e: e.tensor_tensor(out=xn[:n, sl, :], in0=xn[:n, sl, :], in1=lnb[:n, 0, :], op=ALU.mult),
                 reads=[("xn", sl), ("lnb", 0)], writes=[("xn", sl)])
            if final_store is None:
                xs_ = None
                if make_bf:
                    xs_ = state["xb_i"] % NXB
                    state["xb_i"] += 1
                    P.op("dve", lambda e: e.tensor_tensor(out=xb[:n, xs_, :], in0=xn[:n, sl, :], in1=lnb[:n, 1, :], op=ALU.add),
                         reads=[("xn", sl), ("lnb", 1)], writes=[("xb", xs_)])
                P.op("dve", lambda e: e.tensor_tensor(out=xres[:n, t, :], in0=xn[:n, sl, :], in1=lnb[:n, 1, :], op=ALU.add),
                     reads=[("xn", sl), ("lnb", 1)], writes=[("xres", t)])
                return xs_
            dst = final_store
            P.op("dve", lambda e: e.tensor_tensor(out=xn[:n, sl, :], in0=xn[:n, sl, :], in1=lnb[:n, 1, :], op=ALU.add),
                 reads=[("xn", sl), ("lnb", 1)], writes=[("xn", sl)])
            P.op("sp", lambda e: e.dma_start(out=dst, in_=xn[:n, sl, :]), reads=[("xn", sl)],
                 dma="st_xn%d" % sl)
            return None

        def layer_norm_tile(t, n, src_ps_banks, in_scale, eps, b, final_store=None, after_z=None):
            return ln_tail(ln_head(t, n, src_ps_banks, in_scale, eps, after_z=after_z), final_store=final_store)

        def ffn(b, which, in_scale, eps, final, defer_last=False, pre=None):
            tiles = tiles_of(b)
            tgs = tgroups_of(b)
            wd_chunks = 11
            slots = {}

            def load_w(f):
                slots[f] = fetch_wgu(b, which, f)
                if 4 <= f < 4 + wd_chunks:
                    i = f - 4
                    P.op("pool", lambda e: e.dma_start(
                        out=wd[:, 2 * i:2 * i + 2, :],
                        in_=wdd[which][2 * i * 128:(2 * i + 2) * 128, :].rearrange("(f p) d -> p f d", p=128)),
                        reads=[rmode_tok], writes=[("wd", i)], dma="wd%d" % i)

            def u_group(f, tok0, n):
                s_ = slots[f]
                if n == 512:
                    par = state["upar"] % 2
                    state["upar"] += 1
                    bg, bu = par, 2 + par
                    sgt = sg[:, par, :]
                    sgtok = ("sg", par)
                else:
                    bg, bu = 4, 5
                    sgt = sgS[:, :]
                    sgtok = "sgS"

                def pe(e):
                    ins = None
                    for gu, bk in ((0, bg), (1, bu)):
                        for k in range(KC):
                            c0 = (gu * KC + k) * 128
                            ins = e.matmul(ps[:, bk * 512: bk * 512 + n], lhsT=ring[:, s_, c0:c0 + 128],
                                           rhs=xT[:, k, tok0:tok0 + n], start=(k == 0), stop=(k == KC - 1))
                    return ins
                xtoks = [("xT", t0) for (_, t0, nn) in tiles if t0 >= tok0 and t0 < tok0 + max(n, 1)]
                P.op("pe", pe, reads=[("ring", s_)] + xtoks, writes=[("ps", bg), ("ps", bu)])
                P.op("act", lambda e: e.activation(out=sgt[:, :n], in_=ps[:, bg * 512: bg * 512 + n], func=AF.Silu),
                     reads=[("ps", bg)], writes=[sgtok])
                P.op("dve", lambda e: e.tensor_tensor(
                    out=hT[:, f, tok0:tok0 + n], in0=sgt[:, :n], in1=ps[:, bu * 512: bu * 512 + n], op=ALU.mult),
                    reads=[sgtok, ("ps", bu), rmode_tok], writes=[("hT", f, tok0)])

            pre = list(pre) if pre else []
            npre = min(len(pre), NSLOT)
            for f in range(npre):
                load_w(f)
            for f in range(npre):
                u_group(f, *tgs[0])
                pre.pop(0)()
            for th in pre:
                th()
            for f in range(NFC):
                if f >= npre:
                    load_w(f)
                for gi, (tok0, n) in enumerate(tgs):
                    if f < npre and gi == 0:
                        continue
                    u_group(f, tok0, n)
            if which == 0:
                for fo in (12, 16, 13, 17):
                    fetch_win(b, fo, prefetch=True)
            elif b == 0:
                for f in range(NSLOT):
                    fetch_wgu(1, 0, f, prefetch=True)
            wd_toks = [("wd", i) for i in range(wd_chunks)]
            pending = None
            for (t, tok0, n) in tiles:
                yb = 4 + 2 * (t % 2)
                g0 = 0 if tok0 < 512 else (512 if tok0 < BLK else BLK)

                def pe(e, tok0=tok0, n=n, yb=yb):
                    ins = None
                    for f in range(NFC):
                        for h in range(2):
                            ins = e.matmul(ps[:n, (yb + h) * 512:(yb + h + 1) * 512], lhsT=hT[:, f, tok0:tok0 + n],
                                           rhs=wd[:, f, h * 512:(h + 1) * 512], start=(f == 0), stop=(f == NFC - 1))
                    return ins
                P.op("pe", pe, reads=[("hT", f, g0) for f in range(NFC)] + wd_toks + [rmode_tok], writes=pst(yb, 2))
                if pending is not None:
                    transposes_to_xT(b, *pending)
                    pending = None
                if final:
                    hook = None
                    if b == 0:
                        if t >= 1:
                            xpose_x_tile(1, tiles_of(1)[t - 1])
                        if t == 2:
                            xpose_x_tile(1, tiles_of(1)[8])
                        hook = (lambda t=t: load_x(1, tiles_of(1)[t]))
                    dst = yp[b * BLK + tok0: b * BLK + tok0 + n, :] if t < 8 else ys[:, :]
                    layer_norm_tile(t, n, yb, in_scale, eps, b, final_store=dst, after_z=hook)
                else:
                    xs_ = layer_norm_tile(t, n, yb, in_scale, eps, b)
                    pending = (t, tok0, n, None, xs_)
            if pending is not None and not defer_last:
                transposes_to_xT(b, *pending)
                pending = None
            return pending

        def load_x(b, tile):
            (t, tok0, n) = tile
            src = xp[b * BLK + tok0: b * BLK + tok0 + n, :] if t < 8 else xs[:, :]
            P.op("sp", lambda e: e.dma_start(out=xres[:n, t, :], in_=src), writes=[("xres", t)], dma="ld_x%d" % t)

        def xpose_x_tile(b, tile):
            (t, tok0, n) = tile
            transposes_to_xT(b, t, tok0, n)

        def load_x_tile(b, tile):
            load_x(b, tile)
            xpose_x_tile(b, tile)

        def early_sample_state():
            sca3 = sca.rearrange("(s j) c -> s j c", j=30)
            nas3 = nas.rearrange("(s j) c -> s j c", j=30)
            scb3 = scb.rearrange("(s j) c -> s j c", j=2)
            nbs3 = nbs.rearrange("(s j) c -> s j c", j=2)
            for r in range(4):
                rows = 128 if r < 3 else 96
                P.op("sp", lambda e, r=r, rows=rows: e.dma_start(out=sast[:rows, r, :], in_=sca[r * 128: r * 128 + rows, :]),
                     writes=["sast"] if r == 3 else [("sast", r)], dma="sa")
            P.op("sp", lambda e: e.dma_start(out=sbst[:, :], in_=scb), writes=["sbst"], dma="sbld")
            P.op("sp", lambda e: e.dma_start(out=nas3[:, 0:29, :], in_=sca3[:, 1:30, :]), dma="st_nas0")
            P.op("sp", lambda e: e.dma_start(out=nbs3[:, 0:1, :], in_=scb3[:, 1:2, :]), dma="st_nbs0")

        def mixer(b, pendT=None):
            tiles = tiles_of(b)
            tgs = tgroups_of(b)
            has_s = (b == 1)
            rmode_switch()
            MR = [rmode_tok]
            sca3 = sca.rearrange("(s j) c -> s j c", j=30)
            nas3 = nas.rearrange("(s j) c -> s j c", j=30)
            scb3 = scb.rearrange("(s j) c -> s j c", j=2)
            nbs3 = nbs.rearrange("(s j) c -> s j c", j=2)
            if has_s:
                for ch in range(4):
                    pb = ch % 2

                    def pe(e, ch=ch, pb=pb):
                        ins = None
                        for r in range(4):
                            rows = 128 if r < 3 else 96
                            ins = e.transpose(out=ps[:, pb * 512 + r * 128: pb * 512 + r * 128 + rows],
                                              in_=sast[:rows, r, ch * 128:(ch + 1) * 128],
                                              identity=ident[:rows, :rows])
                        return ins
                    P.op("pe", pe, reads=["sast", "ident"], writes=pst(pb))
                    P.op("act", lambda e, ch=ch, pb=pb: e.activation(out=CA[:, ch, :], in_=ps[:, pb * 512: pb * 512 + 480],
                                                                     func=AF.Identity),
                         reads=list(pst(pb)) + MR, writes=[("CA", ch)])
                pb = 2

                def pe2(e, pb=pb):
                    ins = None
                    for ch in range(4):
                        ins = e.transpose(out=ps[:, pb * 512 + ch * 32: pb * 512 + (ch + 1) * 32],
                                          in_=sbst[:32, ch * 128:(ch + 1) * 128], identity=ident[:32, :32])
                    return ins
                P.op("pe", pe2, reads=["sbst", "ident"], writes=pst(pb))
                P.op("act", lambda e, pb=pb: e.activation(out=CB[:, :, :], in_=ps[:, pb * 512: pb * 512 + 128].rearrange("p (c t) -> p c t", c=4),
                                                   func=AF.Identity),
                     reads=list(pst(pb)) + MR, writes=["CB"])

            def win_chunk(fo, consumer):
                s = fetch_win(b, fo)
                for gi, (tok0, n) in enumerate(tgs):
                    bk = state["psr"] % 4
                    state["psr"] += 1

                    def pe(e, tok0=tok0, n=n, bk=bk):
                        ins = None
                        for k in range(KC):
                            ins = e.matmul(ps[:, bk * 512: bk * 512 + n], lhsT=ring[:, s, k * 128:(k + 1) * 128],
                                           rhs=xT[:, k, tok0:tok0 + n], start=(k == 0), stop=(k == KC - 1))
                        return ins
                    xtoks = [("xT", t0) for (_, t0, nn) in tiles if t0 >= tok0 and t0 < tok0 + n]
                    P.op("pe", pe, reads=[("ring", s)] + xtoks, writes=[("ps", bk)])
                    consumer(gi, tok0, n, bk)

            def cc(col):
                return cst[:, col:col + 1]

            P.op("dve", lambda e: e.tensor_copy(out=UV[:, :, 0:2], in_=halo[:, :, 30:32]), reads=["halo"] + MR,
                 writes=[("UVh", c) for c in range(4)])
            for ch in range(4):
                s_gc = fetch_win(b, 12 + ch)
                s_bv = fetch_win(b, 16 + ch)
                for gi, (tok0, n) in enumerate(tgs):
                    b1 = state["psr"] % 4
                    b2 = (state["psr"] + 1) % 4
                    state["psr"] += 2
                    xtoks = [("xT", t0) for (_, t0, nn) in tiles if t0 >= tok0 and t0 < tok0 + n]
                    for (slot_w, bk) in ((s_gc, b1), (s_bv, b2)):
                        def pe(e, tok0=tok0, n=n, bk=bk, slot_w=slot_w):
                            ins = None
                            for k in range(KC):
                                ins = e.matmul(ps[:, bk * 512: bk * 512 + n], lhsT=ring[:, slot_w, k * 128:(k + 1) * 128],
                                               rhs=xT[:, k, tok0:tok0 + n], start=(k == 0), stop=(k == KC - 1))
                            return ins
                        P.op("pe", pe, reads=[("ring", slot_w)] + xtoks, writes=[("ps", bk)])
                    sl = gi % 2
                    P.op("act", lambda e, n=n, b1=b1, sl=sl, ch=ch: e.activation(
                        out=sg[:, sl, :n], in_=ps[:, b1 * 512: b1 * 512 + n], func=AF.Identity, bias=cc(C_BIN + 12 + ch)),
                        reads=[("ps", b1), "cst"], writes=[("sg", sl)])
                    if tok0 < BLK:
                        dst = UV[:, ch, 2 + tok0: 2 + tok0 + n]
                        wtok = ("UV", ch, tok0)
                    else:
                        dst = VSb[:, ch, :n]
                        wtok = ("VS", ch)
                    P.op("dve", lambda e, n=n, b2=b2, sl=sl, ch=ch, dst=dst: e.scalar_tensor_tensor(
                        out=dst, in0=ps[:, b2 * 512: b2 * 512 + n], scalar=cc(C_BIN + 16 + ch), in1=sg[:, sl, :n],
                        op0=ALU.add, op1=ALU.mult),
                        reads=[("ps", b2), ("sg", sl), "cst"] + MR, writes=[wtok])
                    if ch == 0 and gi == 0 and pendT is not None:
                        transposes_to_xT(b, *pendT)
                wb = C_WB + ch * KB
                uvr = [("UV", ch, 0), ("UV", ch, 512), ("UVh", ch)]
                P.op("dve", lambda e, ch=ch, wb=wb: e.tensor_scalar(out=acc[:, ch, :], in0=UV[:, ch, 0:BLK], scalar1=cc(wb),
                                                                   scalar2=None, op0=ALU.mult),
                     reads=uvr + ["cst"] + MR, writes=[("acc", ch)])
                for j in (1, 2):
                    P.op("dve", lambda e, ch=ch, wb=wb, j=j: e.scalar_tensor_tensor(
                        out=acc[:, ch, :], in0=UV[:, ch, j:j + BLK], scalar=cc(wb + j), in1=acc[:, ch, :],
                        op0=ALU.mult, op1=ALU.add),
                        reads=uvr + [("acc", ch), "cst"], writes=[("acc", ch)])
                if has_s:
                    cb3 = CB[:, ch, :].rearrange("p (s j) -> p s j", j=2)
                    P.op("dve", lambda e, ch=ch, wb=wb, cb3=cb3: e.tensor_scalar(
                        out=accS[:, ch, :], in0=cb3[:, :, 0], scalar1=cc(wb), scalar2=None, op0=ALU.mult),
                        reads=["CB", "cst"] + MR, writes=[("accS", ch)])
                    P.op("dve", lambda e, ch=ch, wb=wb, cb3=cb3: e.scalar_tensor_tensor(
                        out=accS[:, ch, :], in0=cb3[:, :, 1], scalar=cc(wb + 1), in1=accS[:, ch, :],
                        op0=ALU.mult, op1=ALU.add),
                        reads=["CB", "cst", ("accS", ch)], writes=[("accS", ch)])
                    P.op("dve", lambda e, ch=ch, wb=wb: e.scalar_tensor_tensor(
                        out=accS[:, ch, :], in0=VSb[:, ch, :], scalar=cc(wb + 2), in1=accS[:, ch, :],
                        op0=ALU.mult, op1=ALU.add),
                        reads=[("VS", ch), "cst", ("accS", ch)], writes=[("accS", ch)])

                def cons_gb(gi, tok0, n, bk, ch=ch):
                    if tok0 < BLK:
                        src = acc[:, ch, tok0:tok0 + n]
                        rt = ("acc", ch)
                    else:
                        src = accS[:, ch, :n]
                        rt = ("accS", ch)
                    P.op("dve", lambda e: e.scalar_tensor_tensor(
                        out=catT[:, 4 + ch, tok0:tok0 + n], in0=ps[:, bk * 512: bk * 512 + n], scalar=cc(C_BIN + 8 + ch),
                        in1=src, op0=ALU.add, op1=ALU.mult),
                        reads=[("ps", bk), rt, "cst"] + MR, writes=[("catT", 4 + ch, tok0)])
                win_chunk(8 + ch, cons_gb)
            P.op("dve", lambda e: e.tensor_copy(out=halo[:, :, 30:32], in_=UV[:, :, BLK:BLK + 2]),
                 reads=[("UV", c, 512) for c in range(4)], writes=["halo"])
            if b == 1:
                pb = 3

                def pev(e, pb=pb):
                    ins = None
                    for ch in range(4):
                        ins = e.transpose(out=ps[:2, pb * 512 + ch * 128: pb * 512 + (ch + 1) * 128],
                                          in_=UV[:, ch, BLK:BLK + 2], identity=ident[:, :])
                    return ins
                P.op("pe", pev, reads=[("UV", c, 512) for c in range(4)] + ["ident"], writes=pst(pb))
                P.op("act", lambda e, pb=pb: e.activation(out=xn[:2, 1, 0:512], in_=ps[:2, pb * 512:(pb + 1) * 512], func=AF.Identity),
                     reads=pst(pb), writes=[("xn", 1)])
                P.op("sp", lambda e: e.dma_start(out=nbp, in_=xn[:2, 1, 0:512]), reads=[("xn", 1)], dma="st_nbp")
                pb = 2

                def pevs(e, pb=pb):
                    ins = None
                    for ch in range(4):
                        ins = e.transpose(out=ps[:NS, pb * 512 + ch * 128: pb * 512 + (ch + 1) * 128],
                                          in_=VSb[:, ch, :], identity=ident[:, :])
                    return ins
                P.op("pe", pevs, reads=[("VS", c) for c in range(4)] + ["ident"], writes=pst(pb))
                P.op("act", lambda e, pb=pb: e.activation(out=xn[:NS, 0, 512:1024], in_=ps[:NS, pb * 512:(pb + 1) * 512], func=AF.Identity),
                     reads=pst(pb), writes=[("xn", 0)])
                P.op("sp", lambda e: e.dma_start(out=nbs3[:, 1, :], in_=xn[:NS, 0, 512:1024]), reads=[("xn", 0)],
                     dma="st_nbs1")

            P.op("pool", lambda e: e.dma_start(out=wo[:, :, :], in_=wod.rearrange("(k p) d -> p k d", p=128)),
                 reads=MR, writes=["wo"], dma="wo")
            P.op("dve", lambda e: e.memset(dummy[:, 7:8], 0.0),
                 writes=["R2g"] + [("UV", c, g_) for c in range(4) for g_ in (0, 512)] + [("UVh", c) for c in range(4)])
            P.op("dve", lambda e: e.memset(U_bf[:, :, 1054:1056], 0.0), reads=MR, writes=["Ubfz"])
            P.op("dve", lambda e: e.tensor_copy(out=U_bf[:, :, 0:30], in_=halo[:, :, 0:30]),
                 reads=[("haloA", c) for c in range(4)] + MR, writes=[("Ubfh", c) for c in range(4)])

            def gv_matmuls(ch):
                s_g = fetch_win(b, 4 + ch)
                s_v = fetch_win(b, ch)
                for gi, (tok0, n) in enumerate(tgs):
                    b1 = state["psr"] % 4
                    b2 = (state["psr"] + 1) % 4
                    state["psr"] += 2
                    xtoks = [("xT", t0) for (_, t0, nn) in tiles if t0 >= tok0 and t0 < tok0 + n]
                    for (slot_w, bk) in ((s_g, b1), (s_v, b2)):
                        def pe(e, tok0=tok0, n=n, bk=bk, slot_w=slot_w):
                            ins = None
                            for k in range(KC):
                                ins = e.matmul(ps[:, bk * 512: bk * 512 + n], lhsT=ring[:, slot_w, k * 128:(k + 1) * 128],
                                               rhs=xT[:, k, tok0:tok0 + n], start=(k == 0), stop=(k == KC - 1))
                            return ins
                        P.op("pe", pe, reads=[("ring", slot_w)] + xtoks, writes=[("ps", bk)])
                    sl = gi % 2
                    P.op("act", lambda e, n=n, b1=b1, sl=sl, ch=ch: e.activation(
                        out=sg[:, sl, :n], in_=ps[:, b1 * 512: b1 * 512 + n], func=AF.Tanh, scale=0.5,
                        bias=hbg[:, ch:ch + 1]),
                        reads=[("ps", b1), "hbg"], writes=[("sg", sl)])
                    P.op("dve", lambda e, n=n, sl=sl: e.tensor_scalar(out=sg[:, sl, :n], in0=sg[:, sl, :n], scalar1=0.5,
                                                                     scalar2=0.5, op0=ALU.mult, op1=ALU.add),
                         reads=[("sg", sl)], writes=[("sg", sl)])
                    if tok0 < BLK:
                        dst = U_bf[:, ch, 30 + tok0: 30 + tok0 + n]
                        wtok = ("Ubf", ch, tok0)
                    else:
                        dst = USb[:, ch, :n]
                        wtok = ("US", ch)
                    P.op("dve", lambda e, n=n, b2=b2, sl=sl, ch=ch, dst=dst: e.scalar_tensor_tensor(
                        out=dst, in0=ps[:, b2 * 512: b2 * 512 + n], scalar=cc(C_BIN + ch), in1=sg[:, sl, :n],
                        op0=ALU.add, op1=ALU.mult),
                        reads=[("ps", b2), ("sg", sl), "cst"] + MR, writes=[wtok])
                    if tok0 == 512:
                        P.op("dve", lambda e, b2=b2, sl=sl, ch=ch: e.scalar_tensor_tensor(
                            out=halo[:, ch, 0:30], in0=ps[:, b2 * 512 + 482: b2 * 512 + 512], scalar=cc(C_BIN + ch),
                            in1=sg[:, sl, 482:512], op0=ALU.add, op1=ALU.mult),
                            reads=[("ps", b2), ("sg", sl), "cst"], writes=[("haloA", ch)])
            def rep_out(ch):
                rs = ch % 2
                P.op("sp", lambda e: e.dma_start(out=uscr[rs], in_=U_bf[:, ch, :]),
                     reads=[("Ubf", ch, 0), ("Ubf", ch, 512), ("Ubfh", ch), "Ubfz"],
                     writes=[("uscr", rs)], dma="us_%d" % rs)

            def rep_in(ch):
                rs = ch % 2
                src = uscr_t[rs].rearrange("(g c) t -> c g t", g=4)
                for s_ in range(4):
                    P.op("sp", lambda e, s_=s_: e.dma_start(out=R2[32 * s_:32 * s_ + 32, rs, :, :],
                                                             in_=src[:, :, s_:s_ + R2W]),
                         reads=[("uscr", rs), "R2g"] + ([("R2done", ch - 2)] if ch >= 2 else []),
                         writes=[("R2", ch)] if s_ == 3 else [("R2", ch, s_)], dma="r2_%d" % rs)

            R2toks = {ch: [("R2", ch)] for ch in range(4)}

            def conv_matmuls(ch):
                wa = C_WA + ch * KA
                cb = (4, 5) if ch % 2 == 0 else (6, 7)
                rs = ch % 2
                ds_ = ch % 2
                wcol = C_WP + ch * 32
                P.op("dve", lambda e: e.tensor_tensor(
                    out=dgb[:, ds_, :].rearrange("p (q g c) -> p q g c", q=8, g=4),
                    in0=E4[:, :, :].unsqueeze(1).to_broadcast([128, 8, 4, 32]),
                    in1=cst[:, wcol:wcol + 32].rearrange("p (q g) -> p q g", q=8).unsqueeze(3).to_broadcast([128, 8, 4, 32]),
                    op=ALU.mult),
                    reads=["E4", "cst"], writes=[("dgb", ds_), "bohi", "bolo"])
                for q in range(8):
                    def pe(e, q=q):
                        ins = None
                        for gi in range(2):
                            for g_ in range(4):
                                ins = e.matmul(ps[32 * g_:32 * g_ + 32, cb[gi] * 512:(cb[gi] + 1) * 512],
                                               lhsT=dgb[:, ds_, q * 128 + 32 * g_: q * 128 + 32 * g_ + 32],
                                               rhs=R2[:, rs, g_, gi * 512 + 4 * q: gi * 512 + 4 * q + 512],
                                               start=(q == 0), stop=(q == 7), tile_position=(0, 32 * g_))
                        return ins
                    P.op("pe", pe, reads=[("dgb", ds_)] + R2toks[ch] + MR, writes=[("ps", cb[0]), ("ps", cb[1])])
                for gi in range(2):
                    P.op("act", lambda e, gi=gi, ch=ch, cb=cb: e.activation(
                        out=acc[:, ch, gi * 512:(gi + 1) * 512], in_=ps[:, cb[gi] * 512:(cb[gi] + 1) * 512], func=AF.Identity,
                        bias=cc(C_BA + ch)),
                        reads=[("ps", cb[gi]), "cst"] + MR,
                        writes=[("acc", ch), ("R2done", ch)] if gi == 1 else [("acc", ch, "lo"), ("acc", ch)])
                if has_s:
                    ca3 = CA[:, ch, :].rearrange("p (s j) -> p s j", j=30)
                    w30 = cst[:, wa:wa + 30].unsqueeze(1).to_broadcast([128, NS, 30])
                    P.op("dve", lambda e, ca3=ca3, w30=w30: e.tensor_tensor(
                        out=prod.rearrange("p (s j) -> p s j", j=30), in0=ca3, in1=w30, op=ALU.mult),
                        reads=[("CA", ch), "cst"] + MR, writes=["prod"])
                    P.op("dve", lambda e, ch=ch: e.tensor_reduce(out=redS[:, ch, :], in_=prod.rearrange("p (s j) -> p s j", j=30),
                                                                 axis=AX.X, op=ALU.add),
                         reads=["prod"], writes=[("redS", ch)])
                    P.op("dve", lambda e, ch=ch, wa=wa: e.scalar_tensor_tensor(
                        out=accS[:, ch, :], in0=USb[:, ch, :], scalar=cc(wa + 30), in1=redS[:, ch, :],
                        op0=ALU.mult, op1=ALU.add),
                        reads=[("US", ch), ("redS", ch), "cst"], writes=[("accS", ch)])
                    P.op("dve", lambda e, ch=ch: e.tensor_scalar(out=accS[:, ch, :], in0=accS[:, ch, :], scalar1=cc(C_BA + ch),
                                                                 scalar2=None, op0=ALU.add),
                         reads=[("accS", ch), "cst"], writes=[("accS", ch)])

            gv_matmuls(0)
            rep_out(0)
            rep_in(0)
            gv_matmuls(1)
            rep_out(1)
            rep_in(1)
            gv_matmuls(2)
            rep_out(2)
            gv_matmuls(3)
            rep_out(3)
            conv_matmuls(0)
            rep_in(2)
            conv_matmuls(1)
            rep_in(3)
            conv_matmuls(2)
            conv_matmuls(3)
            if b == 1:
                pb = 3

                def peu(e, pb=pb):
                    ins = None
                    for ch in range(4):
                        ins = e.transpose(out=ps[:30, pb * 512 + ch * 128: pb * 512 + (ch + 1) * 128],
                                          in_=halo[:, ch, 0:30], identity=ident[:, :])
                    return ins
                P.op("pe", peu, reads=[("haloA", c) for c in range(4)] + ["ident"], writes=pst(pb))
                P.op("act", lambda e, pb=pb: e.activation(out=xn[:30, 1, 512:1024], in_=ps[:30, pb * 512:(pb + 1) * 512], func=AF.Identity),
                     reads=pst(pb), writes=[("xn", 1)])
                P.op("sp", lambda e: e.dma_start(out=nap, in_=xn[:30, 1, 512:1024]), reads=[("xn", 1)], dma="st_nap")
                pb = 2

                def peus(e, pb=pb):
                    ins = None
                    for ch in range(4):
                        ins = e.transpose(out=ps[:NS, pb * 512 + ch * 128: pb * 512 + (ch + 1) * 128],
                                          in_=USb[:, ch, :], identity=ident[:, :])
                    return ins
                P.op("pe", peus, reads=[("US", c) for c in range(4)] + ["ident"], writes=pst(pb))
                P.op("act", lambda e, pb=pb: e.activation(out=xn[:NS, 0, 0:512], in_=ps[:NS, pb * 512:(pb + 1) * 512], func=AF.Identity),
                     reads=pst(pb), writes=[("xn", 0)])
                P.op("sp", lambda e: e.dma_start(out=nas3[:, 29, :], in_=xn[:NS, 0, 0:512]), reads=[("xn", 0)],
                     dma="st_nas1")
            load_ln(1, b)

            def S1(tile):
                (t, tok0, n) = tile
                pb = t % 2
                sl = t % 2
                if t < 8:
                    srcs = [acc[:, ch, tok0:tok0 + n] for ch in range(4)]
                    rts = [("acc", ch) for ch in range(4)] + [("acc", ch, "lo") for ch in range(4)]
                else:
                    srcs = [accS[:, ch, :n] for ch in range(4)]
                    rts = [("accS", ch) for ch in range(4)]

                def pe(e):
                    ins = None
                    for ch in range(4):
                        ins = e.transpose(out=ps[:n, pb * 512 + ch * 128: pb * 512 + (ch + 1) * 128], in_=srcs[ch],
                                          identity=ident[:, :])
                    return ins
                P.op("pe", pe, reads=rts + ["ident"] + MR, writes=pst(pb))

            def S1post(tile):
                (t, tok0, n) = tile
                pb = t % 2
                sl = t % 2
                lsl = t % 2
                P.op("dve", lambda e: e.bn_stats(out=sttc[:n, lsl, :], in_=ps[:n, pb * 512:(pb + 1) * 512]),
                     reads=pst(pb), writes=[("sttc", lsl)])
                P.op("dve", lambda e: e.bn_aggr(out=mvc[:n, lsl, 0:2], in_=sttc[:n, lsl, :]),
                     reads=[("sttc", lsl)], writes=[("mvc", lsl)])
                rstd_chain(n, lsl, EPS, mv=mvc, tg="c")
                P.op("act", lambda e: e.activation(
                    out=xnc[:n, sl, :], in_=ps[:n, pb * 512:(pb + 1) * 512], func=AF.Identity,
                    scale=mvc[:n, lsl, 2:3], bias=mvc[:n, lsl, 3:4]),
                    reads=list(pst(pb)) + [("mv2c", lsl), ("mv3c", lsl)] + MR, writes=[("xnc", sl)])

            def S2(tile):
                (t, tok0, n) = tile
                sl = t % 2
                pb2 = 2

                def pe2(e):
                    ins = None
                    for ch in range(4):
                        ins = e.transpose(out=ps[:, pb2 * 512 + ch * n: pb2 * 512 + (ch + 1) * n],
                                          in_=xnc[:n, sl, ch * 128:(ch + 1) * 128], identity=ident[:n, :n])
                    return ins
                P.op("pe", pe2, reads=[("xnc", sl), "ident"], writes=pst(pb2))
                for ch in range(4):
                    P.op("act", lambda e, ch=ch: e.activation(
                        out=catT[:, ch, tok0:tok0 + n], in_=ps[:, pb2 * 512 + ch * n: pb2 * 512 + (ch + 1) * n],
                        func=AF.Silu, scale=cc(C_LG + ch), bias=cc(C_LB + ch)),
                        reads=list(pst(pb2)) + ["cst"] + MR, writes=[("catT", ch, tok0)])

            def S3(tile):
                (t, tok0, n) = tile
                yb = 4 + 2 * (t % 2)

                def pe(e):
                    ins = None
                    for k in range(KC):
                        for h in range(2):
                            ins = e.matmul(ps[:n, (yb + h) * 512:(yb + h + 1) * 512], lhsT=catT[:, k, tok0:tok0 + n],
                                           rhs=wo[:, k, h * 512:(h + 1) * 512], start=(k == 0), stop=False)
                    for h in range(2):
                        ins = e.matmul(ps[:n, (yb + h) * 512:(yb + h + 1) * 512], lhsT=ones2[0:2, :n],
                                       rhs=bo2[0:2, h * 512:(h + 1) * 512], start=False, stop=True)
                    return ins
                g0 = 0 if tok0 < 512 else (512 if tok0 < BLK else BLK)
                P.op("pe", pe, reads=[("catT", k, tok0) for k in range(4)] + [("catT", k, g0) for k in range(4, 8)]
                     + ["wo", "ones2", "bo2"] + MR,
                     writes=pst(yb, 2))

            lnctx = {}
            xslots = {}

            def S3a(tile):
                (t, tok0, n) = tile
                lnctx[t] = ln_head(t, n, 4 + 2 * (t % 2), ALPHA, EPS)

            def S3b(tile):
                (t, tok0, n) = tile
                ln_tail(lnctx[t], make_bf=False)

            def S3c(tile):
                xslots[tile[0]] = cast_x(tile)

            def S4(tile, bank=3):
                (t, tok0, n) = tile
                transposes_to_xT(b, t, tok0, n, bank=bank, xslot=xslots[t])

            nt = len(tiles)
            def step(i):
                if i < nt:
                    S1(tiles[i])
                if 0 <= i - 1 < nt:
                    S1post(tiles[i - 1])
                if 0 <= i - 2 < nt:
                    S2(tiles[i - 2])
                if 0 <= i - 3 < nt:
                    S3(tiles[i - 3])
                if 0 <= i - 4 < nt:
                    S3a(tiles[i - 4])
                if 0 <= i - 5 < nt:
                    S3b(tiles[i - 5])
                if 0 <= i - 6 < nt:
                    S3c(tiles[i - 6])
                if 0 <= i - 7 < nt:
                    if i < nt + 3:
                        S4(tiles[i - 7])
                    else:
                        S4(tiles[i - 7], bank=4 + 2 * ((nt - 2) % 2) + (i % 2))
            for i in range(nt + 3):
                if i == 2:
                    for f in range(NSLOT):
                        fetch_wgu(b, 1, f, prefetch=True)
                step(i)
            rmode_switch()
            return [(lambda i=i: step(i)) for i in range(nt + 3, nt + 7)]


        for b in range(2):
            if b == 0:
                t0s = tiles_of(0)
                for tile in t0s:
                    load_x(0, tile)
                cs = {}
                for j in range(4 + 2):
                    if j < 4:
                        cs[j] = cast_x(t0s[j])
                    if j >= 2:
                        (t, tok0, n) = t0s[j - 2]
                        transposes_to_xT(0, t, tok0, n, xslot=cs[j - 2])
                load_x(1, tiles_of(1)[8])
                P.op("sp", lambda e: e.dma_start(out=bo2[0:1, :], in_=bohl[0:1, 0, :]), reads=["bohi"], writes=[("bo2", 0)],
                     dma="c_bo2")
                P.op("sp", lambda e: e.dma_start(out=bo2[1:2, :], in_=bohl[0:1, 1, :]), reads=["bolo"], writes=["bo2"],
                     dma="c_bo2")
                early_sample_state()
                pre1 = [(lambda tile=tile: xpose_x_tile(0, tile)) for tile in t0s[4:]]
            else:
                pre1 = [lambda: xpose_x_tile(1, tiles_of(1)[7])]
            load_ln(0, b)
            pend = ffn(b, 0, 2.0 * ALPHA, 4.0 * EPS, final=False, defer_last=True, pre=pre1)
            tail = mixer(b, pend)
            tail.append(lambda b=b: load_ln(2, b))
            ffn(b, 1, 2.0 * ALPHA, 4.0 * EPS, final=True, pre=tail)

        P.finalize()
    return nc


_NC_CACHE = {}


def _prep_weights(inputs):
    f32 = np.float32

    def tile_cols(w, nchunk):
        return np.ascontiguousarray(w.reshape(KC, 128, nchunk, 128).transpose(2, 1, 0, 3))
    out = {}
    for i, (g, u, d) in enumerate((("f1_wg", "f1_wu", "f1_wd"), ("f2_wg", "f2_wu", "f2_wd"))):
        tg = tile_cols(np.asarray(inputs[g][0], f32), NFC)
        tu = tile_cols(np.asarray(inputs[u][0], f32), NFC)
        out["wgu%d" % (i + 1)] = np.ascontiguousarray(np.stack([tg, tu], axis=2).reshape(NFC, 128, 2 * KC * 128))
        out["wd%d" % (i + 1)] = np.ascontiguousarray(np.asarray(inputs[d][0], f32))
    out["win"] = np.ascontiguousarray(tile_cols(np.asarray(inputs["w_in"][0], f32), NIC).reshape(NIC, 128, KC * 128))
    out["wo"] = np.ascontiguousarray(np.asarray(inputs["w_o"][0], f32))
    ln = [inputs[k][0] for k in ("ln_f1_g", "ln_f1_b", "ln_mix_g", "ln_mix_b", "ln_f2_g", "ln_f2_b")]
    out["lnbc"] = np.ascontiguousarray(np.broadcast_to(np.stack(ln).astype(f32)[:, None, :], (6, 128, D)))
    cst = np.zeros((128, NCST), f32)
    cst[:, C_BIN:C_BIN + NIC] = np.asarray(inputs["b_in"][0], f32).reshape(NIC, 128).T
    wa = np.asarray(inputs["w_dw_a"][0], f32)
    cst[:, C_WA:C_WA + 4 * KA] = wa.reshape(KA, 4, 128).transpose(2, 1, 0).reshape(128, 4 * KA)
    cst[:, C_BA:C_BA + 4] = np.asarray(inputs["b_dw_a"][0], f32).reshape(4, 128).T
    cst[:, C_LG:C_LG + 4] = np.asarray(inputs["ln_conv_g"][0], f32).reshape(4, 128).T
    cst[:, C_LB:C_LB + 4] = np.asarray(inputs["ln_conv_b"][0], f32).reshape(4, 128).T
    wb = np.asarray(inputs["w_dw_b"][0], f32)
    cst[:, C_WB:C_WB + 4 * KB] = wb.reshape(KB, 4, 128).transpose(2, 1, 0).reshape(128, 4 * KB)
    wp = np.zeros((4, 32, 4, 8, 4), f32)
    wa4 = wa.reshape(KA, 4, 4, 32)
    for q in range(8):
        for s_ in range(4):
            j = 4 * q + s_
            if j < KA:
                wp[s_, :, :, q, :] = wa4[j].transpose(2, 0, 1)
    cst[:, C_WP:C_WP + 128] = wp.reshape(128, 128)
    out["cst"] = cst
    est = np.zeros((4, 32, 4, 32), f32)
    est[:, np.arange(32), :, np.arange(32)] = 1.0
    out["estk"] = np.ascontiguousarray(est.reshape(128, 4, 32))
    out["bo"] = np.ascontiguousarray(np.asarray(inputs["b_o"], f32).reshape(1, D))
    out["ident"] = np.eye(128, dtype=f32)
    return out


def kernel(**inputs):
    f32 = np.float32
    if "nc" not in _NC_CACHE:
        _NC_CACHE["nc"] = build_nc()
    nc = _NC_CACHE["nc"]
    shared = _prep_weights(inputs)
    x_prompt = np.asarray(inputs["x_prompt"], f32)
    x_sample = np.asarray(inputs["x_sample"], f32)
    sca = np.asarray(inputs["state_conv_a"], f32)[0]
    scb = np.asarray(inputs["state_conv_b"], f32)[0]
    in_maps = []
    for c in range(NCORE):
        m = dict(shared)
        m["xp"] = np.ascontiguousarray(x_prompt[c])
        m["xs"] = np.ascontiguousarray(x_sample[c * NS:(c + 1) * NS, 0, :])
        m["sca"] = np.ascontiguousarray(sca[c * NS:(c + 1) * NS].reshape(NS * 30, DA))
        m["scb"] = np.ascontiguousarray(scb[c * NS:(c + 1) * NS].reshape(NS * 2, DA))
        in_maps.append(m)
    res = run_bass_kernel_spmd(nc, in_maps, core_ids=list(range(NCORE)))
    r = res.results
    y_prompt = np.stack([r[c]["yp"] for c in range(NCORE)]).astype(f32)
    y_sample = np.concatenate([r[c]["ys"] for c in range(NCORE)], axis=0).reshape(NCORE * NS, 1, D).astype(f32)
    nap = np.stack([r[c]["nap"] for c in range(NCORE)])[None].astype(f32)
    nbp = np.stack([r[c]["nbp"] for c in range(NCORE)])[None].astype(f32)
    nas = np.concatenate([r[c]["nas"].reshape(NS, 30, DA) for c in range(NCORE)], axis=0)[None].astype(f32)
    nbs = np.concatenate([r[c]["nbs"].reshape(NS, 2, DA) for c in range(NCORE)], axis=0)[None].astype(f32)
    return (y_prompt, y_sample, nap, nbp, nas, nbs)
```

```python
import contextlib
import numpy as np
import concourse.bass as bass
import concourse.mybir as mybir
from concourse.bass_utils import run_bass_kernel_spmd

F32 = mybir.dt.float32
BF16 = mybir.dt.bfloat16
AF = mybir.ActivationFunctionType
ALU = mybir.AluOpType
AX = mybir.AxisListType

D = 1024
FF = 2816
NFC = FF // 128
KC = D // 128
DA = 512
DIN = 2560
NIC = DIN // 128
SEQ = 2048
NS = 16
KA = 31
KB = 3
NCORE = 8
BLK = 1024
NTOKB = BLK + NS
ALPHA = float(2.0 ** 0.25)
EPS = 1e-5
NSLOT = 4

C_BIN = 0
C_WA = C_BIN + NIC
C_BA = C_WA + 4 * KA
C_LG = C_BA + 4
C_LB = C_LG + 4
C_WB = C_LB + 4
C_WP = C_WB + 4 * KB
NCST = C_WP + 128
R2W = 1052


class _Op:
    __slots__ = ("eng", "fn", "deps", "dma", "idx", "inc", "waits", "count")


class Prog:
    ENGS = ("pe", "act", "dve", "pool", "sp")

    def __init__(self, nc, stack):
        self.nc = nc
        self.stack = stack
        self.ops = []
        self.lastw = {}
        self.readers = {}
        self.dma_cnt = {}
        self.sems = {}
        self.eng_ops = {e: [] for e in self.ENGS}

    def sem(self, name):
        if name not in self.sems:
            self.sems[name] = self.stack.enter_context(self.nc.semaphore(name))
        return self.sems[name]

    def op(self, eng, fn, reads=(), writes=(), dma=None):
        o = _Op()
        o.eng = eng
        o.fn = fn
        o.inc = False
        o.waits = []
        o.count = 0
        oid = len(self.ops)
        deps = {}
        for r in reads:
            w = self.lastw.get(r)
            if w is not None:
                deps[w] = True
        for wt in writes:
            w = self.lastw.get(wt)
            if w is not None and w not in deps:
                deps[w] = False
            for rd in self.readers.get(wt, ()):
                if rd not in deps:
                    deps[rd] = False
        deps.pop(oid, None)
        o.deps = deps
        if dma is not None:
            self.dma_cnt[dma] = self.dma_cnt.get(dma, 0) + 16
            o.dma = (dma, self.dma_cnt[dma])
            self.sem(dma)
        else:
            o.dma = None
        for r in reads:
            self.readers.setdefault(r, []).append(oid)
        for wt in writes:
            self.lastw[wt] = oid
            self.readers[wt] = []
        o.idx = len(self.eng_ops[eng])
        self.eng_ops[eng].append(o)
        self.ops.append(o)
        return oid

    def finalize(self, final_wait_eng="sp"):
        nc = self.nc
        esem = {e: self.sem("S_" + e) for e in ("pe", "act", "dve", "pool")}
        seen_eng = {e: {f: -1 for f in self.ENGS} for e in self.ENGS}
        seen_dma = {e: {} for e in self.ENGS}
        pending = []
        for o in self.ops:
            e = o.eng
            dma_w = {}
            eng_w = {}
            for pid, raw in o.deps.items():
                p = self.ops[pid]
                if p.dma is not None:
                    s, v = p.dma
                    if seen_dma[e].get(s, 0) >= v:
                        continue
                    if dma_w.get(s, 0) < v:
                        dma_w[s] = v
                else:
                    if p.eng == e:
                        if e == "pe" or e == "sp":
                            continue
                    if seen_eng[e][p.eng] >= p.idx:
                        continue
                    if eng_w.get(p.eng, -1) < p.idx:
                        eng_w[p.eng] = p.idx
            for s, v in dma_w.items():
                seen_dma[e][s] = v
                o.waits.append(("dma", s, v))
            for f, idx in eng_w.items():
                seen_eng[e][f] = idx
                prod = self.eng_ops[f][idx]
                prod.inc = True
                o.waits.append(("eng", f, prod))
        for e in ("pe", "act", "dve", "pool"):
            c = 0
            for o in self.eng_ops[e]:
                if o.inc:
                    c += 1
                    o.count = c
        final = [(self.sems[s], v) for s, v in self.dma_cnt.items()]
        sems = self.sems

        def body_for(ename):
            def body(eng):
                for o in self.eng_ops[ename]:
                    for w in o.waits:
                        if w[0] == "dma":
                            eng.wait_ge(sems[w[1]], w[2])
                        else:
                            eng.wait_ge(esem[w[1]], w[2].count)
                    ins = o.fn(eng)
                    if o.dma is not None:
                        ins.then_inc(sems[o.dma[0]], 16)
                    elif o.inc:
                        ins.then_inc(esem[ename], 1)
                if ename == final_wait_eng:
                    for s, v in final:
                        eng.wait_ge(s, v)
            return body

        with nc.Block() as block:
            block.tensor(body_for("pe"))
            block.scalar(body_for("act"))
            block.vector(body_for("dve"))
            block.gpsimd(body_for("pool"))
            block.sync(body_for("sp"))


def build_nc():
    nc = bass.Bass("TRN2", target_bir_lowering=False)
    dt = nc.dram_tensor

    def din(name, shape):
        return dt(name, list(shape), F32, kind="ExternalInput").ap()

    def dout(name, shape):
        return dt(name, list(shape), F32, kind="ExternalOutput").ap()

    xp = din("xp", [SEQ, D])
    xs = din("xs", [NS, D])
    sca = din("sca", [NS * 30, DA])
    scb = din("scb", [NS * 2, DA])
    wgu = [din("wgu1", [NFC, 128, 2 * KC * 128]), din("wgu2", [NFC, 128, 2 * KC * 128])]
    wdd = [din("wd1", [FF, D]), din("wd2", [FF, D])]
    win = din("win", [NIC, 128, KC * 128])
    wod = din("wo", [D, D])
    lnbc = din("lnbc", [6, 128, D])
    cstd = din("cst", [128, NCST])
    bod = din("bo", [1, D])
    identd = din("ident", [128, 128])
    estkd = din("estk", [128, 4, 32])
    yp = dout("yp", [SEQ, D])
    ys = dout("ys", [NS, D])
    nap = dout("nap", [30, DA])
    nbp = dout("nbp", [2, DA])
    nas = dout("nas", [NS * 30, DA])
    nbs = dout("nbs", [NS * 2, DA])

    stack = contextlib.ExitStack()
    with stack:
        def sb(name, shape, dtype):
            return stack.enter_context(nc.sbuf_tensor("sb_" + name, list(shape), dtype))

        P = Prog(nc, stack)
        ident = sb("ident", [128, 128], F32)
        cst = sb("cst", [128, NCST], F32)
        lnb = sb("lnb", [128, 2, D], F32)
        bohl = sb("bohl", [1, 2, D], BF16)
        ones = sb("ones", [1, 128], BF16)
        ones2 = sb("ones2", [2, 128], BF16)
        bo2 = sb("bo2", [2, D], BF16)
        xT = sb("xT", [128, KC, NTOKB], BF16)
        ring = sb("ring", [128, NSLOT, 2 * KC * 128], BF16)
        xres = sb("xres", [128, 9, D], F32)
        zt = sb("zt", [128, 2, D], F32)
        xn = sb("xn", [128, 2, D], F32)
        sg = sb("sg", [128, 2, 512], F32)
        sgS = sb("sgS", [128, NS], F32)
        NXB = 4
        xb = zt[:].bitcast(BF16).rearrange("p a (s d) -> p (a s) d", s=2)
        ident_bf = sb("ident_bf", [128, 128], BF16)
        sast = sb("sast", [128, 4, 512], F32)
        sbst = sb("sbst", [32, 512], F32)
        stt = sb("stt", [128, 2, 2, 6], F32)
        mv = sb("mv", [128, 2, 8], F32)
        mv_ln = mv
        mvc = sb("mvc", [128, 2, 8], F32)
        sttc = sb("sttc", [128, 2, 6], F32)
        neghalf = sb("neghalf", [128, 2], F32)
        hbg = sb("hbg", [128, 4], F32)
        halo = sb("halo", [128, 4, 32], F32)
        dummy = sb("dummy", [128, 8], F32)
        NDG = 8
        dg = sb("dg", [128, NDG, 128], BF16)
        E4 = sb("E4", [128, 4, 32], F32)
        R_HT = NFC * NTOKB
        R_WD = NFC * D
        RB = sb("R", [128, R_HT + R_WD], BF16)
        hT = RB[:, 0:R_HT].rearrange("p (f t) -> p f t", f=NFC)
        wd = RB[:, R_HT:R_HT + R_WD].rearrange("p (f d) -> p f d", f=NFC)
        RF = RB[:].bitcast(F32)
        off = 0

        def rf(n):
            nonlocal off
            a = RF[:, off:off + n]
            off += n
            return a
        UV = rf(4 * 1056).rearrange("p (c t) -> p c t", c=4)
        R2 = RB[:, 0:2 * 4 * R2W].rearrange("p (s g t) -> p s g t", s=2, g=4)
        acc = rf(4 * 1024).rearrange("p (c t) -> p c t", c=4)
        CA = rf(4 * 480).rearrange("p (c t) -> p c t", c=4)
        CB = rf(4 * 32).rearrange("p (c t) -> p c t", c=4)
        prod = rf(480)
        xnc = rf(2 * 512).rearrange("p (s t) -> p s t", s=2)
        USb = rf(4 * 16).rearrange("p (c t) -> p c t", c=4)
        VSb = rf(4 * 16).rearrange("p (c t) -> p c t", c=4)
        accS = rf(4 * 16).rearrange("p (c t) -> p c t", c=4)
        redS = rf(4 * 16).rearrange("p (c t) -> p c t", c=4)
        off_bf = off * 2
        catT = RB[:, off_bf:off_bf + KC * NTOKB].rearrange("p (k t) -> p k t", k=KC)
        off_bf += KC * NTOKB
        wo = RB[:, off_bf:off_bf + KC * D].rearrange("p (k d) -> p k d", k=KC)
        off_bf += KC * D
        U_bf = RB[:, off_bf:off_bf + 4 * 1056].rearrange("p (c t) -> p c t", c=4)
        off_bf += 4 * 1056
        assert off_bf <= R_HT + R_WD, (off_bf, R_HT + R_WD)

        ps = stack.enter_context(nc.psum_tensor("ps", [128, 8 * 512], F32))
        psb = ps[:].bitcast(BF16)

        def pst(b, n=1):
            return tuple(("ps", b + i) for i in range(n))

        P.op("sp", lambda e: e.dma_start(out=ident[:], in_=identd), writes=["ident"], dma="c_ident")
        P.op("sp", lambda e: e.dma_start(out=cst[:], in_=cstd), writes=["cst"], dma="c_cst")
        P.op("sp", lambda e: e.dma_start(out=E4[:], in_=estkd), writes=["E4"], dma="c_e4")
        P.op("sp", lambda e: e.dma_start(out=xn[0:1, 1, :], in_=bod), writes=[("xn", 1)], dma="c_bo")
        P.op("dve", lambda e: e.memset(ones[:], 1.0), writes=["ones"])
        P.op("dve", lambda e: e.tensor_copy(out=ident_bf[:], in_=ident[:]), reads=["ident"], writes=["ident_bf"])
        P.op("dve", lambda e: e.tensor_copy(out=bohl[0:1, 0, :], in_=xn[0:1, 1, :]),
             reads=[("xn", 1)], writes=["bohi"])
        P.op("dve", lambda e: e.tensor_copy(out=xn[0:1, 0, :], in_=bohl[0:1, 0, :]),
             reads=["bohi"], writes=[("xn", 0)])
        P.op("dve", lambda e: e.tensor_tensor(out=bohl[0:1, 1, :], in0=xn[0:1, 1, :], in1=xn[0:1, 0, :],
                                              op=ALU.subtract),
             reads=[("xn", 1), ("xn", 0)], writes=["bolo"])
        P.op("dve", lambda e: e.memset(ones2[:], 1.0), writes=["ones2"])
        P.op("act", lambda e: e.activation(out=dummy[0:2, 4:5], in_=ones2[0:2, 0:1], func=AF.Silu), reads=["ones2"],
             writes=["dummy_act"])
        P.op("dve", lambda e: e.memset(halo[:], 0.0), writes=["halo"] + [("haloA", c) for c in range(4)])
        P.op("dve", lambda e: e.memset(neghalf[:], -0.5), writes=["neghalf"])
        P.op("dve", lambda e: e.tensor_scalar(out=hbg[:, :], in0=cst[:, C_BIN + 4:C_BIN + 8], scalar1=0.5, scalar2=None,
                                              op0=ALU.mult),
             reads=["cst"], writes=["hbg"])

        rmode_tok = "RMODE"
        state = {"ring_i": 0, "psr": 0, "lnslot": 0, "dg_i": 0, "upar": 0, "xb_i": 0}

        def ring_slot():
            s = state["ring_i"] % NSLOT
            state["ring_i"] += 1
            return s

        preloaded = {}

        def fetch_win(b, fo, prefetch=False):
            key = ("win", b, fo)
            if key in preloaded and not prefetch:
                return preloaded.pop(key)
            s_ = ring_slot()
            P.op("pool", lambda e: e.dma_start(out=ring[:, s_, 0:KC * 128], in_=win[fo]),
                 writes=[("ring", s_)], dma="ring%d" % s_)
            if prefetch:
                preloaded[key] = s_
            return s_

        def fetch_wgu(b, which, f, prefetch=False):
            key = ("wgu", b, which, f)
            if key in preloaded and not prefetch:
                return preloaded.pop(key)
            s_ = ring_slot()
            extra = [("xres", 3)] if (b == 0 and which == 0 and 1 <= f < NSLOT) else []
            P.op("pool", lambda e: e.dma_start(out=ring[:, s_, :], in_=wgu[which][f], max_dma_last_dim=8192),
                 reads=extra, writes=[("ring", s_)], dma="ring%d" % s_)
            if prefetch:
                preloaded[key] = s_
            return s_

        def tiles_of(b):
            t = [(i, i * 128, 128) for i in range(8)]
            if b == 1:
                t.append((8, BLK, NS))
            return t

        def tgroups_of(b):
            g = [(0, 512), (512, 512)]
            if b == 1:
                g.append((BLK, NS))
            return g

        def load_ln(idx, b):
            P.op("sp", lambda e: e.dma_start(out=lnb[:, 0, :], in_=lnbc[2 * idx]),
                 writes=[("lnb", 0)], dma="c_lnb0")
            P.op("sp", lambda e: e.dma_start(out=lnb[:, 1, :], in_=lnbc[2 * idx + 1]),
                 writes=[("lnb", 1)], dma="c_lnb1")

        def rmode_switch():
            P.op("dve", lambda e: e.memset(dummy[:, 0:1], 0.0), reads=[], writes=[rmode_tok])

        def rstd_chain(n, sl, eps, mv=None, tg=""):
            if mv is None:
                mv = mv_ln
            P.op("dve", lambda e: e.tensor_scalar(out=mv[:n, sl, 4:5], in0=mv[:n, sl, 1:2], scalar1=eps, scalar2=None,
                                                  op0=ALU.add),
                 reads=[("mv" + tg, sl)], writes=[("mv1" + tg, sl)])
            P.op("pool", lambda e: e.tensor_tensor(out=mv[:n, sl, 2:3], in0=mv[:n, sl, 4:5], in1=neghalf[:n, 0:1], op=ALU.pow),
                 reads=[("mv1" + tg, sl), "neghalf"], writes=[("mv2" + tg, sl)])
            P.op("dve", lambda e: e.scalar_tensor_tensor(out=mv[:n, sl, 3:4], in0=mv[:n, sl, 0:1], scalar=-1.0,
                                                          in1=mv[:n, sl, 2:3], op0=ALU.mult, op1=ALU.mult),
                 reads=[("mv" + tg, sl), ("mv2" + tg, sl)], writes=[("mv3" + tg, sl)])

        def cast_x(tile):
            (t, tok0, n) = tile
            sl = state["xb_i"] % NXB
            state["xb_i"] += 1
            P.op("act", lambda e: e.activation(out=xb[:n, sl, :], in_=xres[:n, t, :], func=AF.Identity),
                 reads=[("xres", t)], writes=[("xb", sl)])
            return sl

        def transposes_to_xT(b, t, tok0, n, bank=None, xslot=None):
            if bank is None:
                bank = state["psr"] % 4
                state["psr"] += 1
            if xslot is None:
                sl = state["xb_i"] % NXB
                state["xb_i"] += 1
                P.op("act", lambda e: e.activation(out=xb[:n, sl, :], in_=xres[:n, t, :], func=AF.Identity),
                     reads=[("xres", t)], writes=[("xb", sl)])
            else:
                sl = xslot

            def pe(e):
                ins = None
                for k in range(KC):
                    ins = e.transpose(out=psb[:, bank * 1024 + k * n: bank * 1024 + (k + 1) * n],
                                      in_=xb[:n, sl, k * 128:(k + 1) * 128],
                                      identity=ident_bf[:n, :n])
                return ins
            P.op("pe", pe, reads=[("xb", sl), "ident_bf"], writes=pst(bank))

            def ev(e):
                return e.activation(out=xT[:, :, tok0:tok0 + n],
                                    in_=psb[:, bank * 1024: bank * 1024 + KC * n].rearrange("p (k n) -> p k n", k=KC),
                                    func=AF.Identity)
            P.op("act", ev, reads=pst(bank), writes=[("xT", tok0)])

        def ln_head(t, n, src_ps_banks, in_scale, eps, after_z=None):
            sl = state["lnslot"] % 2
            state["lnslot"] += 1
            pb = src_ps_banks
            P.op("dve", lambda e: e.scalar_tensor_tensor(out=xn[:n, sl, :], in0=xres[:n, t, :], scalar=in_scale,
                                                          in1=ps[:n, pb * 512:(pb + 2) * 512],
                                                          op0=ALU.mult, op1=ALU.add),
                 reads=[("xres", t)] + list(pst(pb, 2)), writes=[("xn", sl)])
            if after_z is not None:
                after_z()
            for h in range(2):
                P.op("dve", lambda e, h=h: e.bn_stats(out=stt[:n, sl, h, :], in_=xn[:n, sl, h * 512:(h + 1) * 512]),
                     reads=[("xn", sl)], writes=[("stt", sl, h)])
            P.op("dve", lambda e: e.bn_aggr(out=mv[:n, sl, 0:2], in_=stt[:n, sl, :, :].rearrange("p a b -> p (a b)")),
                 reads=[("stt", sl, 0), ("stt", sl, 1)], writes=[("mv", sl)])
            rstd_chain(n, sl, eps)
            return (t, n, sl)

        def ln_tail(ctx, final_store=None, make_bf=True):
            (t, n, sl) = ctx
            P.op("act", lambda e: e.activation(out=xn[:n, sl, :], in_=xn[:n, sl, :], func=AF.Identity,
                                               scale=mv[:n, sl, 2:3], bias=mv[:n, sl, 3:4]),
                 reads=[("xn", sl), ("mv2", sl), ("mv3", sl)], writes=[("xn", sl)])
            P.op("dve", lambda e: e.tensor_tensor(out=xn[:n, sl, :], in0=xn[:n, sl, :], in1=lnb[:n, 0, :], op=ALU.mult),
                 reads=[("xn", sl), ("lnb", 0)], writes=[("xn", sl)])
            if final_store is None:
                xs_ = None
                if make_bf:
                    xs_ = state["xb_i"] % NXB
                    state["xb_i"] += 1
                    P.op("dve", lambda e: e.tensor_tensor(out=xb[:n, xs_, :], in0=xn[:n, sl, :], in1=lnb[:n, 1, :], op=ALU.add),
                         reads=[("xn", sl), ("lnb", 1)], writes=[("xb", xs_)])
                P.op("dve", lambda e: e.tensor_tensor(out=xres[:n, t, :], in0=xn[:n, sl, :], in1=lnb[:n, 1, :], op=ALU.add),
                     reads=[("xn", sl), ("lnb", 1)], writes=[("xres", t)])
                return xs_
            dst = final_store
            P.op("dve", lambda e: e.tensor_tensor(out=xn[:n, sl, :], in0=xn[:n, sl, :], in1=lnb[:n, 1, :], op=ALU.add),
                 reads=[("xn", sl), ("lnb", 1)], writes=[("xn", sl)])
            P.op("sp", lambda e: e.dma_start(out=dst, in_=xn[:n, sl, :]), reads=[("xn", sl)],
                 dma="st_xn%d" % sl)
            return None

        def layer_norm_tile(t, n, src_ps_banks, in_scale, eps, b, final_store=None, after_z=None):
            return ln_tail(ln_head(t, n, src_ps_banks, in_scale, eps, after_z=after_z), final_store=final_store)

        def ffn(b, which, in_scale, eps, final, defer_last=False, pre=None):
            tiles = tiles_of(b)
            tgs = tgroups_of(b)
            wd_chunks = 11
            slots = {}

            def load_w(f):
                slots[f] = fetch_wgu(b, which, f)
                if 4 <= f < 4 + wd_chunks:
                    i = f - 4
                    P.op("pool", lambda e: e.dma_start(
                        out=wd[:, 2 * i:2 * i + 2, :],
                        in_=wdd[which][2 * i * 128:(2 * i + 2) * 128, :].rearrange("(f p) d -> p f d", p=128)),
                        reads=[rmode_tok], writes=[("wd", i)], dma="wd%d" % i)

            def u_group(f, tok0, n):
                s_ = slots[f]
                if n == 512:
                    par = state["upar"] % 2
                    state["upar"] += 1
                    bg, bu = par, 2 + par
                    sgt = sg[:, par, :]
                    sgtok = ("sg", par)
                else:
                    bg, bu = 4, 5
                    sgt = sgS[:, :]
                    sgtok = "sgS"

                def pe(e):
                    ins = None
                    for gu, bk in ((0, bg), (1, bu)):
                        for k in range(KC):
                            c0 = (gu * KC + k) * 128
                            ins = e.matmul(ps[:, bk * 512: bk * 512 + n], lhsT=ring[:, s_, c0:c0 + 128],
                                           rhs=xT[:, k, tok0:tok0 + n], start=(k == 0), stop=(k == KC - 1))
                    return ins
                xtoks = [("xT", t0) for (_, t0, nn) in tiles if t0 >= tok0 and t0 < tok0 + max(n, 1)]
                P.op("pe", pe, reads=[("ring", s_)] + xtoks, writes=[("ps", bg), ("ps", bu)])
                P.op("act", lambda e: e.activation(out=sgt[:, :n], in_=ps[:, bg * 512: bg * 512 + n], func=AF.Silu),
                     reads=[("ps", bg)], writes=[sgtok])
                P.op("dve", lambda e: e.tensor_tensor(
                    out=hT[:, f, tok0:tok0 + n], in0=sgt[:, :n], in1=ps[:, bu * 512: bu * 512 + n], op=ALU.mult),
                    reads=[sgtok, ("ps", bu), rmode_tok], writes=[("hT", f, tok0)])

            pre = list(pre) if pre else []
            npre = min(len(pre), NSLOT)
            for f in range(npre):
                load_w(f)
            for f in range(npre):
                u_group(f, *tgs[0])
                pre.pop(0)()
            for th in pre:
                th()
            for f in range(NFC):
                if f >= npre:
                    load_w(f)
                for gi, (tok0, n) in enumerate(tgs):
                    if f < npre and gi == 0:
                        continue
                    u_group(f, tok0, n)
            if which == 0:
                for fo in (12, 16, 13, 17):
                    fetch_win(b, fo, prefetch=True)
            elif b == 0:
                for f in range(NSLOT):
                    fetch_wgu(1, 0, f, prefetch=True)
            wd_toks = [("wd", i) for i in range(wd_chunks)]
            pending = None
            for (t, tok0, n) in tiles:
                yb = 4 + 2 * (t % 2)
                g0 = 0 if tok0 < 512 else (512 if tok0 < BLK else BLK)

                def pe(e, tok0=tok0, n=n, yb=yb):
                    ins = None
                    for f in range(NFC):
                        for h in range(2):
                            ins = e.matmul(ps[:n, (yb + h) * 512:(yb + h + 1) * 512], lhsT=hT[:, f, tok0:tok0 + n],
                                           rhs=wd[:, f, h * 512:(h + 1) * 512], start=(f == 0), stop=(f == NFC - 1))
                    return ins
                P.op("pe", pe, reads=[("hT", f, g0) for f in range(NFC)] + wd_toks + [rmode_tok], writes=pst(yb, 2))
                if pending is not None:
                    transposes_to_xT(b, *pending)
                    pending = None
                if final:
                    hook = None
                    if b == 0:
                        if t >= 1:
                            xpose_x_tile(1, tiles_of(1)[t - 1])
                        if t == 2:
                            xpose_x_tile(1, tiles_of(1)[8])
                        hook = (lambda t=t: load_x(1, tiles_of(1)[t]))
                    dst = yp[b * BLK + tok0: b * BLK + tok0 + n, :] if t < 8 else ys[:, :]
                    layer_norm_tile(t, n, yb, in_scale, eps, b, final_store=dst, after_z=hook)
                else:
                    xs_ = layer_norm_tile(t, n, yb, in_scale, eps, b)
                    pending = (t, tok0, n, None, xs_)
            if pending is not None and not defer_last:
                transposes_to_xT(b, *pending)
                pending = None
            return pending

        def load_x(b, tile):
            (t, tok0, n) = tile
            src = xp[b * BLK + tok0: b * BLK + tok0 + n, :] if t < 8 else xs[:, :]
            P.op("sp", lambda e: e.dma_start(out=xres[:n, t, :], in_=src), writes=[("xres", t)], dma="ld_x%d" % t)

        def xpose_x_tile(b, tile):
            (t, tok0, n) = tile
            transposes_to_xT(b, t, tok0, n)

        def load_x_tile(b, tile):
            load_x(b, tile)
            xpose_x_tile(b, tile)

        def early_sample_state():
            sca3 = sca.rearrange("(s j) c -> s j c", j=30)
            nas3 = nas.rearrange("(s j) c -> s j c", j=30)
            scb3 = scb.rearrange("(s j) c -> s j c", j=2)
            nbs3 = nbs.rearrange("(s j) c -> s j c", j=2)
            for r in range(4):
                rows = 128 if r < 3 else 96
                P.op("sp", lambda e, r=r, rows=rows: e.dma_start(out=sast[:rows, r, :], in_=sca[r * 128: r * 128 + rows, :]),
                     writes=["sast"] if r == 3 else [("sast", r)], dma="sa")
            P.op("sp", lambda e: e.dma_start(out=sbst[:, :], in_=scb), writes=["sbst"], dma="sbld")
            P.op("sp", lambda e: e.dma_start(out=nas3[:, 0:29, :], in_=sca3[:, 1:30, :]), dma="st_nas0")
            P.op("sp", lambda e: e.dma_start(out=nbs3[:, 0:1, :], in_=scb3[:, 1:2, :]), dma="st_nbs0")

        def mixer(b, pendT=None):
            tiles = tiles_of(b)
            tgs = tgroups_of(b)
            has_s = (b == 1)
            rmode_switch()
            MR = [rmode_tok]
            sca3 = sca.rearrange("(s j) c -> s j c", j=30)
            nas3 = nas.rearrange("(s j) c -> s j c", j=30)
            scb3 = scb.rearrange("(s j) c -> s j c", j=2)
            nbs3 = nbs.rearrange("(s j) c -> s j c", j=2)
            if has_s:
                for ch in range(4):
                    pb = ch % 2

                    def pe(e, ch=ch, pb=pb):
                        ins = None
                        for r in range(4):
                            rows = 128 if r < 3 else 96
                            ins = e.transpose(out=ps[:, pb * 512 + r * 128: pb * 512 + r * 128 + rows],
                                              in_=sast[:rows, r, ch * 128:(ch + 1) * 128],
                                              identity=ident[:rows, :rows])
                        return ins
                    P.op("pe", pe, reads=["sast", "ident"], writes=pst(pb))
                    P.op("act", lambda e, ch=ch, pb=pb: e.activation(out=CA[:, ch, :], in_=ps[:, pb * 512: pb * 512 + 480],
                                                                     func=AF.Identity),
                         reads=list(pst(pb)) + MR, writes=[("CA", ch)])
                pb = 2

                def pe2(e, pb=pb):
                    ins = None
                    for ch in range(4):
                        ins = e.transpose(out=ps[:, pb * 512 + ch * 32: pb * 512 + (ch + 1) * 32],
                                          in_=sbst[:32, ch * 128:(ch + 1) * 128], identity=ident[:32, :32])
                    return ins
                P.op("pe", pe2, reads=["sbst", "ident"], writes=pst(pb))
                P.op("act", lambda e, pb=pb: e.activation(out=CB[:, :, :], in_=ps[:, pb * 512: pb * 512 + 128].rearrange("p (c t) -> p c t", c=4),
                                                   func=AF.Identity),
                     reads=list(pst(pb)) + MR, writes=["CB"])

            def win_chunk(fo, consumer):
                s = fetch_win(b, fo)
                for gi, (tok0, n) in enumerate(tgs):
                    bk = state["psr"] % 4
                    state["psr"] += 1

                    def pe(e, tok0=tok0, n=n, bk=bk):
                        ins = None
                        for k in range(KC):
                            ins = e.matmul(ps[:, bk * 512: bk * 512 + n], lhsT=ring[:, s, k * 128:(k + 1) * 128],
                                           rhs=xT[:, k, tok0:tok0 + n], start=(k == 0), stop=(k == KC - 1))
                        return ins
                    xtoks = [("xT", t0) for (_, t0, nn) in tiles if t0 >= tok0 and t0 < tok0 + n]
                    P.op("pe", pe, reads=[("ring", s)] + xtoks, writes=[("ps", bk)])
                    consumer(gi, tok0, n, bk)

            def cc(col):
                return cst[:, col:col + 1]

            P.op("dve", lambda e: e.tensor_copy(out=UV[:, :, 0:2], in_=halo[:, :, 30:32]), reads=["halo"] + MR,
                 writes=[("UVh", c) for c in range(4)])
            for ch in range(4):
                s_gc = fetch_win(b, 12 + ch)
                s_bv = fetch_win(b, 16 + ch)
                for gi, (tok0, n) in enumerate(tgs):
                    b1 = state["psr"] % 4
                    b2 = (state["psr"] + 1) % 4
                    state["psr"] += 2
                    xtoks = [("xT", t0) for (_, t0, nn) in tiles if t0 >= tok0 and t0 < tok0 + n]
                    for (slot_w, bk) in ((s_gc, b1), (s_bv, b2)):
                        def pe(e, tok0=tok0, n=n, bk=bk, slot_w=slot_w):
                            ins = None
                            for k in range(KC):
                                ins = e.matmul(ps[:, bk * 512: bk * 512 + n], lhsT=ring[:, slot_w, k * 128:(k + 1) * 128],
                                               rhs=xT[:, k, tok0:tok0 + n], start=(k == 0), stop=(k == KC - 1))
                            return ins
                        P.op("pe", pe, reads=[("ring", slot_w)] + xtoks, writes=[("ps", bk)])
                    sl = gi % 2
                    P.op("act", lambda e, n=n, b1=b1, sl=sl, ch=ch: e.activation(
                        out=sg[:, sl, :n], in_=ps[:, b1 * 512: b1 * 512 + n], func=AF.Identity, bias=cc(C_BIN + 12 + ch)),
                        reads=[("ps", b1), "cst"], writes=[("sg", sl)])
                    if tok0 < BLK:
                        dst = UV[:, ch, 2 + tok0: 2 + tok0 + n]
                        wtok = ("UV", ch, tok0)
                    else:
                        dst = VSb[:, ch, :n]
                        wtok = ("VS", ch)
                    P.op("dve", lambda e, n=n, b2=b2, sl=sl, ch=ch, dst=dst: e.scalar_tensor_tensor(
                        out=dst, in0=ps[:, b2 * 512: b2 * 512 + n], scalar=cc(C_BIN + 16 + ch), in1=sg[:, sl, :n],
                        op0=ALU.add, op1=ALU.mult),
                        reads=[("ps", b2), ("sg", sl), "cst"] + MR, writes=[wtok])
                    if ch == 0 and gi == 0 and pendT is not None:
                        transposes_to_xT(b, *pendT)
                wb = C_WB + ch * KB
                uvr = [("UV", ch, 0), ("UV", ch, 512), ("UVh", ch)]
                P.op("dve", lambda e, ch=ch, wb=wb: e.tensor_scalar(out=acc[:, ch, :], in0=UV[:, ch, 0:BLK], scalar1=cc(wb),
                                                                   scalar2=None, op0=ALU.mult),
                     reads=uvr + ["cst"] + MR, writes=[("acc", ch)])
                for j in (1, 2):
                    P.op("dve", lambda e, ch=ch, wb=wb, j=j: e.scalar_tensor_tensor(
                        out=acc[:, ch, :], in0=UV[:, ch, j:j + BLK], scalar=cc(wb + j), in1=acc[:, ch, :],
                        op0=ALU.mult, op1=ALU.add),
                        reads=uvr + [("acc", ch), "cst"], writes=[("acc", ch)])
                if has_s:
                    cb3 = CB[:, ch, :].rearrange("p (s j) -> p s j", j=2)
                    P.op("dve", lambda e, ch=ch, wb=wb, cb3=cb3: e.tensor_scalar(
                        out=accS[:, ch, :], in0=cb3[:, :, 0], scalar1=cc(wb), scalar2=None, op0=ALU.mult),
                        reads=["CB", "cst"] + MR, writes=[("accS", ch)])
                    P.op("dve", lambda e, ch=ch, wb=wb, cb3=cb3: e.scalar_tensor_tensor(
                        out=accS[:, ch, :], in0=cb3[:, :, 1], scalar=cc(wb + 1), in1=accS[:, ch, :],
                        op0=ALU.mult, op1=ALU.add),
                        reads=["CB", "cst", ("accS", ch)], writes=[("accS", ch)])
                    P.op("dve", lambda e, ch=ch, wb=wb: e.scalar_tensor_tensor(
                        out=accS[:, ch, :], in0=VSb[:, ch, :], scalar=cc(wb + 2), in1=accS[:, ch, :],
                        op0=ALU.mult, op1=ALU.add),
                        reads=[("VS", ch), "cst", ("accS", ch)], writes=[("accS", ch)])

                def cons_gb(gi, tok0, n, bk, ch=ch):
                    if tok0 < BLK:
                        src = acc[:, ch, tok0:tok0 + n]
                        rt = ("acc", ch)
                    else:
                        src = accS[:, ch, :n]
                        rt = ("accS", ch)
                    P.op("dve", lambda e: e.scalar_tensor_tensor(
                        out=catT[:, 4 + ch, tok0:tok0 + n], in0=ps[:, bk * 512: bk * 512 + n], scalar=cc(C_BIN + 8 + ch),
                        in1=src, op0=ALU.add, op1=ALU.mult),
                        reads=[("ps", bk), rt, "cst"] + MR, writes=[("catT", 4 + ch, tok0)])
                win_chunk(8 + ch, cons_gb)
            P.op("dve", lambda e: e.tensor_copy(out=halo[:, :, 30:32], in_=UV[:, :, BLK:BLK + 2]),
                 reads=[("UV", c, 512) for c in range(4)], writes=["halo"])
            if b == 1:
                pb = 3

                def pev(e, pb=pb):
                    ins = None
                    for ch in range(4):
                        ins = e.transpose(out=ps[:2, pb * 512 + ch * 128: pb * 512 + (ch + 1) * 128],
                                          in_=UV[:, ch, BLK:BLK + 2], identity=ident[:, :])
                    return ins
                P.op("pe", pev, reads=[("UV", c, 512) for c in range(4)] + ["ident"], writes=pst(pb))
                P.op("act", lambda e, pb=pb: e.activation(out=xn[:2, 1, 0:512], in_=ps[:2, pb * 512:(pb + 1) * 512], func=AF.Identity),
                     reads=pst(pb), writes=[("xn", 1)])
                P.op("sp", lambda e: e.dma_start(out=nbp, in_=xn[:2, 1, 0:512]), reads=[("xn", 1)], dma="st_nbp")
                pb = 2

                def pevs(e, pb=pb):
                    ins = None
                    for ch in range(4):
                        ins = e.transpose(out=ps[:NS, pb * 512 + ch * 128: pb * 512 + (ch + 1) * 128],
                                          in_=VSb[:, ch, :], identity=ident[:, :])
                    return ins
                P.op("pe", pevs, reads=[("VS", c) for c in range(4)] + ["ident"], writes=pst(pb))
                P.op("act", lambda e, pb=pb: e.activation(out=xn[:NS, 0, 512:1024], in_=ps[:NS, pb * 512:(pb + 1) * 512], func=AF.Identity),
                     reads=pst(pb), writes=[("xn", 0)])
                P.op("sp", lambda e: e.dma_start(out=nbs3[:, 1, :], in_=xn[:NS, 0, 512:1024]), reads=[("xn", 0)],
                     dma="st_nbs1")

            P.op("pool", lambda e: e.dma_start(out=wo[:, :, :], in_=wod.rearrange("(k p) d -> p k d", p=128)),
                 reads=MR, writes=["wo"], dma="wo")
            P.op("dve", lambda e: e.memset(dummy[:, 7:8], 0.0),
                 writes=["R2g"] + [("UV", c, g_) for c in range(4) for g_ in (0, 512)] + [("UVh", c) for c in range(4)])
            P.op("dve", lambda e: e.memset(U_bf[:, :, 1054:1056], 0.0), reads=MR, writes=["Ubfz"])
            P.op("dve", lambda e: e.tensor_copy(out=U_bf[:, :, 0:30], in_=halo[:, :, 0:30]),
                 reads=[("haloA", c) for c in range(4)] + MR, writes=[("Ubfh", c) for c in range(4)])

            def gv_matmuls(ch):
                s_g = fetch_win(b, 4 + ch)
                s_v = fetch_win(b, ch)
                for gi, (tok0, n) in enumerate(tgs):
                    b1 = state["psr"] % 4
                    b2 = (state["psr"] + 1) % 4
                    state["psr"] += 2
                    xtoks = [("xT", t0) for (_, t0, nn) in tiles if t0 >= tok0 and t0 < tok0 + n]
                    for (slot_w, bk) in ((s_g, b1), (s_v, b2)):
                        def pe(e, tok0=tok0, n=n, bk=bk, slot_w=slot_w):
                            ins = None
                            for k in range(KC):
                                ins = e.matmul(ps[:, bk * 512: bk * 512 + n], lhsT=ring[:, slot_w, k * 128:(k + 1) * 128],
                                               rhs=xT[:, k, tok0:tok0 + n], start=(k == 0), stop=(k == KC - 1))
                            return ins
                        P.op("pe", pe, reads=[("ring", slot_w)] + xtoks, writes=[("ps", bk)])
                    sl = gi % 2
                    P.op("act", lambda e, n=n, b1=b1, sl=sl, ch=ch: e.activation(
                        out=sg[:, sl, :n], in_=ps[:, b1 * 512: b1 * 512 + n], func=AF.Tanh, scale=0.5,
                        bias=hbg[:, ch:ch + 1]),
                        reads=[("ps", b1), "hbg"], writes=[("sg", sl)])
                    P.op("dve", lambda e, n=n, sl=sl: e.tensor_scalar(out=sg[:, sl, :n], in0=sg[:, sl, :n], scalar1=0.5,
                                                                     scalar2=0.5, op0=ALU.mult, op1=ALU.add),
                         reads=[("sg", sl)], writes=[("sg", sl)])
                    if tok0 < BLK:
                        dst = U_bf[:, ch, 30 + tok0: 30 + tok0 + n]
                        wtok = ("Ubf", ch, tok0)
                    else:
                        dst = USb[:, ch, :n]
                        wtok = ("US", ch)
                    P.op("dve", lambda e, n=n, b2=b2, sl=sl, ch=ch, dst=dst: e.scalar_tensor_tensor(
                        out=dst, in0=ps[:, b2 * 512: b2 * 512 + n], scalar=cc(C_BIN + ch), in1=sg[:, sl, :n],
                        op0=ALU.add, op1=ALU.mult),
                        reads=[("ps", b2), ("sg", sl), "cst"] + MR, writes=[wtok])
                    if tok0 == 512:
                        P.op("dve", lambda e, b2=b2, sl=sl, ch=ch: e.scalar_tensor_tensor(
                            out=halo[:, ch, 0:30], in0=ps[:, b2 * 512 + 482: b2 * 512 + 512], scalar=cc(C_BIN + ch),
                            in1=sg[:, sl, 482:512], op0=ALU.add, op1=ALU.mult),
                            reads=[("ps", b2), ("sg", sl), "cst"], writes=[("haloA", ch)])
            def replicas(ch, queues=("sp",)):
                rs = ch % 2
                for g_ in range(4):
                    for s_ in range(4):
                        P.op(queues[(4 * g_ + s_) % len(queues)], lambda e, g_=g_, s_=s_, rs=rs, ch=ch: e.dma_start(
                            out=R2[32 * s_:32 * s_ + 32, rs, g_, :], in_=U_bf[32 * g_:32 * g_ + 32, ch, s_:s_ + R2W]),
                            reads=[("Ubf", ch, 0), ("Ubf", ch, 512), ("Ubfh", ch), "Ubfz", "R2g"]
                            + ([("R2done", ch - 2)] if ch >= 2 else []),
                            writes=[("R2", ch, g_, s_)], dma="r2_%d" % rs)

            R2toks = {ch: [("R2", ch, g_, s_) for g_ in range(4) for s_ in range(4)] for ch in range(4)}

            def conv_matmuls(ch):
                wa = C_WA + ch * KA
                cb = (4, 5) if ch % 2 == 0 else (6, 7)
                rs = ch % 2
                for q in range(8):
                    ds_ = state["dg_i"] % NDG
                    state["dg_i"] += 1
                    wcol = C_WP + ch * 32 + q * 4
                    P.op("dve", lambda e, ds_=ds_, wcol=wcol: e.tensor_tensor(
                        out=dg[:, ds_, :].rearrange("p (g c) -> p g c", g=4), in0=E4[:, :, :],
                        in1=cst[:, wcol:wcol + 4].unsqueeze(2).to_broadcast([128, 4, 32]), op=ALU.mult),
                        reads=["E4", "cst"], writes=[("dg", ds_)])

                    def pe(e, ds_=ds_, q=q, rs=rs, cb=cb):
                        ins = None
                        for gi in range(2):
                            for g_ in range(4):
                                ins = e.matmul(ps[32 * g_:32 * g_ + 32, cb[gi] * 512:(cb[gi] + 1) * 512],
                                               lhsT=dg[:, ds_, 32 * g_:32 * g_ + 32],
                                               rhs=R2[:, rs, g_, gi * 512 + 4 * q: gi * 512 + 4 * q + 512],
                                               start=(q == 0), stop=(q == 7), tile_position=(0, 32 * g_))
                        return ins
                    P.op("pe", pe, reads=[("dg", ds_)] + R2toks[ch] + MR, writes=[("ps", cb[0]), ("ps", cb[1])])
                for gi in range(2):
                    P.op("act", lambda e, gi=gi, ch=ch, cb=cb: e.activation(
                        out=acc[:, ch, gi * 512:(gi + 1) * 512], in_=ps[:, cb[gi] * 512:(cb[gi] + 1) * 512], func=AF.Identity,
                        bias=cc(C_BA + ch)),
                        reads=[("ps", cb[gi]), "cst"] + MR,
                        writes=[("acc", ch), ("R2done", ch)] if gi == 1 else [("acc", ch, "lo"), ("acc", ch)])
                if has_s:
                    ca3 = CA[:, ch, :].rearrange("p (s j) -> p s j", j=30)
                    w30 = cst[:, wa:wa + 30].unsqueeze(1).to_broadcast([128, NS, 30])
                    P.op("dve", lambda e, ca3=ca3, w30=w30: e.tensor_tensor(
                        out=prod.rearrange("p (s j) -> p s j", j=30), in0=ca3, in1=w30, op=ALU.mult),
                        reads=[("CA", ch), "cst"] + MR, writes=["prod"])
                    P.op("dve", lambda e, ch=ch: e.tensor_reduce(out=redS[:, ch, :], in_=prod.rearrange("p (s j) -> p s j", j=30),
                                                                 axis=AX.X, op=ALU.add),
                         reads=["prod"], writes=[("redS", ch)])
                    P.op("dve", lambda e, ch=ch, wa=wa: e.scalar_tensor_tensor(
                        out=accS[:, ch, :], in0=USb[:, ch, :], scalar=cc(wa + 30), in1=redS[:, ch, :],
                        op0=ALU.mult, op1=ALU.add),
                        reads=[("US", ch), ("redS", ch), "cst"], writes=[("accS", ch)])
                    P.op("dve", lambda e, ch=ch: e.tensor_scalar(out=accS[:, ch, :], in0=accS[:, ch, :], scalar1=cc(C_BA + ch),
                                                                 scalar2=None, op0=ALU.add),
                         reads=[("accS", ch), "cst"], writes=[("accS", ch)])

            gv_matmuls(0)
            replicas(0)
            gv_matmuls(1)
            replicas(1)
            gv_matmuls(2)
            gv_matmuls(3)
            conv_matmuls(0)
            replicas(2, ("sp", "act", "pool"))
            conv_matmuls(1)
            replicas(3, ("sp", "act", "pool"))
            conv_matmuls(2)
            conv_matmuls(3)
            if b == 1:
                pb = 3

                def peu(e, pb=pb):
                    ins = None
                    for ch in range(4):
                        ins = e.transpose(out=ps[:30, pb * 512 + ch * 128: pb * 512 + (ch + 1) * 128],
                                          in_=halo[:, ch, 0:30], identity=ident[:, :])
                    return ins
                P.op("pe", peu, reads=[("haloA", c) for c in range(4)] + ["ident"], writes=pst(pb))
                P.op("act", lambda e, pb=pb: e.activation(out=xn[:30, 1, 512:1024], in_=ps[:30, pb * 512:(pb + 1) * 512], func=AF.Identity),
                     reads=pst(pb), writes=[("xn", 1)])
                P.op("sp", lambda e: e.dma_start(out=nap, in_=xn[:30, 1, 512:1024]), reads=[("xn", 1)], dma="st_nap")
                pb = 2

                def peus(e, pb=pb):
                    ins = None
                    for ch in range(4):
                        ins = e.transpose(out=ps[:NS, pb * 512 + ch * 128: pb * 512 + (ch + 1) * 128],
                                          in_=USb[:, ch, :], identity=ident[:, :])
                    return ins
                P.op("pe", peus, reads=[("US", c) for c in range(4)] + ["ident"], writes=pst(pb))
                P.op("act", lambda e, pb=pb: e.activation(out=xn[:NS, 0, 0:512], in_=ps[:NS, pb * 512:(pb + 1) * 512], func=AF.Identity),
                     reads=pst(pb), writes=[("xn", 0)])
                P.op("sp", lambda e: e.dma_start(out=nas3[:, 29, :], in_=xn[:NS, 0, 0:512]), reads=[("xn", 0)],
                     dma="st_nas1")
            load_ln(1, b)

            def S1(tile):
                (t, tok0, n) = tile
                pb = t % 2
                sl = t % 2
                if t < 8:
                    srcs = [acc[:, ch, tok0:tok0 + n] for ch in range(4)]
                    rts = [("acc", ch) for ch in range(4)] + [("acc", ch, "lo") for ch in range(4)]
                else:
                    srcs = [accS[:, ch, :n] for ch in range(4)]
                    rts = [("accS", ch) for ch in range(4)]

                def pe(e):
                    ins = None
                    for ch in range(4):
                        ins = e.transpose(out=ps[:n, pb * 512 + ch * 128: pb * 512 + (ch + 1) * 128], in_=srcs[ch],
                                          identity=ident[:, :])
                    return ins
                P.op("pe", pe, reads=rts + ["ident"] + MR, writes=pst(pb))

            def S1post(tile):
                (t, tok0, n) = tile
                pb = t % 2
                sl = t % 2
                lsl = t % 2
                P.op("dve", lambda e: e.bn_stats(out=sttc[:n, lsl, :], in_=ps[:n, pb * 512:(pb + 1) * 512]),
                     reads=pst(pb), writes=[("sttc", lsl)])
                P.op("dve", lambda e: e.bn_aggr(out=mvc[:n, lsl, 0:2], in_=sttc[:n, lsl, :]),
                     reads=[("sttc", lsl)], writes=[("mvc", lsl)])
                rstd_chain(n, lsl, EPS, mv=mvc, tg="c")
                P.op("act", lambda e: e.activation(
                    out=xnc[:n, sl, :], in_=ps[:n, pb * 512:(pb + 1) * 512], func=AF.Identity,
                    scale=mvc[:n, lsl, 2:3], bias=mvc[:n, lsl, 3:4]),
                    reads=list(pst(pb)) + [("mv2c", lsl), ("mv3c", lsl)] + MR, writes=[("xnc", sl)])

            def S2(tile):
                (t, tok0, n) = tile
                sl = t % 2
                pb2 = 2

                def pe2(e):
                    ins = None
                    for ch in range(4):
                        ins = e.transpose(out=ps[:, pb2 * 512 + ch * n: pb2 * 512 + (ch + 1) * n],
                                          in_=xnc[:n, sl, ch * 128:(ch + 1) * 128], identity=ident[:n, :n])
                    return ins
                P.op("pe", pe2, reads=[("xnc", sl), "ident"], writes=pst(pb2))
                for ch in range(4):
                    P.op("act", lambda e, ch=ch: e.activation(
                        out=catT[:, ch, tok0:tok0 + n], in_=ps[:, pb2 * 512 + ch * n: pb2 * 512 + (ch + 1) * n],
                        func=AF.Silu, scale=cc(C_LG + ch), bias=cc(C_LB + ch)),
                        reads=list(pst(pb2)) + ["cst"] + MR, writes=[("catT", ch, tok0)])

            def S3(tile):
                (t, tok0, n) = tile
                yb = 4 + 2 * (t % 2)

                def pe(e):
                    ins = None
                    for k in range(KC):
                        for h in range(2):
                            ins = e.matmul(ps[:n, (yb + h) * 512:(yb + h + 1) * 512], lhsT=catT[:, k, tok0:tok0 + n],
                                           rhs=wo[:, k, h * 512:(h + 1) * 512], start=(k == 0), stop=False)
                    for h in range(2):
                        ins = e.matmul(ps[:n, (yb + h) * 512:(yb + h + 1) * 512], lhsT=ones2[0:2, :n],
                                       rhs=bo2[0:2, h * 512:(h + 1) * 512], start=False, stop=True)
                    return ins
                g0 = 0 if tok0 < 512 else (512 if tok0 < BLK else BLK)
                P.op("pe", pe, reads=[("catT", k, tok0) for k in range(4)] + [("catT", k, g0) for k in range(4, 8)]
                     + ["wo", "ones2", "bo2"] + MR,
                     writes=pst(yb, 2))

            lnctx = {}
            xslots = {}

            def S3a(tile):
                (t, tok0, n) = tile
                lnctx[t] = ln_head(t, n, 4 + 2 * (t % 2), ALPHA, EPS)

            def S3b(tile):
                (t, tok0, n) = tile
                ln_tail(lnctx[t], make_bf=False)

            def S3c(tile):
                xslots[tile[0]] = cast_x(tile)

            def S4(tile, bank=3):
                (t, tok0, n) = tile
                transposes_to_xT(b, t, tok0, n, bank=bank, xslot=xslots[t])

            nt = len(tiles)
            def step(i):
                if i < nt:
                    S1(tiles[i])
                if 0 <= i - 1 < nt:
                    S1post(tiles[i - 1])
                if 0 <= i - 2 < nt:
                    S2(tiles[i - 2])
                if 0 <= i - 3 < nt:
                    S3(tiles[i - 3])
                if 0 <= i - 4 < nt:
                    S3a(tiles[i - 4])
                if 0 <= i - 5 < nt:
                    S3b(tiles[i - 5])
                if 0 <= i - 6 < nt:
                    S3c(tiles[i - 6])
                if 0 <= i - 7 < nt:
                    if i < nt + 3:
                        S4(tiles[i - 7])
                    else:
                        S4(tiles[i - 7], bank=4 + 2 * ((nt - 2) % 2) + (i % 2))
            for i in range(nt + 3):
                if i == 2:
                    for f in range(NSLOT):
                        fetch_wgu(b, 1, f, prefetch=True)
                step(i)
            rmode_switch()
            return [(lambda i=i: step(i)) for i in range(nt + 3, nt + 7)]


        for b in range(2):
            if b == 0:
                t0s = tiles_of(0)
                for tile in t0s:
                    load_x(0, tile)
                cs = {}
                for j in range(4 + 2):
                    if j < 4:
                        cs[j] = cast_x(t0s[j])
                    if j >= 2:
                        (t, tok0, n) = t0s[j - 2]
                        transposes_to_xT(0, t, tok0, n, xslot=cs[j - 2])
                load_x(1, tiles_of(1)[8])
                P.op("sp", lambda e: e.dma_start(out=bo2[0:1, :], in_=bohl[0:1, 0, :]), reads=["bohi"], writes=[("bo2", 0)],
                     dma="c_bo2")
                P.op("sp", lambda e: e.dma_start(out=bo2[1:2, :], in_=bohl[0:1, 1, :]), reads=["bolo"], writes=["bo2"],
                     dma="c_bo2")
                early_sample_state()
                pre1 = [(lambda tile=tile: xpose_x_tile(0, tile)) for tile in t0s[4:]]
            else:
                pre1 = [lambda: xpose_x_tile(1, tiles_of(1)[7])]
            load_ln(0, b)
            pend = ffn(b, 0, 2.0 * ALPHA, 4.0 * EPS, final=False, defer_last=True, pre=pre1)
            tail = mixer(b, pend)
            tail.append(lambda b=b: load_ln(2, b))
            ffn(b, 1, 2.0 * ALPHA, 4.0 * EPS, final=True, pre=tail)

        P.finalize()
    return nc


_NC_CACHE = {}


def _prep_weights(inputs):
    f32 = np.float32

    def tile_cols(w, nchunk):
        return np.ascontiguousarray(w.reshape(KC, 128, nchunk, 128).transpose(2, 1, 0, 3))
    out = {}
    for i, (g, u, d) in enumerate((("f1_wg", "f1_wu", "f1_wd"), ("f2_wg", "f2_wu", "f2_wd"))):
        tg = tile_cols(np.asarray(inputs[g][0], f32), NFC)
        tu = tile_cols(np.asarray(inputs[u][0], f32), NFC)
        out["wgu%d" % (i + 1)] = np.ascontiguousarray(np.stack([tg, tu], axis=2).reshape(NFC, 128, 2 * KC * 128))
        out["wd%d" % (i + 1)] = np.ascontiguousarray(np.asarray(inputs[d][0], f32))
    out["win"] = np.ascontiguousarray(tile_cols(np.asarray(inputs["w_in"][0], f32), NIC).reshape(NIC, 128, KC * 128))
    out["wo"] = np.ascontiguousarray(np.asarray(inputs["w_o"][0], f32))
    ln = [inputs[k][0] for k in ("ln_f1_g", "ln_f1_b", "ln_mix_g", "ln_mix_b", "ln_f2_g", "ln_f2_b")]
    out["lnbc"] = np.ascontiguousarray(np.broadcast_to(np.stack(ln).astype(f32)[:, None, :], (6, 128, D)))
    cst = np.zeros((128, NCST), f32)
    cst[:, C_BIN:C_BIN + NIC] = np.asarray(inputs["b_in"][0], f32).reshape(NIC, 128).T
    wa = np.asarray(inputs["w_dw_a"][0], f32)
    cst[:, C_WA:C_WA + 4 * KA] = wa.reshape(KA, 4, 128).transpose(2, 1, 0).reshape(128, 4 * KA)
    cst[:, C_BA:C_BA + 4] = np.asarray(inputs["b_dw_a"][0], f32).reshape(4, 128).T
    cst[:, C_LG:C_LG + 4] = np.asarray(inputs["ln_conv_g"][0], f32).reshape(4, 128).T
    cst[:, C_LB:C_LB + 4] = np.asarray(inputs["ln_conv_b"][0], f32).reshape(4, 128).T
    wb = np.asarray(inputs["w_dw_b"][0], f32)
    cst[:, C_WB:C_WB + 4 * KB] = wb.reshape(KB, 4, 128).transpose(2, 1, 0).reshape(128, 4 * KB)
    wp = np.zeros((4, 32, 4, 8, 4), f32)
    wa4 = wa.reshape(KA, 4, 4, 32)
    for q in range(8):
        for s_ in range(4):
            j = 4 * q + s_
            if j < KA:
                wp[s_, :, :, q, :] = wa4[j].transpose(2, 0, 1)
    cst[:, C_WP:C_WP + 128] = wp.reshape(128, 128)
    out["cst"] = cst
    est = np.zeros((4, 32, 4, 32), f32)
    est[:, np.arange(32), :, np.arange(32)] = 1.0
    out["estk"] = np.ascontiguousarray(est.reshape(128, 4, 32))
    out["bo"] = np.ascontiguousarray(np.asarray(inputs["b_o"], f32).reshape(1, D))
    out["ident"] = np.eye(128, dtype=f32)
    return out


def kernel(**inputs):
    f32 = np.float32
    if "nc" not in _NC_CACHE:
        _NC_CACHE["nc"] = build_nc()
    nc = _NC_CACHE["nc"]
    shared = _prep_weights(inputs)
    x_prompt = np.asarray(inputs["x_prompt"], f32)
    x_sample = np.asarray(inputs["x_sample"], f32)
    sca = np.asarray(inputs["state_conv_a"], f32)[0]
    scb = np.asarray(inputs["state_conv_b"], f32)[0]
    in_maps = []
    for c in range(NCORE):
        m = dict(shared)
        m["xp"] = np.ascontiguousarray(x_prompt[c])
        m["xs"] = np.ascontiguousarray(x_sample[c * NS:(c + 1) * NS, 0, :])
        m["sca"] = np.ascontiguousarray(sca[c * NS:(c + 1) * NS].reshape(NS * 30, DA))
        m["scb"] = np.ascontiguousarray(scb[c * NS:(c + 1) * NS].reshape(NS * 2, DA))
        in_maps.append(m)
    res = run_bass_kernel_spmd(nc, in_maps, core_ids=list(range(NCORE)))
    r = res.results
    y_prompt = np.stack([r[c]["yp"] for c in range(NCORE)]).astype(f32)
    y_sample = np.concatenate([r[c]["ys"] for c in range(NCORE)], axis=0).reshape(NCORE * NS, 1, D).astype(f32)
    nap = np.stack([r[c]["nap"] for c in range(NCORE)])[None].astype(f32)
    nbp = np.stack([r[c]["nbp"] for c in range(NCORE)])[None].astype(f32)
    nas = np.concatenate([r[c]["nas"].reshape(NS, 30, DA) for c in range(NCORE)], axis=0)[None].astype(f32)
    nbs = np.concatenate([r[c]["nbs"].reshape(NS, 2, DA) for c in range(NCORE)], axis=0)[None].astype(f32)
    return (y_prompt, y_sample, nap, nbp, nas, nbs)
```
